# Optimizing a Trainium2 kernel written in Bass

```python
import math
import jax, jax.numpy as jnp
from jax import lax
import numpy as np

D_MODEL = 2048
BATCH = 8
SEQ = 2048
DEPTH = 4

CTX_LEN = 256
GRID_W = 64
N_MIXERS = 2
EPS = 1e-6
DA_HEADS = 8
DA_QK_DIM = 128
DA_V_DIM = 2 * DA_QK_DIM
DA_QK_W = DA_HEADS * 2 * DA_QK_DIM
DA_V_W = DA_HEADS * DA_V_DIM
ROPE_BASE = 10000.0
Q_BLOCK = 128
GDN_QK_HEADS = 16
GDN_V_HEADS = 32
GDN_QK_DIM = 128
GDN_V_DIM = 128
GDN_QK_W = GDN_QK_HEADS * GDN_QK_DIM
GDN_V_W = GDN_V_HEADS * GDN_V_DIM
GDN_QKV_W = 2 * GDN_QK_W + GDN_V_W
GDN_IN_W = GDN_QKV_W + GDN_V_W + 4 * GDN_V_HEADS
GDN_CONV = 5
GDN_CHUNK = 64
D_FF = 5504
FFN_CONV = 3
N_DA_LAYERS = (DEPTH + 1) // 2
N_GDN_LAYERS = DEPTH // 2

kernel_name = 'hybrid_diffattn_gdeltanet_convffn_dit'


def rmsnorm(x, w):
    xf = x.astype(jnp.float32)
    y = xf * lax.rsqrt(jnp.mean(xf * xf, axis=-1, keepdims=True) + EPS)
    return (y * w.astype(jnp.float32)).astype(x.dtype)


def l2norm(x):
    xf = x.astype(jnp.float32)
    return (xf * lax.rsqrt(jnp.sum(xf * xf, axis=-1, keepdims=True) + EPS)).astype(x.dtype)


def dwconv_centred(x, w):
    k = w.shape[0]
    p = k // 2
    t = x.shape[1]
    xp = jnp.pad(x, ((0, 0), (p, p), (0, 0)))
    out = xp[:, 0:t] * w[0]
    for j in range(1, k):
        out = out + xp[:, j:j + t] * w[j]
    return out


def axial_rope_tables(t_len, rot_dim):
    t = jnp.arange(t_len, dtype=jnp.int32)
    rows = (t // GRID_W).astype(jnp.float32)
    cols = (t % GRID_W).astype(jnp.float32)
    n_freq = rot_dim // 4
    inv_freq = ROPE_BASE ** (-jnp.arange(n_freq, dtype=jnp.float32) / n_freq)
    ang = jnp.stack([rows[:, None] * inv_freq, cols[:, None] * inv_freq], axis=1)
    return jnp.cos(ang), jnp.sin(ang)


def apply_axial_rope(x, cos, sin):
    shp = x.shape
    xr = x.reshape(shp[:-1] + (2, 2, shp[-1] // 4))
    x1, x2 = xr[..., 0, :], xr[..., 1, :]
    c = cos.astype(x.dtype)
    s = sin.astype(x.dtype)
    out = jnp.stack([x1 * c - x2 * s, x2 * c + x1 * s], axis=-2)
    return out.reshape(shp)


def diff_attend(q, k, v, lam):
    s = jnp.einsum('bhcqd,bhckd->bhcqk', q, k).astype(jnp.float32) * (DA_QK_DIM ** -0.5)
    p = jax.nn.softmax(s, axis=-1)
    a = p[:, :, 0] - lam * p[:, :, 1]
    return jnp.einsum('bhqk,bhkd->bhqd', a.astype(v.dtype), v)


def diff_attention_mixer(h_ctx, h_lat, w_qkv, lam_vecs, head_gain, w_o, lambda_init, cos, sin, with_ctx_out):
    def project(h):
        b, t, _ = h.shape
        q, k, v = jnp.split(h @ w_qkv, [DA_QK_W, 2 * DA_QK_W], axis=-1)
        q = q.reshape(b, t, DA_HEADS, 2, DA_QK_DIM).transpose(0, 2, 3, 1, 4)
        k = k.reshape(b, t, DA_HEADS, 2, DA_QK_DIM).transpose(0, 2, 3, 1, 4)
        v = v.reshape(b, t, DA_HEADS, DA_V_DIM).transpose(0, 2, 1, 3)
        return q, k, v

    lv = lam_vecs.astype(jnp.float32)
    lam = jnp.exp(jnp.sum(lv[0] * lv[1])) - jnp.exp(jnp.sum(lv[2] * lv[3])) + lambda_init
    q_c, k_c, v_c = project(h_ctx)
    q_l, k_l, v_l = project(h_lat)
    q_l = apply_axial_rope(q_l, cos, sin)
    k_l = apply_axial_rope(k_l, cos, sin)
    k_all = jnp.concatenate([k_c, k_l], axis=3)
    v_all = jnp.concatenate([v_c, v_l], axis=2)
    b, h, _, t, d = q_l.shape
    nb = t // Q_BLOCK
    q_blocks = jnp.moveaxis(q_l.reshape(b, h, 2, nb, Q_BLOCK, d), 3, 0)
    o_blocks = lax.map(lambda qb: diff_attend(qb, k_all, v_all, lam), q_blocks)
    o_lat = jnp.moveaxis(o_blocks, 0, 2).reshape(b, h, t, DA_V_DIM)

    def out_proj(o):
        o = rmsnorm(o, head_gain) * (1.0 - lambda_init)
        bb, hh, tt, dv = o.shape
        return o.transpose(0, 2, 1, 3).reshape(bb, tt, hh * dv) @ w_o

    y_lat = out_proj(o_lat)
    y_ctx = out_proj(diff_attend(q_c, k_c, v_c, lam)) if with_ctx_out else None
    return y_ctx, y_lat


def gdn_features(h, w_in, conv_w, a_log, dt_bias):
    b, t, _ = h.shape
    qkv, z, ab = jnp.split(h @ w_in, [GDN_QKV_W, GDN_QKV_W + GDN_V_W], axis=-1)
    qkv = jax.nn.silu(dwconv_centred(qkv, conv_w))
    q, k, v = jnp.split(qkv, [GDN_QK_W, 2 * GDN_QK_W], axis=-1)
    rep = GDN_V_HEADS // GDN_QK_HEADS
    q = jnp.repeat(l2norm(q.reshape(b, t, GDN_QK_HEADS, GDN_QK_DIM)), rep, axis=2) * (GDN_QK_DIM ** -0.5)
    k = jnp.repeat(l2norm(k.reshape(b, t, GDN_QK_HEADS, GDN_QK_DIM)), rep, axis=2)
    v = v.reshape(b, t, GDN_V_HEADS, GDN_V_DIM)
    z = z.reshape(b, t, GDN_V_HEADS, GDN_V_DIM)
    ab = ab.reshape(b, t, 2, 2, GDN_V_HEADS).astype(jnp.float32)
    beta = jax.nn.sigmoid(ab[:, :, :, 0])
    g = -jnp.exp(a_log.astype(jnp.float32)) * jax.nn.softplus(ab[:, :, :, 1] + dt_bias.astype(jnp.float32))
    return q, k, v, z, beta, g


def gated_delta_chunked(q, k, v, beta, g, s0, want_out):
    out_dtype = v.dtype
    b, t, h, dk = k.shape
    dv = v.shape[-1]
    n = t // GDN_CHUNK

    def chunks(a):
        a = a.astype(jnp.float32).reshape((b, n, GDN_CHUNK) + a.shape[2:])
        return jnp.swapaxes(a, 2, 3)

    k, v, beta, g = chunks(k), chunks(v), chunks(beta), chunks(g)
    gam = jnp.cumsum(g, axis=-1)
    idx = jnp.arange(GDN_CHUNK)
    incl = idx[:, None] >= idx[None, :]
    strict = idx[:, None] > idx[None, :]
    decay = jnp.exp(jnp.where(incl, gam[..., :, None] - gam[..., None, :], -jnp.inf))
    kk = jnp.einsum('bnhid,bnhjd->bnhij', k, k)
    a_mat = jnp.where(strict, beta[..., :, None] * kk * decay, 0.0)
    rhs = jnp.concatenate([v * beta[..., None], k * (beta * jnp.exp(gam))[..., None]], axis=-1)
    uw = lax.linalg.triangular_solve(a_mat, rhs, left_side=True, lower=True, unit_diagonal=True)
    u, w = uw[..., :dv], uw[..., dv:]
    k_dec = k * jnp.exp(gam[..., -1:] - gam)[..., None]
    g_last = jnp.exp(gam[..., -1])
    xs = [u, w, k_dec, g_last]
    if want_out:
        q = chunks(q)
        qk = jnp.where(incl, jnp.einsum('bnhid,bnhjd->bnhij', q, k) * decay, 0.0)
        xs = xs + [q * jnp.exp(gam)[..., None], qk]
    xs = [jnp.moveaxis(a, 1, 0) for a in xs]

    def step(state, inp):
        u_c, w_c, kd_c, gl_c = inp[0], inp[1], inp[2], inp[3]
        v_new = u_c - jnp.einsum('bhck,bhkv->bhcv', w_c, state)
        new_state = state * gl_c[..., None, None] + jnp.einsum('bhck,bhcv->bhkv', kd_c, v_new)
        if want_out:
            o = jnp.einsum('bhck,bhkv->bhcv', inp[4], state) + jnp.einsum('bhij,bhjv->bhiv', inp[5], v_new)
            return new_state, o
        return new_state, None

    s_final, o = lax.scan(step, s0, xs)
    if want_out:
        o = jnp.swapaxes(jnp.moveaxis(o, 0, 1), 2, 3).reshape(b, t, h, dv).astype(out_dtype)
    return o, s_final


def gdn_output(o, z, norm_gain, w_o):
    b, t, h, dv = o.shape
    y = rmsnorm(o, norm_gain) * jax.nn.silu(z)
    return y.reshape(b, t, h * dv) @ w_o


def gated_deltanet_mixer(h_ctx, h_lat, w_in, conv_w, a_log, dt_bias, norm_gain, w_o, with_ctx_out):
    q_c, k_c, v_c, z_c, beta_c, g_c = gdn_features(h_ctx, w_in, conv_w, a_log, dt_bias)
    q_l, k_l, v_l, z_l, beta_l, g_l = gdn_features(h_lat, w_in, conv_w, a_log, dt_bias)
    s0 = jnp.zeros((h_lat.shape[0], GDN_V_HEADS, GDN_QK_DIM, GDN_V_DIM), jnp.float32)

    def flip(a):
        return jnp.flip(a, axis=1)

    o_cf, s_cf = gated_delta_chunked(q_c, k_c, v_c, beta_c[:, :, 0], g_c[:, :, 0], s0, with_ctx_out)
    o_cb, s_cb = gated_delta_chunked(flip(q_c), flip(k_c), flip(v_c), flip(beta_c[:, :, 1]), flip(g_c[:, :, 1]), s0, with_ctx_out)
    o_lf, _ = gated_delta_chunked(q_l, k_l, v_l, beta_l[:, :, 0], g_l[:, :, 0], s_cf, True)
    o_lb, _ = gated_delta_chunked(flip(q_l), flip(k_l), flip(v_l), flip(beta_l[:, :, 1]), flip(g_l[:, :, 1]), s_cb, True)
    y_lat = gdn_output(o_lf + flip(o_lb), z_l, norm_gain, w_o)
    y_ctx = gdn_output(o_cf + flip(o_cb), z_c, norm_gain, w_o) if with_ctx_out else None
    return y_ctx, y_lat


def conv_ffn(h, w_up, conv_w, w_down):
    u = dwconv_centred(h @ w_up, conv_w)
    gate, val = jnp.split(u, 2, axis=-1)
    return (jax.nn.silu(gate) * val) @ w_down


def setup_inputs(seed: int = 0) -> dict:
    key = jax.random.key(seed)
    ks = jax.random.split(key, 24)

    def nrm(k, shape, scale):
        return jax.random.normal(k, shape, jnp.float32) * scale

    dt = jnp.exp(jax.random.uniform(ks[15], (N_GDN_LAYERS, 2, GDN_V_HEADS), jnp.float32, math.log(1e-3), math.log(1e-1)))
    return {
        'x': nrm(ks[0], (BATCH, SEQ, D_MODEL), 1.0),
        'c': nrm(ks[1], (BATCH, D_MODEL), 1.0),
        'ctx': nrm(ks[2], (BATCH, CTX_LEN, D_MODEL), 1.0),
        'c_ctx': nrm(ks[3], (D_MODEL,), 1.0),
        'w_mod': nrm(ks[4], (DEPTH, D_MODEL, 6 * D_MODEL), 0.5 * D_MODEL ** -0.5),
        'b_mod': nrm(ks[5], (DEPTH, 6 * D_MODEL), 0.02),
        'norm_mix': 1.0 + nrm(ks[6], (DEPTH, D_MODEL), 0.02),
        'norm_ffn': 1.0 + nrm(ks[7], (DEPTH, D_MODEL), 0.02),
        'da_w_qkv': nrm(ks[8], (N_DA_LAYERS, D_MODEL, 2 * DA_QK_W + DA_V_W), D_MODEL ** -0.5),
        'da_lambda': nrm(ks[9], (N_DA_LAYERS, 4, DA_QK_DIM), 0.1),
        'da_head_gain': 1.0 + nrm(ks[10], (N_DA_LAYERS, DA_V_DIM), 0.02),
        'da_w_o': nrm(ks[11], (N_DA_LAYERS, DA_V_W, D_MODEL), DA_V_W ** -0.5),
        'gdn_w_in': nrm(ks[12], (N_GDN_LAYERS, D_MODEL, GDN_IN_W), D_MODEL ** -0.5),
        'gdn_conv': nrm(ks[13], (N_GDN_LAYERS, GDN_CONV, GDN_QKV_W), GDN_CONV ** -0.5),
        'gdn_a_log': jnp.log(jax.random.uniform(ks[14], (N_GDN_LAYERS, 2, GDN_V_HEADS), jnp.float32, 1.0, 16.0)),
        'gdn_dt_bias': dt + jnp.log(-jnp.expm1(-dt)),
        'gdn_norm_gain': 1.0 + nrm(ks[16], (N_GDN_LAYERS, GDN_V_DIM), 0.02),
        'gdn_w_o': nrm(ks[17], (N_GDN_LAYERS, GDN_V_W, D_MODEL), GDN_V_W ** -0.5),
        'ffn_w_up': nrm(ks[18], (DEPTH, D_MODEL, 2 * D_FF), D_MODEL ** -0.5),
        'ffn_conv': nrm(ks[19], (DEPTH, FFN_CONV, 2 * D_FF), FFN_CONV ** -0.5),
        'ffn_w_down': nrm(ks[20], (DEPTH, D_FF, D_MODEL), D_FF ** -0.5),
        'final_norm': 1.0 + nrm(ks[21], (D_MODEL,), 0.02),
    }


def reference(x, c, ctx, c_ctx, w_mod, b_mod, norm_mix, norm_ffn, da_w_qkv, da_lambda, da_head_gain, da_w_o,
              gdn_w_in, gdn_conv, gdn_a_log, gdn_dt_bias, gdn_norm_gain, gdn_w_o, ffn_w_up, ffn_conv, ffn_w_down,
              final_norm):
    seq = x.shape[1]
    cos, sin = axial_rope_tables(seq, DA_QK_DIM)
    silu_c = jax.nn.silu(c)[:, None, :]
    silu_cc = jax.nn.silu(c_ctx)[None, None, :]
    for i in range(DEPTH):
        last = i == DEPTH - 1
        m_l = jnp.split(silu_c @ w_mod[i] + b_mod[i], 6, axis=-1)
        m_c = jnp.split(silu_cc @ w_mod[i] + b_mod[i], 6, axis=-1)
        h_lat = rmsnorm(x, norm_mix[i]) * (1 + m_l[1]) + m_l[0]
        h_ctx = rmsnorm(ctx, norm_mix[i]) * (1 + m_c[1]) + m_c[0]
        j = i // N_MIXERS
        if i % N_MIXERS == 0:
            lambda_init = 0.8 - 0.6 * math.exp(-0.3 * i)
            y_ctx, y_lat = diff_attention_mixer(h_ctx, h_lat, da_w_qkv[j], da_lambda[j], da_head_gain[j], da_w_o[j],
                                                lambda_init, cos, sin, not last)
        else:
            y_ctx, y_lat = gated_deltanet_mixer(h_ctx, h_lat, gdn_w_in[j], gdn_conv[j], gdn_a_log[j], gdn_dt_bias[j],
                                                gdn_norm_gain[j], gdn_w_o[j], not last)
        x = x + m_l[2] * y_lat
        h_lat = rmsnorm(x, norm_ffn[i]) * (1 + m_l[4]) + m_l[3]
        x = x + m_l[5] * conv_ffn(h_lat, ffn_w_up[i], ffn_conv[i], ffn_w_down[i])
        if not last:
            ctx = ctx + m_c[2] * y_ctx
            h_ctx = rmsnorm(ctx, norm_ffn[i]) * (1 + m_c[4]) + m_c[3]
            ctx = ctx + m_c[5] * conv_ffn(h_ctx, ffn_w_up[i], ffn_conv[i], ffn_w_down[i])
    return rmsnorm(x, final_norm)
```

```python
import math
from contextlib import ExitStack

import numpy as np
import concourse.bass as bass
import concourse.mybir as mybir
from concourse.bass_utils import run_bass_kernel_spmd

F32 = mybir.dt.float32
BF16 = mybir.dt.bfloat16
AF = mybir.ActivationFunctionType
ALU = mybir.AluOpType
AX = mybir.AxisListType

D = 2048
T_LAT = 2048
T_CTX = 256
T_ALL = T_LAT + T_CTX
NT = T_ALL // 128
KD = D // 128
DEPTH = 4
EPS = 1e-6
D_FF = 5504
NFF = D_FF // 128
GDN_IN_W = 12416
NQ = 12


class T:
    __slots__ = ("name", "w", "r")

    def __init__(self, name=""):
        self.name = name
        self.w = None
        self.r = []


class Op:
    __slots__ = ("eng", "sig")

    def __init__(self, eng):
        self.eng = eng
        self.sig = None


class FW:
    CE = ("pe", "act", "dve", "pool")

    def __init__(self, nc, stack):
        self.nc = nc
        self.h = {"pe": nc.tensor, "act": nc.scalar, "dve": nc.vector, "pool": nc.gpsimd, "sp": nc.sync}
        self.sem = {e: stack.enter_context(nc.semaphore("s_" + e)) for e in self.CE}
        self.cnt = {e: 0 for e in self.CE}
        self.pending = {e: [] for e in self.CE}
        self.waited = {e: {} for e in self.h}
        self.dsem, self.dcnt, self.drr, self.dlast = {}, {}, {}, {}
        for q in ("sp", "pool"):
            self.dsem[q] = [stack.enter_context(nc.semaphore("d_%s%d" % (q, i))) for i in range(NQ)]
            self.dcnt[q] = [0] * NQ
            self.drr[q] = 0
            self.dlast[q] = [None] * NQ
        self.n_inst = 0
        self.n_wait = 0
        self.uid = 0

    def name(self, p):
        self.uid += 1
        return "%s_%d" % (p, self.uid)

    def _deps(self, reads, writes):
        deps = []
        for t in reads:
            if t.w is not None:
                deps.append((t.w, True))
        for t in writes:
            if t.w is not None:
                deps.append((t.w, False))
            for r in t.r:
                deps.append((r, False))
        return deps

    def _update(self, op, reads, writes):
        for t in writes:
            t.w = op
            t.r = []
        for t in reads:
            if t.w is not op:
                t.r.append(op)

    def _wait(self, eng, sem, val):
        w = self.waited[eng]
        k = id(sem)
        if w.get(k, 0) >= val:
            return
        self.h[eng].wait_ge(sem, val)
        self.n_wait += 1
        w[k] = val

    def _emit_waits(self, eng, deps, extra=()):
        need = {}
        for (d, raw) in deps:
            if d.eng == eng and eng in self.CE:
                if not raw or eng == "pe":
                    continue
            if d.sig is None:
                raise RuntimeError("dependency on unsignaled op")
            sem, val = d.sig
            k = id(sem)
            if k not in need or need[k][1] < val:
                need[k] = (sem, val)
        for d in extra:
            sem, val = d.sig
            k = id(sem)
            if k not in need or need[k][1] < val:
                need[k] = (sem, val)
        for k, (sem, val) in need.items():
            self._wait(eng, sem, val)

    def op(self, eng, fn, reads=(), writes=(), sig=True):
        deps = self._deps(reads, writes)
        self._emit_waits(eng, deps)
        inst = fn(self.h[eng])
        o = Op(eng)
        self.n_inst += 1
        if sig:
            self.cnt[eng] += 1
            inst.then_inc(self.sem[eng], 1)
            o.sig = (self.sem[eng], self.cnt[eng])
            for p in self.pending[eng]:
                p.sig = o.sig
            self.pending[eng] = []
        else:
            self.pending[eng].append(o)
        self._update(o, reads, writes)
        return o

    def dma(self, q, out, in_, reads=(), writes=(), **kw):
        deps = self._deps(reads, writes)
        i = self.drr[q]
        self.drr[q] = (i + 1) % NQ
        extra = [self.dlast[q][i]] if self.dlast[q][i] is not None else []
        self._emit_waits(q, deps, extra)
        self.dcnt[q][i] += 16
        self.h[q].dma_start(out=out, in_=in_, **kw).then_inc(self.dsem[q][i], 16)
        o = Op("dma_" + q)
        o.sig = (self.dsem[q][i], self.dcnt[q][i])
        self.dlast[q][i] = o
        self.n_inst += 1
        self._update(o, reads, writes)
        return o

    def barrier(self):
        for e in self.CE:
            assert not self.pending[e], "unsignaled ops pending at barrier"
        for eng in self.h:
            for e in self.CE:
                if e != eng and self.cnt[e] > 0:
                    self._wait(eng, self.sem[e], self.cnt[e])
            if eng in self.CE and eng != "pe" and self.cnt[eng] > 0:
                self._wait(eng, self.sem[eng], self.cnt[eng])
            for q in self.dsem:
                for i in range(NQ):
                    if self.dcnt[q][i] > 0:
                        self._wait(eng, self.dsem[q][i], self.dcnt[q][i])


class Phase:
    def __init__(self, fw):
        self.fw = fw
        self.st = ExitStack()

    def sb(self, shape, dtype, name="sb"):
        t = self.st.enter_context(self.fw.nc.sbuf_tensor(self.fw.name(name), list(shape), dtype))
        return t

    def ps(self, shape, dtype, name="ps"):
        t = self.st.enter_context(self.fw.nc.psum_tensor(self.fw.name(name), list(shape), dtype))
        return t

    def close(self):
        self.fw.barrier()
        self.st.close()


class Prog:
    def __init__(self, nc, fw, n_layers, dbg):
        self.nc, self.fw, self.n_layers, self.dbg = nc, fw, n_layers, dbg
        dt = nc.dram_tensor
        I = {}
        I["x"] = dt("x", [T_LAT, D], F32, kind="ExternalInput").ap()
        I["ctx"] = dt("ctx", [T_CTX, D], F32, kind="ExternalInput").ap()
        I["cT"] = dt("cT", [128, KD, 2], F32, kind="ExternalInput").ap()
        I["w_mod"] = dt("w_mod", [DEPTH, D, 6 * D], F32, kind="ExternalInput").ap()
        I["b_mod"] = dt("b_mod", [DEPTH, 6 * D], F32, kind="ExternalInput").ap()
        I["norm_mix"] = dt("norm_mix", [DEPTH, D], F32, kind="ExternalInput").ap()
        I["norm_ffn"] = dt("norm_ffn", [DEPTH, D], F32, kind="ExternalInput").ap()
        I["da_w_qkv"] = dt("da_w_qkv", [2, D, 3 * D], F32, kind="ExternalInput").ap()
        I["da_lambda"] = dt("da_lambda", [2, 4, 128], F32, kind="ExternalInput").ap()
        I["da_gainT"] = dt("da_gainT", [2, 128, 2], F32, kind="ExternalInput").ap()
        I["da_w_o"] = dt("da_w_o", [2, D, D], F32, kind="ExternalInput").ap()
        I["gdn_w_in"] = dt("gdn_w_in", [2, D, GDN_IN_W], F32, kind="ExternalInput").ap()
        I["gdn_convT"] = dt("gdn_convT", [2, 128, 64, 5], F32, kind="ExternalInput").ap()
        I["gdn_a_log"] = dt("gdn_a_log", [2, 64], F32, kind="ExternalInput").ap()
        I["gdn_dt_bias"] = dt("gdn_dt_bias", [2, 64], F32, kind="ExternalInput").ap()
        I["gdn_norm_gain"] = dt("gdn_norm_gain", [2, 128], F32, kind="ExternalInput").ap()
        I["gdn_w_o"] = dt("gdn_w_o", [2, 2 * D, D], F32, kind="ExternalInput").ap()
        I["ffn_w_up"] = dt("ffn_w_up", [DEPTH, D, 2 * D_FF], F32, kind="ExternalInput").ap()
        I["ffn_convT"] = dt("ffn_convT", [DEPTH, 128, 2 * NFF, 3], F32, kind="ExternalInput").ap()
        I["ffn_w_down"] = dt("ffn_w_down", [DEPTH, D_FF, D], F32, kind="ExternalInput").ap()
        I["final_norm"] = dt("final_norm", [D], F32, kind="ExternalInput").ap()
        I["rope"] = dt("rope", [128, 16, 2, 64], F32, kind="ExternalInput").ap()
        self.I = I
        self.out = dt("out", [T_LAT, D], F32, kind="ExternalOutput").ap()
        self.xs = dt("xs", [T_ALL, D], F32, kind="Internal").ap()
        self.mods = dt("mods", [DEPTH, 2, 6 * D], F32, kind="Internal").ap()
        self.qT_s = dt("qT_s", [16, 128, T_ALL], BF16, kind="Internal").ap()
        self.kT_s = dt("kT_s", [16, 128, T_ALL], BF16, kind="Internal").ap()
        self.v_s = dt("v_s", [T_ALL, D], BF16, kind="Internal").ap()
        self.g_s = dt("g_s", [NFF, 128, T_ALL], BF16, kind="Internal").ap()
        self.t_xs = [T("xs%d" % i) for i in range(NT)]
        self.t_mods = [T("mods%d" % i) for i in range(DEPTH)]
        self.t_qT = [T("qT%d" % i) for i in range(16)]
        self.t_kT = [T("kT%d" % i) for i in range(16)]
        self.t_v = [T("v%d" % i) for i in range(4)]
        self.t_g = [T("g%d" % i) for i in range(NFF)]
        self.t_out = T("out")
        self.t_none = T("const")

    def consts(self, ph):
        fw = self.fw
        self.ident_bf = ph.sb([128, 128], BF16, "identb")
        self.ident_f = ph.sb([128, 128], F32, "identf")
        self.ones_bf = ph.sb([128, 128], BF16, "onesb")
        self.ones_f = ph.sb([128, 128], F32, "onesf")
        tc_ = T("consts")
        self.t_c = tc_

        fw.op("pool", lambda e: e.memset(self.ident_f[:], 0.0), writes=[tc_])
        fw.op("pool", lambda e: e.affine_select(out=self.ident_f[:], in_=self.ident_f[:], compare_op=ALU.not_equal, fill=1.0, base=0,
                                                pattern=[[-1, 128]], channel_multiplier=1), reads=[tc_], writes=[tc_])
        fw.op("pool", lambda e: e.memset(self.ones_f[:], 1.0), writes=[tc_])
        fw.op("pool", lambda e: e.memset(self.ones_bf[:], 1.0), writes=[tc_])
        fw.op("dve", lambda e: e.tensor_copy(out=self.ident_bf[:], in_=self.ident_f[:]), reads=[tc_], writes=[tc_])

    def init_xs(self, ph):
        fw = self.fw
        buf = [ph.sb([128, D], F32, "cp") for _ in range(3)]
        tb = [T("cp%d" % i) for i in range(3)]
        for tt in range(NT):
            src = self.I["x"][tt * 128:(tt + 1) * 128, :] if tt < 16 else self.I["ctx"][(tt - 16) * 128:(tt - 15) * 128, :]
            b = tt % 3
            fw.dma("sp", buf[b][:], src, writes=[tb[b]])
            fw.dma("sp", self.xs[tt * 128:(tt + 1) * 128, :], buf[b][:], reads=[tb[b]], writes=[self.t_xs[tt]])

    def mods_phase(self, ph):
        fw, I = self.fw, self.I
        cT = ph.sb([128, KD, 2], F32, "cT")
        sc = ph.sb([128, KD, 2], F32, "scT")
        t_c, t_sc = T("cT"), T("scT")
        fw.dma("sp", cT[:], I["cT"], writes=[t_c])
        fw.op("act", lambda e: e.activation(out=sc[:], in_=cT[:], func=AF.Silu), reads=[t_c], writes=[t_sc])
        wb = [ph.sb([128, KD, 512], F32, "wmod") for _ in range(2)]
        t_wb = [T("wmod0"), T("wmod1")]
        m_sb = ph.sb([2, 6 * D], F32, "m_sb")
        b2 = ph.sb([2, 6 * D], F32, "b2")
        nm = ph.sb([2, 2, D], F32, "nm")
        t_m, t_b2, t_nm = T("m_sb"), T("b2"), T("nm")
        pp = [ph.ps([2, 512], F32, "modps") for _ in range(2)]
        t_pp = [T("modps0"), T("modps1")]
        cnt = 0
        for i in range(self.n_layers):
            fw.dma("sp", b2[:], I["b_mod"][i, :].partition_broadcast(2), reads=[], writes=[t_b2])
            fw.dma("sp", nm[:, 0, :], I["norm_mix"][i, :].partition_broadcast(2), writes=[t_nm])
            fw.dma("sp", nm[:, 1, :], I["norm_ffn"][i, :].partition_broadcast(2), writes=[t_nm])
            for n in range(24):
                b = cnt % 2
                cnt += 1
                w = wb[b]
                fw.dma("sp", w[:], I["w_mod"][i, :, n * 512:(n + 1) * 512].rearrange("(k p) n -> p k n", p=128),
                       writes=[t_wb[b]])
                p_ = pp[b]
                for k in range(KD):
                    fw.op("pe", lambda e, k=k: e.matmul(p_[:], lhsT=sc[:, k, :], rhs=w[:, k, :], start=(k == 0), stop=(k == KD - 1)),
                          reads=[t_sc, t_wb[b]], writes=[t_pp[b]], sig=(k == KD - 1))
                fw.op("dve", lambda e: e.tensor_tensor(out=m_sb[:, n * 512:(n + 1) * 512], in0=p_[:], in1=b2[:, n * 512:(n + 1) * 512], op=ALU.add),
                      reads=[t_pp[b], t_b2], writes=[t_m])
            for s, r in ((1, 0), (4, 1)):
                fw.op("dve", lambda e, s=s, r=r: e.scalar_tensor_tensor(out=m_sb[:, s * D:(s + 1) * D], in0=m_sb[:, s * D:(s + 1) * D], scalar=1.0,
                                                                   in1=nm[:, r, :], op0=ALU.add, op1=ALU.mult),
                      reads=[t_m, t_nm], writes=[t_m])
            fw.dma("sp", self.mods[i], m_sb[:], reads=[t_m], writes=[self.t_mods[i]])

    def load_bc(self, dst, t_dst, layer, row, slot):
        self.fw.dma("sp", dst, self.mods[layer, row, slot * D:(slot + 1) * D].partition_broadcast(128),
                    reads=[self.t_mods[layer]], writes=[t_dst])

    def norm_phase(self, ph, layer, gslot, sslot, hT, t_hT, tiles):
        fw = self.fw
        G = [ph.sb([128, D], F32, "G") for _ in range(2)]
        S = [ph.sb([128, D], F32, "S") for _ in range(2)]
        t_G = T("G")
        for r in range(2):
            self.load_bc(G[r][:], t_G, layer, r, gslot)
            self.load_bc(S[r][:], t_G, layer, r, sslot)
        xb = [ph.sb([128, D], F32, "xb") for _ in range(2)]
        t_xb = [T("xb0"), T("xb1")]
        junk = ph.sb([128, D], BF16, "junk")
        t_junk = T("junk")
        ss = ph.sb([128, 4], F32, "ss")
        t_ss = T("ss")
        hf = ph.sb([128, D], F32, "hf")
        hb = ph.sb([128, D], BF16, "hb")
        t_hf, t_hb = T("hf"), T("hb")
        pt = [ph.ps([128, 1024], BF16, "pt") for _ in range(2)]
        t_pt = [T("pt0"), T("pt1")]

        def load(i):
            tt = tiles[i]
            fw.dma("sp", xb[i % 2][:], self.xs[tt * 128:(tt + 1) * 128, :], reads=[self.t_xs[tt]], writes=[t_xb[i % 2]])

        load(0)
        for i, tt in enumerate(tiles):
            if i + 1 < len(tiles):
                load(i + 1)
            x_ = xb[i % 2]
            tx = t_xb[i % 2]
            r = 0 if tt < 16 else 1
            fw.op("act", lambda e: e.activation(out=junk[:], in_=x_[:], func=AF.Square, accum_out=ss[:, 0:1]),
                  reads=[tx], writes=[t_junk, t_ss])
            fw.op("act", lambda e: e.activation(out=ss[:, 1:2], in_=ss[:, 0:1], func=AF.Sqrt, scale=1.0 / D, bias=EPS),
                  reads=[t_ss], writes=[t_ss])
            fw.op("dve", lambda e: e.reciprocal(out=ss[:, 2:3], in_=ss[:, 1:2]), reads=[t_ss], writes=[t_ss])
            fw.op("dve", lambda e: e.scalar_tensor_tensor(out=hf[:], in0=x_[:], scalar=ss[:, 2:3], in1=G[r][:], op0=ALU.mult, op1=ALU.mult),
                  reads=[tx, t_ss, t_G], writes=[t_hf])
            fw.op("dve", lambda e: e.tensor_tensor(out=hb[:], in0=hf[:], in1=S[r][:], op=ALU.add), reads=[t_hf, t_G], writes=[t_hb])
            for half in range(2):
                p_ = pt[half]
                for kk in range(8):
                    k = half * 8 + kk
                    fw.op("pe", lambda e, k=k, kk=kk: e.transpose(out=p_[:, kk * 128:(kk + 1) * 128], in_=hb[:, k * 128:(k + 1) * 128], identity=self.ident_bf[:]),
                          reads=[t_hb, self.t_c], writes=[t_pt[half]], sig=(kk == 7))
                eng = "act" if half == 0 else "dve"
                dst = hT[:, half * 8:(half + 1) * 8, tt * 128:(tt + 1) * 128]
                src = p_[:].rearrange("p (k t) -> p k t", k=8)
                if eng == "act":
                    fw.op("act", lambda e: e.activation(out=dst, in_=src, func=AF.Copy), reads=[t_pt[half]], writes=[t_hT[tt]])
                else:
                    fw.op("dve", lambda e: e.tensor_copy(out=dst, in_=src), reads=[t_pt[half]], writes=[t_hT[tt]])

    def ffn(self, layer, tiles):
        fw, I = self.fw, self.I
        has_ctx = len(tiles) == NT
        ph0 = Phase(fw)
        hT = ph0.sb([128, KD, T_ALL], BF16, "h2T")
        t_hT = [T("h2T%d" % i) for i in range(NT)]
        ph = Phase(fw)
        self.norm_phase(ph, layer, 4, 3, hT, t_hT, tiles)
        ph.close()
        ph = Phase(fw)
        WP = 2308
        wu = [ph.sb([128, KD, 256], BF16, "wu") for _ in range(3)]
        t_wu = [T("wu%d" % i) for i in range(3)]
        cw = ph.sb([128, 2 * NFF, 3], F32, "cw")
        t_cw = T("cw")
        fw.dma("sp", cw[:], I["ffn_convT"][layer], writes=[t_cw])
        U = [[ph.sb([128, WP], F32, "U") for _ in range(2)] for _ in range(2)]
        t_U = [[T("U") for _ in range(2)] for _ in range(2)]
        for gv in range(2):
            for b in range(2):
                fw.op("pool", lambda e, gv=gv, b=b: e.memset(U[gv][b][:], 0.0), writes=[t_U[gv][b]])
        cg = ph.sb([128, WP], F32, "cg")
        cv = ph.sb([128, WP], F32, "cv")
        ctmp = ph.sb([128, WP], F32, "ctmp")
        sg = ph.sb([128, WP], F32, "sg")
        t_cg, t_cv, t_ctmp, t_sg = T("cg"), T("cv"), T("ctmp"), T("sg")
        gst = [ph.sb([128, WP], BF16, "gst") for _ in range(2)]
        t_gst = [T("gst0"), T("gst1")]
        PA = [ph.ps([128, 2048], F32, "PA"), ph.ps([128, 2048], F32, "PB")]
        t_PA = [T("PA"), T("PB")]

        def load_w(j):
            b = j % 3
            for gv in range(2):
                c0 = gv * D_FF + j * 128
                fw.dma("pool", wu[b][:, :, gv * 128:(gv + 1) * 128],
                       I["ffn_w_up"][layer, :, c0:c0 + 128].rearrange("(k p) n -> p k n", p=128), writes=[t_wu[b]])

        load_w(0)
        load_w(1)
        for j in range(NFF):
            if j + 2 < NFF:
                load_w(j + 2)
            w = wu[j % 3]
            tw = t_wu[j % 3]
            ub = j % 2
            for gv in range(2):
                for blk in range(4):
                    for k in range(KD):
                        fw.op("pe", lambda e, gv=gv, blk=blk, k=k: e.matmul(PA[gv][:, blk * 512:(blk + 1) * 512], lhsT=w[:, k, gv * 128:(gv + 1) * 128],
                                                                          rhs=hT[:, k, blk * 512:(blk + 1) * 512], start=(k == 0), stop=(k == KD - 1)),
                              reads=[tw] + t_hT[blk * 4:(blk + 1) * 4], writes=[t_PA[gv]], sig=(k == KD - 1))
                eng = "act" if gv == 0 else "dve"
                if eng == "act":
                    fw.op("act", lambda e, gv=gv: e.activation(out=U[gv][ub][:, 1:2049], in_=PA[gv][:], func=AF.Copy), reads=[t_PA[gv]], writes=[t_U[gv][ub]])
                else:
                    fw.op("dve", lambda e, gv=gv: e.tensor_copy(out=U[gv][ub][:, 1:2049], in_=PA[gv][:]), reads=[t_PA[gv]], writes=[t_U[gv][ub]])
            if has_ctx:
                for gv in range(2):
                    for k in range(KD):
                        fw.op("pe", lambda e, gv=gv, k=k: e.matmul(PA[gv][:, 0:256], lhsT=w[:, k, gv * 128:(gv + 1) * 128], rhs=hT[:, k, 2048:2304],
                                                                  start=(k == 0), stop=(k == KD - 1)),
                              reads=[tw] + t_hT[16:18], writes=[t_PA[gv]], sig=(k == KD - 1))
                    fw.op("act", lambda e, gv=gv: e.activation(out=U[gv][ub][:, 2051:2307], in_=PA[gv][:, 0:256], func=AF.Copy), reads=[t_PA[gv]], writes=[t_U[gv][ub]])
            Ug, Uv = U[0][ub], U[1][ub]
            jg, jv = j, NFF + j
            fw.op("dve", lambda e: e.tensor_scalar(out=cg[:, 1:2307], in0=Ug[:, 1:2307], scalar1=cw[:, jg, 1:2], scalar2=None, op0=ALU.mult),
                  reads=[t_U[0][ub], t_cw], writes=[t_cg])
            fw.op("dve", lambda e: e.scalar_tensor_tensor(out=cg[:, 1:2307], in0=Ug[:, 0:2306], scalar=cw[:, jg, 0:1], in1=cg[:, 1:2307], op0=ALU.mult, op1=ALU.add),
                  reads=[t_U[0][ub], t_cw, t_cg], writes=[t_cg])
            fw.op("dve", lambda e: e.scalar_tensor_tensor(out=cg[:, 1:2307], in0=Ug[:, 2:2308], scalar=cw[:, jg, 2:3], in1=cg[:, 1:2307], op0=ALU.mult, op1=ALU.add),
                  reads=[t_U[0][ub], t_cw, t_cg], writes=[t_cg])
            fw.op("act", lambda e: e.activation(out=sg[:, 1:2307], in_=cg[:, 1:2307], func=AF.Silu), reads=[t_cg], writes=[t_sg])
            fw.op("pool", lambda e: e.tensor_scalar(out=cv[:, 1:2307], in0=Uv[:, 1:2307], scalar1=cw[:, jv, 1:2], scalar2=0.0, op0=ALU.mult, op1=ALU.add),
                  reads=[t_U[1][ub], t_cw], writes=[t_cv])
            fw.op("pool", lambda e: e.tensor_scalar(out=ctmp[:, 1:2307], in0=Uv[:, 0:2306], scalar1=cw[:, jv, 0:1], scalar2=0.0, op0=ALU.mult, op1=ALU.add),
                  reads=[t_U[1][ub], t_cw], writes=[t_ctmp])
            fw.op("pool", lambda e: e.tensor_tensor(out=cv[:, 1:2307], in0=cv[:, 1:2307], in1=ctmp[:, 1:2307], op=ALU.add),
                  reads=[t_cv, t_ctmp], writes=[t_cv])
            fw.op("pool", lambda e: e.tensor_scalar(out=ctmp[:, 1:2307], in0=Uv[:, 2:2308], scalar1=cw[:, jv, 2:3], scalar2=0.0, op0=ALU.mult, op1=ALU.add),
                  reads=[t_U[1][ub], t_cw], writes=[t_ctmp])
            fw.op("pool", lambda e: e.tensor_tensor(out=cv[:, 1:2307], in0=cv[:, 1:2307], in1=ctmp[:, 1:2307], op=ALU.add),
                  reads=[t_cv, t_ctmp], writes=[t_cv])
            gs_ = gst[j % 2]
            fw.op("dve", lambda e: e.tensor_tensor(out=gs_[:, 1:2307], in0=sg[:, 1:2307], in1=cv[:, 1:2307], op=ALU.mult),
                  reads=[t_sg, t_cv], writes=[t_gst[j % 2]])
            fw.dma("sp", self.g_s[j, :, 0:2048], gs_[:, 1:2049], reads=[t_gst[j % 2]], writes=[self.t_g[j]])
            if has_ctx:
                fw.dma("sp", self.g_s[j, :, 2048:2304], gs_[:, 2051:2307], reads=[t_gst[j % 2]], writes=[self.t_g[j]])
        ph.close()
        ph0.close()
        ph = Phase(fw)
        gate = [ph.sb([128, D], F32, "gate2") for _ in range(2)]
        t_gate = T("gate2")
        for r in range(2):
            self.load_bc(gate[r][:], t_gate, layer, r, 5)
        wd = [ph.sb([128, NFF, 512], BF16, "wd") for _ in range(2)]
        t_wd = [T("wd0"), T("wd1")]
        gb = [ph.sb([128, NFF, 512], BF16, "gb") for _ in range(2)]
        t_gb = [T("gb0"), T("gb1")]
        xt = [ph.sb([128, 512], F32, "xt") for _ in range(3)]
        t_xt = [T("xt%d" % i) for i in range(3)]
        tmp = ph.sb([128, 512], F32, "tmp")
        t_tmp = T("tmp")
        pd = [ph.ps([128, 512], F32, "pd") for _ in range(2)]
        t_pd = [T("pd0"), T("pd1")]
        blocks = [(0, 512), (512, 512), (1024, 512), (1536, 512)] + ([(2048, 256)] if has_ctx else [])

        def load_wd(n):
            b = n % 2
            for c in range(0, NFF, 11):
                c1 = min(NFF, c + 11)
                fw.dma("pool", wd[b][:, c:c1, :],
                       I["ffn_w_down"][layer, c * 128:c1 * 128, n * 512:(n + 1) * 512].rearrange("(j p) n -> p j n", p=128), writes=[t_wd[b]])

        seq = [(n, bi) for n in range(4) for bi in range(len(blocks))]

        def load_gb(si):
            n, bi = seq[si]
            t0, tl = blocks[bi]
            fw.dma("sp", gb[si % 2][:, :, 0:tl], self.g_s[:, :, t0:t0 + tl].rearrange("j p t -> p j t"), reads=self.t_g, writes=[t_gb[si % 2]])

        load_wd(0)
        load_gb(0)
        xi = 0
        for si, (n, bi) in enumerate(seq):
            if bi == 0 and n + 1 < 4:
                load_wd(n + 1)
            if si + 1 < len(seq):
                load_gb(si + 1)
            t0, tl = blocks[bi]
            g_ = gb[si % 2]
            w_ = wd[n % 2]
            for lt in range(tl // 128):
                tt = (t0 + lt * 128) // 128
                r = 0 if tt < 16 else 1
                xb_ = xt[xi % 3]
                txb = t_xt[xi % 3]
                xi += 1
                fw.dma("sp", xb_[:], self.xs[tt * 128:(tt + 1) * 128, n * 512:(n + 1) * 512], reads=[self.t_xs[tt]], writes=[txb])
                p_ = pd[lt % 2]
                tp = t_pd[lt % 2]
                for j in range(NFF):
                    fw.op("pe", lambda e, j=j, lt=lt: e.matmul(p_[:], lhsT=g_[:, j, lt * 128:(lt + 1) * 128], rhs=w_[:, j, :], start=(j == 0), stop=(j == NFF - 1)),
                          reads=[t_gb[si % 2], t_wd[n % 2]], writes=[tp], sig=(j == NFF - 1))
                fw.op("dve", lambda e: e.tensor_tensor(out=tmp[:], in0=p_[:], in1=gate[r][:, n * 512:(n + 1) * 512], op=ALU.mult),
                      reads=[tp, t_gate], writes=[t_tmp])
                fw.op("dve", lambda e: e.tensor_tensor(out=xb_[:], in0=xb_[:], in1=tmp[:], op=ALU.add), reads=[txb, t_tmp], writes=[txb])
                fw.dma("sp", self.xs[tt * 128:(tt + 1) * 128, n * 512:(n + 1) * 512], xb_[:], reads=[txb], writes=[self.t_xs[tt]])
        ph.close()

    def da_layer(self, layer):
        fw, I = self.fw, self.I
        j = layer // 2
        li = 0.8 - 0.6 * math.exp(-0.3 * layer)
        tiles = list(range(NT))
        ph0 = Phase(fw)
        hT = ph0.sb([128, KD, T_ALL], BF16, "h1T")
        t_hT = [T("h1T%d" % i) for i in range(NT)]
        ph = Phase(fw)
        self.norm_phase(ph, layer, 1, 0, hT, t_hT, tiles)
        ph.close()
        ph = Phase(fw)
        wq = [ph.sb([128, KD, 512], BF16, "wq") for _ in range(2)]
        t_wq = [T("wq0"), T("wq1")]
        rope = ph.sb([128, 16, 2, 64], F32, "rope")
        t_rope = T("rope")
        fw.dma("sp", rope[:], I["rope"], writes=[t_rope])
        pq = [ph.ps([128, 512], F32, "pq") for _ in range(2)]
        t_pq = [T("pq0"), T("pq1")]
        ptr = [ph.ps([128, 512], BF16, "ptr") for _ in range(2)]
        t_ptr = [T("ptr0"), T("ptr1")]
        qf = [ph.sb([128, 512], F32, "qf") for _ in range(2)]
        t_qf = [T("qf0"), T("qf1")]
        r1 = ph.sb([128, 256], F32, "r1")
        r2 = ph.sb([128, 256], F32, "r2")
        t_r1, t_r2 = T("r1"), T("r2")
        qr = [ph.sb([128, 512], BF16, "qr") for _ in range(2)]
        t_qr = [T("qr0"), T("qr1")]
        stT = [ph.sb([128, 4, T_ALL], BF16, "stT") for _ in range(2)]
        t_stT = [T("stT0"), T("stT1")]
        stV = [ph.sb([128, NT, 512], BF16, "stV") for _ in range(2)]
        t_stV = [T("stV0"), T("stV1")]

        def load_wq(n):
            fw.dma("pool", wq[n % 2][:], I["da_w_qkv"][j, :, n * 512:(n + 1) * 512].rearrange("(k p) n -> p k n", p=128), writes=[t_wq[n % 2]])

        load_wq(0)
        it = 0
        pend_tr = []
        for n in range(12):
            if n + 1 < 12:
                load_wq(n + 1)
            w_ = wq[n % 2]
            tw = t_wq[n % 2]
            for tt in tiles:
                b = it % 2
                it += 1
                p_ = pq[b]
                for k in range(KD):
                    fw.op("pe", lambda e, k=k, tt=tt: e.matmul(p_[:], lhsT=hT[:, k, tt * 128:(tt + 1) * 128], rhs=w_[:, k, :], start=(k == 0), stop=(k == KD - 1)),
                          reads=[tw, t_hT[tt]], writes=[t_pq[b]], sig=(k == KD - 1))
                while pend_tr:
                    pend_tr.pop(0)()
                if n >= 8:
                    sv = stV[n % 2]
                    fw.op("act", lambda e, tt=tt: e.activation(out=sv[:, tt, :], in_=p_[:], func=AF.Copy), reads=[t_pq[b]], writes=[t_stV[n % 2]])
                    continue
                q_ = qr[b]
                if tt < 16:
                    f_ = qf[b]
                    fw.op("act", lambda e: e.activation(out=f_[:], in_=p_[:], func=AF.Copy), reads=[t_pq[b]], writes=[t_qf[b]])
                    fv = f_[:].rearrange("p (h a two f) -> p h a two f", h=4, a=2, two=2)
                    qv = q_[:].rearrange("p (h a two f) -> p h a two f", h=4, a=2, two=2)
                    x1, x2 = fv[:, :, :, 0, :], fv[:, :, :, 1, :]
                    cosv = rope[:, tt, 0, :].rearrange("p (a f) -> p a f", a=2).unsqueeze(1).to_broadcast([128, 4, 2, 32])
                    sinv = rope[:, tt, 1, :].rearrange("p (a f) -> p a f", a=2).unsqueeze(1).to_broadcast([128, 4, 2, 32])
                    r1v = r1[:].rearrange("p (h a f) -> p h a f", h=4, a=2)
                    r2v = r2[:].rearrange("p (h a f) -> p h a f", h=4, a=2)
                    fw.op("dve", lambda e: e.tensor_tensor(out=r1v, in0=x1, in1=cosv, op=ALU.mult), reads=[t_qf[b], t_rope], writes=[t_r1])
                    fw.op("dve", lambda e: e.tensor_tensor(out=r2v, in0=x2, in1=sinv, op=ALU.mult), reads=[t_qf[b], t_rope], writes=[t_r2])
                    fw.op("dve", lambda e: e.tensor_tensor(out=qv[:, :, :, 0, :], in0=r1v, in1=r2v, op=ALU.subtract), reads=[t_r1, t_r2], writes=[t_qr[b]])
                    fw.op("dve", lambda e: e.tensor_tensor(out=r1v, in0=x2, in1=cosv, op=ALU.mult), reads=[t_qf[b], t_rope], writes=[t_r1])
                    fw.op("dve", lambda e: e.tensor_tensor(out=r2v, in0=x1, in1=sinv, op=ALU.mult), reads=[t_qf[b], t_rope], writes=[t_r2])
                    fw.op("dve", lambda e: e.tensor_tensor(out=qv[:, :, :, 1, :], in0=r1v, in1=r2v, op=ALU.add), reads=[t_r1, t_r2], writes=[t_qr[b]])
                else:
                    fw.op("act", lambda e: e.activation(out=q_[:], in_=p_[:], func=AF.Copy), reads=[t_pq[b]], writes=[t_qr[b]])
                def tr_step(b=b, q_=q_, tt=tt, n=n):
                    pt_ = ptr[b]
                    for c in range(4):
                        fw.op("pe", lambda e, c=c: e.transpose(out=pt_[:, c * 128:(c + 1) * 128], in_=q_[:, c * 128:(c + 1) * 128], identity=self.ident_bf[:]),
                              reads=[t_qr[b], self.t_c], writes=[t_ptr[b]], sig=(c == 3))
                    st_ = stT[n % 2]
                    fw.op("act", lambda e: e.activation(out=st_[:, :, tt * 128:(tt + 1) * 128], in_=pt_[:].rearrange("p (c t) -> p c t", c=4), func=AF.Copy),
                          reads=[t_ptr[b]], writes=[t_stT[n % 2]])
                pend_tr.append(tr_step)
            while pend_tr:
                pend_tr.pop(0)()
            if n < 8:
                dst = self.qT_s if n < 4 else self.kT_s
                tl = self.t_qT if n < 4 else self.t_kT
                m = n % 4
                fw.dma("sp", dst[m * 4:(m + 1) * 4].rearrange("c p t -> p c t"), stT[n % 2][:], reads=[t_stT[n % 2]], writes=tl[m * 4:(m + 1) * 4])
            else:
                m = n - 8
                fw.dma("sp", self.v_s[:, m * 512:(m + 1) * 512].rearrange("(t p) d -> p t d", p=128), stV[n % 2][:], reads=[t_stV[n % 2]], writes=[self.t_v[m]])
        ph.close()
        ph0.close()
        ph0 = Phase(fw)
        aoT = ph0.sb([128, 16, T_ALL], BF16, "aoT")
        t_ao = [T("ao%d" % i) for i in range(5)]
        ph = Phase(fw)
        lamb = ph.sb([128, 4, 128], F32, "lamb")
        lw = ph.sb([128, 2, 128], F32, "lw")
        lsc = ph.sb([128, 8], F32, "lsc")
        t_lam = T("lam")
        gcol = ph.sb([128, 2], F32, "gcol")
        fw.dma("sp", gcol[:], I["da_gainT"][j], writes=[t_lam])
        fw.dma("sp", lamb[:].rearrange("p a d -> p (a d)"), I["da_lambda"][j].rearrange("a d -> (a d)").partition_broadcast(128), writes=[t_lam])
        fw.op("dve", lambda e: e.tensor_tensor(out=lw[:, 0, :], in0=lamb[:, 0, :], in1=lamb[:, 1, :], op=ALU.mult), reads=[t_lam], writes=[t_lam])
        fw.op("dve", lambda e: e.tensor_tensor(out=lw[:, 1, :], in0=lamb[:, 2, :], in1=lamb[:, 3, :], op=ALU.mult), reads=[t_lam], writes=[t_lam])
        fw.op("dve", lambda e: e.tensor_reduce(out=lsc[:, 0:2], in_=lw[:], axis=AX.X, op=ALU.add), reads=[t_lam], writes=[t_lam])
        fw.op("act", lambda e: e.activation(out=lsc[:, 2:4], in_=lsc[:, 0:2], func=AF.Exp), reads=[t_lam], writes=[t_lam])
        fw.op("dve", lambda e: e.tensor_tensor(out=lsc[:, 4:5], in0=lsc[:, 3:4], in1=lsc[:, 2:3], op=ALU.subtract), reads=[t_lam], writes=[t_lam])
        fw.op("dve", lambda e: e.tensor_scalar(out=lsc[:, 5:6], in0=lsc[:, 4:5], scalar1=-li, scalar2=None, op0=ALU.add), reads=[t_lam], writes=[t_lam])
        neglam = lsc[:, 5:6]
        qh = [ph.sb([128, 2, T_ALL], BF16, "qh") for _ in range(2)]
        kh = [ph.sb([128, 2, T_ALL], BF16, "kh") for _ in range(2)]
        vh = [ph.sb([128, NT, 256], BF16, "vh") for _ in range(2)]
        t_qh, t_kh, t_vh = [T("qh0"), T("qh1")], [T("kh0"), T("kh1")], [T("vh0"), T("vh1")]
        NE = 4
        eb = [ph.sb([128, 512], BF16, "eb") for _ in range(NE)]
        t_eb = [T("eb%d" % i) for i in range(NE)]
        ps_s = [ph.ps([128, 512], F32, "ps_s") for _ in range(2)]
        t_ps_s = [T("ps_s0"), T("ps_s1")]
        ps_sum2 = [ph.ps([128, 512], F32, "ps_sum") for _ in range(2)]
        ps_o2 = [[ph.ps([128, 512], F32, "ps_o") for _ in range(2)] for _ in range(2)]
        t_ps_sum2, t_ps_o2 = [T("ps_sum0"), T("ps_sum1")], [[T("ps_o00"), T("ps_o01")], [T("ps_o10"), T("ps_o11")]]
        ps_ss, t_ps_ss = ps_s[0], t_ps_s[0]
        rc = ph.sb([128, 512], F32, "rc")
        t_rc = T("rc")
        oc = [[ph.sb([128, 512], F32, "oc") for _ in range(2)] for _ in range(2)]
        t_oc = [[T("oc") for _ in range(2)] for _ in range(2)]
        osum = [ph.sb([128, 512], F32, "osum") for _ in range(2)]
        t_osum = [T("osum0"), T("osum1")]
        sq = [ph.sb([128, 512], F32, "sq") for _ in range(2)]
        t_sq = [T("sq0"), T("sq1")]
        sd = ph.sb([128, 512], F32, "sd")
        rs = ph.sb([128, 512], F32, "rs")
        t_sd, t_rs = T("sd"), T("rs")
        qblocks = [(0, 512), (512, 512), (1024, 512), (1536, 512), (2048, 256)]
        scale = 128 ** -0.5

        def load_head(h):
            b = h % 2
            fw.dma("sp", qh[b][:], self.qT_s[2 * h:2 * h + 2].rearrange("c p t -> p c t"), reads=self.t_qT[2 * h:2 * h + 2], writes=[t_qh[b]])
            fw.dma("sp", kh[b][:], self.kT_s[2 * h:2 * h + 2].rearrange("c p t -> p c t"), reads=self.t_kT[2 * h:2 * h + 2], writes=[t_kh[b]])
            fw.dma("sp", vh[b][:], self.v_s[:, h * 256:(h + 1) * 256].rearrange("(t p) d -> p t d", p=128), reads=[self.t_v[h // 2]], writes=[t_vh[b]])

        load_head(0)
        ei = 0
        si = 0
        for h in range(8):
            if h + 1 < 8:
                load_head(h + 1)
            b = h % 2
            q_, k_, v_ = qh[b], kh[b], vh[b]
            for qi, (q0, Q) in enumerate(qblocks):
                keys = list(range(NT)) if q0 < T_LAT else [16, 17]
                for c in range(2):
                    ps_sum, t_ps_sum, ps_o, t_ps_o = ps_sum2[c], t_ps_sum2[c], ps_o2[c], t_ps_o2[c]

                    def av_step(ki, kc, e_, te):
                        first, last = ki == 0, ki == len(keys) - 1
                        fw.op("pe", lambda e: e.matmul(ps_sum[:, 0:Q], lhsT=self.ones_bf[:], rhs=e_[:, 0:Q], start=first, stop=last),
                              reads=[te, self.t_c], writes=[t_ps_sum], sig=last)
                        for half in range(2):
                            fw.op("pe", lambda e, half=half: e.matmul(ps_o[half][:, 0:Q], lhsT=v_[:, kc, half * 128:(half + 1) * 128], rhs=e_[:, 0:Q], start=first, stop=last),
                                  reads=[te, t_vh[b]], writes=[t_ps_o[half]], sig=last)
                    pend = None
                    for ki, kc in enumerate(keys):
                        p_ = ps_s[si % 2]
                        tp = t_ps_s[si % 2]
                        si += 1
                        fw.op("pe", lambda e, c=c, kc=kc: e.matmul(p_[:, 0:Q], lhsT=k_[:, c, kc * 128:(kc + 1) * 128], rhs=q_[:, c, q0:q0 + Q], start=True, stop=True),
                              reads=[t_kh[b], t_qh[b]], writes=[tp])
                        e_ = eb[ei % NE]
                        te = t_eb[ei % NE]
                        ei += 1
                        fw.op("act", lambda e: e.activation(out=e_[:, 0:Q], in_=p_[:, 0:Q], func=AF.Exp, scale=scale), reads=[tp], writes=[te])
                        if pend is not None:
                            av_step(*pend)
                        pend = (ki, kc, e_, te)
                    av_step(*pend)
                    fw.op("dve", lambda e: e.reciprocal(out=rc[:, 0:Q], in_=ps_sum[:, 0:Q]), reads=[t_ps_sum], writes=[t_rc])
                    for half in range(2):
                        if c == 0:
                            fw.op("dve", lambda e, half=half: e.tensor_tensor(out=oc[0][half][:, 0:Q], in0=ps_o[half][:, 0:Q], in1=rc[:, 0:Q], op=ALU.mult),
                                  reads=[t_ps_o[half], t_rc], writes=[t_oc[0][half]])
                        else:
                            fw.op("dve", lambda e, half=half: e.scalar_tensor_tensor(out=oc[1][half][:, 0:Q], in0=ps_o[half][:, 0:Q], scalar=neglam, in1=rc[:, 0:Q],
                                                                                 op0=ALU.mult, op1=ALU.mult),
                                  reads=[t_ps_o[half], t_rc, t_lam], writes=[t_oc[1][half]])
                for half in range(2):
                    fw.op("pool", lambda e, half=half: e.tensor_tensor(out=osum[half][:, 0:Q], in0=oc[0][half][:, 0:Q], in1=oc[1][half][:, 0:Q], op=ALU.add),
                          reads=[t_oc[0][half], t_oc[1][half]], writes=[t_osum[half]])
                    fw.op("act", lambda e, half=half: e.activation(out=sq[half][:, 0:Q], in_=osum[half][:, 0:Q], func=AF.Square), reads=[t_osum[half]], writes=[t_sq[half]])
                    fw.op("pe", lambda e, half=half: e.matmul(ps_ss[:, 0:Q], lhsT=self.ones_f[:], rhs=sq[half][:, 0:Q], start=(half == 0), stop=(half == 1)),
                          reads=[t_sq[half], self.t_c], writes=[t_ps_ss], sig=(half == 1))
                a_ = 1.0 / (256.0 * (1 - li) ** 2)
                fw.op("act", lambda e: e.activation(out=sd[:, 0:Q], in_=ps_ss[:, 0:Q], func=AF.Sqrt, scale=a_, bias=EPS / (1 - li) ** 2), reads=[t_ps_ss], writes=[t_sd])
                fw.op("dve", lambda e: e.reciprocal(out=rs[:, 0:Q], in_=sd[:, 0:Q]), reads=[t_sd], writes=[t_rs])
                for half in range(2):
                    fw.op("dve", lambda e, half=half: e.scalar_tensor_tensor(out=aoT[:, 2 * h + half, q0:q0 + Q], in0=osum[half][:, 0:Q], scalar=gcol[:, half:half + 1],
                                                                         in1=rs[:, 0:Q], op0=ALU.mult, op1=ALU.mult),
                          reads=[t_osum[half], t_rs, t_lam], writes=[t_ao[qi]])
        ph.close()
        ph = Phase(fw)
        gate = [ph.sb([128, D], F32, "gate1") for _ in range(2)]
        t_gate = T("gate1")
        for r in range(2):
            self.load_bc(gate[r][:], t_gate, layer, r, 2)
        wo = ph.sb([128, KD, D], BF16, "wo")
        t_wo = [T("wo%d" % i) for i in range(4)]
        for n in range(4):
            fw.dma("pool", wo[:, :, n * 512:(n + 1) * 512], I["da_w_o"][j, :, n * 512:(n + 1) * 512].rearrange("(k p) n -> p k n", p=128), writes=[t_wo[n]])
        self.oproj(ph, tiles, gate, t_gate, wo, t_wo, KD, aoT, lambda tt: [t_ao[min(tt // 4, 4)]])
        ph.close()
        ph0.close()

    def oproj(self, ph, tiles, gate, t_gate, wo, t_wo, nk, aT, t_a_of):
        fw = self.fw
        xt = [ph.sb([128, D], F32, "xt") for _ in range(2)]
        t_xt = [T("xt0"), T("xt1")]
        tmp = ph.sb([128, 512], F32, "tmp")
        t_tmp = T("tmp")
        po = [ph.ps([128, 512], F32, "po") for _ in range(2)]
        t_po = [T("po0"), T("po1")]

        def load(i):
            tt = tiles[i]
            fw.dma("sp", xt[i % 2][:], self.xs[tt * 128:(tt + 1) * 128, :], reads=[self.t_xs[tt]], writes=[t_xt[i % 2]])

        load(0)
        pi = 0
        for i, tt in enumerate(tiles):
            if i + 1 < len(tiles):
                load(i + 1)
            x_, tx = xt[i % 2], t_xt[i % 2]
            r = 0 if tt < 16 else 1
            for n in range(4):
                p_, tp = po[pi % 2], t_po[pi % 2]
                pi += 1
                for k in range(nk):
                    fw.op("pe", lambda e, k=k, n=n: e.matmul(p_[:], lhsT=aT[:, k, tt * 128:(tt + 1) * 128], rhs=wo[:, k, n * 512:(n + 1) * 512], start=(k == 0), stop=(k == nk - 1)),
                          reads=t_a_of(tt) + [t_wo[n]], writes=[tp], sig=(k == nk - 1))
                fw.op("dve", lambda e, n=n: e.tensor_tensor(out=tmp[:], in0=p_[:], in1=gate[r][:, n * 512:(n + 1) * 512], op=ALU.mult), reads=[tp, t_gate], writes=[t_tmp])
                fw.op("dve", lambda e, n=n: e.tensor_tensor(out=x_[:, n * 512:(n + 1) * 512], in0=x_[:, n * 512:(n + 1) * 512], in1=tmp[:], op=ALU.add),
                      reads=[tx, t_tmp], writes=[tx])
            fw.dma("sp", self.xs[tt * 128:(tt + 1) * 128, :], x_[:], reads=[tx], writes=[self.t_xs[tt]])

    def final_phase(self, ph):
        fw, I = self.fw, self.I
        gw = ph.sb([128, D], F32, "gw")
        t_gw = T("gw")
        fw.dma("sp", gw[:], I["final_norm"].partition_broadcast(128), writes=[t_gw])
        xb = [ph.sb([128, D], F32, "xb") for _ in range(2)]
        t_xb = [T("xb0"), T("xb1")]
        ob = [ph.sb([128, D], F32, "ob") for _ in range(2)]
        t_ob = [T("ob0"), T("ob1")]
        junk = ph.sb([128, D], BF16, "junk")
        t_junk = T("junk")
        ss = ph.sb([128, 4], F32, "ss")
        t_ss = T("ss")
        for tt in range(16):
            b = tt % 2
            fw.dma("sp", xb[b][:], self.xs[tt * 128:(tt + 1) * 128, :], reads=[self.t_xs[tt]], writes=[t_xb[b]])
            fw.op("act", lambda e: e.activation(out=junk[:], in_=xb[b][:], func=AF.Square, accum_out=ss[:, 0:1]), reads=[t_xb[b]], writes=[t_junk, t_ss])
            fw.op("act", lambda e: e.activation(out=ss[:, 1:2], in_=ss[:, 0:1], func=AF.Sqrt, scale=1.0 / D, bias=EPS), reads=[t_ss], writes=[t_ss])
            fw.op("dve", lambda e: e.reciprocal(out=ss[:, 2:3], in_=ss[:, 1:2]), reads=[t_ss], writes=[t_ss])
            fw.op("dve", lambda e: e.scalar_tensor_tensor(out=ob[b][:], in0=xb[b][:], scalar=ss[:, 2:3], in1=gw[:], op0=ALU.mult, op1=ALU.mult),
                  reads=[t_xb[b], t_ss, t_gw], writes=[t_ob[b]])
            fw.dma("sp", self.out[tt * 128:(tt + 1) * 128, :], ob[b][:], reads=[t_ob[b]], writes=[self.t_out])

    def dump_xs(self, ph):
        fw = self.fw
        buf = [ph.sb([128, D], F32, "cp") for _ in range(2)]
        tb = [T("cp0"), T("cp1")]
        for tt in range(16):
            b = tt % 2
            fw.dma("sp", buf[b][:], self.xs[tt * 128:(tt + 1) * 128, :], reads=[self.t_xs[tt]], writes=[tb[b]])
            fw.dma("sp", self.out[tt * 128:(tt + 1) * 128, :], buf[b][:], reads=[tb[b]], writes=[self.t_out])


def build(n_layers=DEPTH, dbg=None):
    nc = bass.Bass("TRN2", target_bir_lowering=False)
    with ExitStack() as st:
        fw = FW(nc, st)
        pg = Prog(nc, fw, n_layers, dbg)
        phc = Phase(fw)
        pg.consts(phc)
        ph = Phase(fw)
        pg.init_xs(ph)
        pg.mods_phase(ph)
        ph.close()
        for layer in range(n_layers):
            last = layer == DEPTH - 1
            if layer % 2 == 0:
                pg.da_layer(layer)
            else:
                pg.gdn_layer(layer, last)
            if dbg == "mix%d" % layer:
                break
            pg.ffn(layer, list(range(16)) if last else list(range(NT)))
        ph = Phase(fw)
        if dbg is None:
            pg.final_phase(ph)
        else:
            pg.dump_xs(ph)
        fw.barrier()
        ph.close()
        phc.close()
        print("program: %d instructions, %d waits" % (fw.n_inst, fw.n_wait))
    return nc


def rope_tables():
    t = np.arange(T_LAT)
    rows = (t // 64).astype(np.float32)
    cols = (t % 64).astype(np.float32)
    inv = (np.float32(10000.0) ** (-np.arange(32, dtype=np.float32) / np.float32(32))).astype(np.float32)
    ang = np.stack([rows[:, None] * inv, cols[:, None] * inv], axis=1).astype(np.float32)
    cs = np.stack([np.cos(ang), np.sin(ang)], axis=1).astype(np.float32)
    cs = cs.reshape(16, 128, 2, 64).transpose(1, 0, 2, 3)
    return np.ascontiguousarray(cs)


def make_in_maps(inp, cores):
    f = lambda a: np.ascontiguousarray(np.asarray(a, dtype=np.float32))
    shared = {
        "w_mod": f(inp["w_mod"]), "b_mod": f(inp["b_mod"]), "norm_mix": f(inp["norm_mix"]), "norm_ffn": f(inp["norm_ffn"]),
        "da_w_qkv": f(inp["da_w_qkv"]), "da_lambda": f(inp["da_lambda"]),
        "da_gainT": f(np.asarray(inp["da_head_gain"]).reshape(2, 2, 128).transpose(0, 2, 1)),
        "da_w_o": f(inp["da_w_o"]), "gdn_w_in": f(inp["gdn_w_in"]),
        "gdn_convT": f(np.asarray(inp["gdn_conv"]).reshape(2, 5, 64, 128).transpose(0, 3, 2, 1)),
        "gdn_a_log": f(np.asarray(inp["gdn_a_log"]).reshape(2, 64)), "gdn_dt_bias": f(np.asarray(inp["gdn_dt_bias"]).reshape(2, 64)),
        "gdn_norm_gain": f(inp["gdn_norm_gain"]), "gdn_w_o": f(inp["gdn_w_o"]),
        "ffn_w_up": f(inp["ffn_w_up"]),
        "ffn_convT": f(np.asarray(inp["ffn_conv"]).reshape(DEPTH, 3, 2 * NFF, 128).transpose(0, 3, 2, 1)),
        "ffn_w_down": f(inp["ffn_w_down"]), "final_norm": f(inp["final_norm"]),
        "rope": rope_tables(),
    }
    maps = []
    for b in cores:
        m = dict(shared)
        m["x"] = f(inp["x"][b])
        m["ctx"] = f(inp["ctx"][b])
        cT = np.stack([np.asarray(inp["c"][b]).reshape(KD, 128).T, np.asarray(inp["c_ctx"]).reshape(KD, 128).T], axis=-1)
        m["cT"] = f(cT)
        maps.append(m)
    return maps


def kernel(**inputs):
    nc = build()
    maps = make_in_maps(inputs, list(range(8)))
    res = run_bass_kernel_spmd(nc, maps, core_ids=list(range(8)))
    return np.stack([np.asarray(r["out"], dtype=np.float32) for r in res.results], axis=0)


def _gdn_layer(self, layer, last):
    fw, I = self.fw, self.I
    nc = self.nc
    j = layer // 2
    tiles = list(range(NT))
    dt = nc.dram_tensor
    if not hasattr(self, "gq_s"):
        self.gq_s = dt("gq_s", [36, 128, 16, 64], F32, kind="Internal").ap()
        self.gk_s = dt("gk_s", [36, 128, 16, 64], F32, kind="Internal").ap()
        self.gv_s = dt("gv_s", [36, 128, 32, 64], F32, kind="Internal").ap()
        self.z_s = dt("z_s", [T_ALL, 2 * D], BF16, kind="Internal").ap()
        self.o_s = dt("o_s", [2, T_ALL, 2 * D], F32, kind="Internal").ap()
        self.t_gq = [T("gq%d" % i) for i in range(16)]
        self.t_gk = [T("gk%d" % i) for i in range(16)]
        self.t_gv = [T("gv%d" % i) for i in range(32)]
        self.t_z = [T("z%d" % i) for i in range(8)]
        self.t_os = [[T("os") for _ in range(36)] for _ in range(2)]
    phA = Phase(fw)
    betaS = phA.sb([64, 36, 2, 32], F32, "betaS")
    gS = phA.sb([64, 36, 2, 32], F32, "gS")
    t_bg = T("betag")
    ph0 = Phase(fw)
    hT = ph0.sb([128, KD, T_ALL], BF16, "g_h1T")
    t_hT = [T("gh1T%d" % i) for i in range(NT)]
    ph = Phase(fw)
    self.norm_phase(ph, layer, 1, 0, hT, t_hT, tiles)
    ph.close()
    ph = Phase(fw)
    WP = 2312
    LAT0, CTX0 = 2, 2054
    wu = [ph.sb([128, KD, 128], BF16, "gwu") for _ in range(3)]
    t_wu = [T("gwu%d" % i) for i in range(3)]
    cw = ph.sb([128, 64, 5], F32, "gcw")
    t_cw = T("gcw")
    fw.dma("sp", cw[:], I["gdn_convT"][j], writes=[t_cw])
    epsc = ph.sb([128, 1], F32, "epsc")
    fw.op("pool", lambda e: e.memset(epsc[:], EPS), writes=[t_cw])
    U = [ph.sb([128, WP], F32, "gU") for _ in range(2)]
    t_U = [T("gU0"), T("gU1")]
    for b in range(2):
        fw.op("pool", lambda e, b=b: e.memset(U[b][:], 0.0), writes=[t_U[b]])
    cg = ph.sb([128, WP], F32, "gcg")
    cp = ph.sb([128, WP], F32, "gcp")
    sg2 = [ph.sb([128, WP], F32, "gsg") for _ in range(2)]
    sq_ = ph.sb([128, WP], F32, "gsq")
    sq2 = [sq_, sq_]
    t_sq_ = T("sq")
    t_sg2, t_sq2 = [T("sg0"), T("sg1")], [t_sq_, t_sq_]
    pend_post = []
    sd = ph.sb([128, WP], F32, "gsd")
    xn = [ph.sb([128, WP], F32, "gxn") for _ in range(2)]
    t_cg, t_cp, t_sd = T("cg"), T("cp"), T("sd")
    t_xn = [T("xn0"), T("xn1")]
    PA = ph.ps([128, 2048], F32, "gPA")
    PB = ph.ps([128, 2048], F32, "gPB")
    t_PA, t_PB = T("gPA"), T("gPB")
    C0, C1 = 2, 2310

    def load_w(jc):
        fw.dma("pool", wu[jc % 3][:], I["gdn_w_in"][j, :, jc * 128:(jc + 1) * 128].rearrange("(k p) n -> p k n", p=128), writes=[t_wu[jc % 3]])

    load_w(0)
    load_w(1)
    for jc in range(64):
        if jc + 2 < 64:
            load_w(jc + 2)
        w, tw = wu[jc % 3], t_wu[jc % 3]
        ub = jc % 2
        Ub, tU = U[ub], t_U[ub]
        for blk in range(4):
            for k in range(KD):
                fw.op("pe", lambda e, blk=blk, k=k: e.matmul(PA[:, blk * 512:(blk + 1) * 512], lhsT=w[:, k, :], rhs=hT[:, k, blk * 512:(blk + 1) * 512],
                                                            start=(k == 0), stop=(k == KD - 1)),
                      reads=[tw] + t_hT[blk * 4:(blk + 1) * 4], writes=[t_PA], sig=(k == KD - 1))
        fw.op("act", lambda e: e.activation(out=Ub[:, LAT0:LAT0 + 2048], in_=PA[:], func=AF.Copy), reads=[t_PA], writes=[tU])
        while pend_post:
            pend_post.pop(0)()
        for k in range(KD):
            fw.op("pe", lambda e, k=k: e.matmul(PA[:, 0:256], lhsT=w[:, k, :], rhs=hT[:, k, 2048:2304], start=(k == 0), stop=(k == KD - 1)),
                  reads=[tw] + t_hT[16:18], writes=[t_PA], sig=(k == KD - 1))
        fw.op("act", lambda e: e.activation(out=Ub[:, CTX0:CTX0 + 256], in_=PA[:, 0:256], func=AF.Copy), reads=[t_PA], writes=[tU])
        fw.op("dve", lambda e: e.tensor_scalar(out=cg[:, C0:C1], in0=Ub[:, C0 - 2:C1 - 2], scalar1=cw[:, jc, 0:1], scalar2=None, op0=ALU.mult),
              reads=[tU, t_cw], writes=[t_cg])
        for tap in (1, 2):
            fw.op("dve", lambda e, tap=tap: e.scalar_tensor_tensor(out=cg[:, C0:C1], in0=Ub[:, C0 - 2 + tap:C1 - 2 + tap], scalar=cw[:, jc, tap:tap + 1],
                                                                in1=cg[:, C0:C1], op0=ALU.mult, op1=ALU.add),
                  reads=[tU, t_cw, t_cg], writes=[t_cg])
        fw.op("pool", lambda e: e.tensor_scalar(out=cp[:, C0:C1], in0=Ub[:, C0 + 1:C1 + 1], scalar1=cw[:, jc, 3:4], scalar2=0.0, op0=ALU.mult, op1=ALU.add),
              reads=[tU, t_cw], writes=[t_cp])
        fw.op("dve", lambda e: e.scalar_tensor_tensor(out=cg[:, C0:C1], in0=Ub[:, C0 + 2:C1 + 2], scalar=cw[:, jc, 4:5], in1=cg[:, C0:C1], op0=ALU.mult, op1=ALU.add),
              reads=[tU, t_cw, t_cg], writes=[t_cg])
        fw.op("dve", lambda e: e.tensor_tensor(out=cg[:, C0:C1], in0=cg[:, C0:C1], in1=cp[:, C0:C1], op=ALU.add), reads=[t_cg, t_cp], writes=[t_cg])
        xo, txo = xn[jc % 2], t_xn[jc % 2]
        if jc < 32:
            sg, sq, t_sg, t_sq = sg2[jc % 2], sq2[jc % 2], t_sg2[jc % 2], t_sq2[jc % 2]
            fw.op("act", lambda e: e.activation(out=sg[:, C0:C1], in_=cg[:, C0:C1], func=AF.Silu), reads=[t_cg], writes=[t_sg])
            fw.op("dve", lambda e: e.tensor_tensor(out=sq[:, C0:C1], in0=sg[:, C0:C1], in1=sg[:, C0:C1], op=ALU.mult), reads=[t_sg], writes=[t_sq])

            def post(jc=jc, sg=sg, sq=sq, t_sg=t_sg, t_sq=t_sq, xo=xo, txo=txo):
                for blk in range(5):
                    a0 = C0 + blk * 512
                    a1 = min(C1, a0 + 512)
                    dstp = PB[:, blk * 512:blk * 512 + (a1 - a0)] if blk < 4 else PB[:, 0:a1 - a0]
                    fw.op("pe", lambda e: e.matmul(dstp, lhsT=self.ones_f[:], rhs=sq[:, a0:a1], start=True, stop=True),
                          reads=[t_sq, self.t_c], writes=[t_PB])
                    if blk == 3:
                        fw.op("act", lambda e: e.activation(out=sd[:, C0:C0 + 2048], in_=PB[:], func=AF.Ln, scale=1.0, bias=epsc[:, 0:1]), reads=[t_PB, t_cw], writes=[t_sd])
                    if blk == 4:
                        fw.op("act", lambda e: e.activation(out=sd[:, a0:a1], in_=PB[:, 0:a1 - a0], func=AF.Ln, scale=1.0, bias=epsc[:, 0:1]), reads=[t_PB, t_cw], writes=[t_sd])
                fw.op("act", lambda e: e.activation(out=sd[:, C0:C1], in_=sd[:, C0:C1], func=AF.Exp, scale=-0.5), reads=[t_sd], writes=[t_sd])
                qs = (128 ** -0.5) if jc < 16 else 1.0
                fw.op("dve", lambda e: e.scalar_tensor_tensor(out=xo[:, C0:C1], in0=sg[:, C0:C1], scalar=qs, in1=sd[:, C0:C1], op0=ALU.mult, op1=ALU.mult),
                      reads=[t_sg, t_sd], writes=[txo])
                dst, tl, h = (self.gq_s, self.t_gq, jc) if jc < 16 else (self.gk_s, self.t_gk, jc - 16)
                fw.dma("sp", dst[0:32, :, h, :].rearrange("c p t -> p c t"), xo[:, LAT0:LAT0 + 2048].rearrange("p (c t) -> p c t", t=64), reads=[txo], writes=[tl[h]])
                fw.dma("sp", dst[32:36, :, h, :].rearrange("c p t -> p c t"), xo[:, CTX0:CTX0 + 256].rearrange("p (c t) -> p c t", t=64), reads=[txo], writes=[tl[h]])
            pend_post.append(post)
        else:
            fw.op("act", lambda e: e.activation(out=xo[:, C0:C1], in_=cg[:, C0:C1], func=AF.Silu), reads=[t_cg], writes=[txo])
            dst, tl, h = self.gv_s, self.t_gv, jc - 32
            fw.dma("sp", dst[0:32, :, h, :].rearrange("c p t -> p c t"), xo[:, LAT0:LAT0 + 2048].rearrange("p (c t) -> p c t", t=64), reads=[txo], writes=[tl[h]])
            fw.dma("sp", dst[32:36, :, h, :].rearrange("c p t -> p c t"), xo[:, CTX0:CTX0 + 256].rearrange("p (c t) -> p c t", t=64), reads=[txo], writes=[tl[h]])
    while pend_post:
        pend_post.pop(0)()
    ph.close()
    ph = Phase(fw)
    wz = [ph.sb([128, KD, 512], BF16, "wz") for _ in range(2)]
    t_wz = [T("wz0"), T("wz1")]
    stZ = [ph.sb([128, NT, 512], BF16, "stZ") for _ in range(2)]
    t_stZ = [T("stZ0"), T("stZ1")]
    pz = [ph.ps([128, 512], F32, "pz") for _ in range(2)]
    t_pz = [T("pz0"), T("pz1")]

    def load_wz(n):
        fw.dma("pool", wz[n % 2][:], I["gdn_w_in"][j, :, 8192 + n * 512:8192 + (n + 1) * 512].rearrange("(k p) n -> p k n", p=128), writes=[t_wz[n % 2]])

    load_wz(0)
    it = 0
    for n in range(8):
        if n + 1 < 8:
            load_wz(n + 1)
        w_, tw = wz[n % 2], t_wz[n % 2]
        for tt in tiles:
            b = it % 2
            it += 1
            for k in range(KD):
                fw.op("pe", lambda e, k=k, tt=tt: e.matmul(pz[b][:], lhsT=hT[:, k, tt * 128:(tt + 1) * 128], rhs=w_[:, k, :], start=(k == 0), stop=(k == KD - 1)),
                      reads=[tw, t_hT[tt]], writes=[t_pz[b]], sig=(k == KD - 1))
            fw.op("act", lambda e, tt=tt: e.activation(out=stZ[n % 2][:, tt, :], in_=pz[b][:], func=AF.Silu), reads=[t_pz[b]], writes=[t_stZ[n % 2]])
        fw.dma("sp", self.z_s[:, n * 512:(n + 1) * 512].rearrange("(t p) d -> p t d", p=128), stZ[n % 2][:], reads=[t_stZ[n % 2]], writes=[self.t_z[n]])
    wab = ph.sb([128, KD, 128], BF16, "wab")
    t_wab = T("wab")
    fw.dma("pool", wab[:], I["gdn_w_in"][j, :, 12288:12416].rearrange("(k p) n -> p k n", p=128), writes=[t_wab])
    dtb = ph.sb([64, 64], F32, "dtb")
    nega = ph.sb([64, 64], F32, "nega")
    t_dtb = T("dtb")
    fw.dma("sp", dtb[:], I["gdn_dt_bias"][j].partition_broadcast(64), writes=[t_dtb])
    fw.dma("sp", nega[:], I["gdn_a_log"][j].partition_broadcast(64), writes=[t_dtb])
    fw.op("act", lambda e: e.activation(out=nega[:], in_=nega[:], func=AF.Exp), reads=[t_dtb], writes=[t_dtb])
    fw.op("dve", lambda e: e.tensor_scalar(out=nega[:], in0=nega[:], scalar1=-1.0, scalar2=None, op0=ALU.mult), reads=[t_dtb], writes=[t_dtb])
    sp_t = ph.sb([64, 4, 2, 32], F32, "sp_t")
    t_sp = T("sp_t")
    for c4 in range(9):
        b = c4 % 2
        for cc in range(4):
            c = c4 * 4 + cc
            for k in range(KD):
                fw.op("pe", lambda e, k=k, cc=cc, c=c: e.matmul(pz[b][0:64, cc * 128:(cc + 1) * 128], lhsT=hT[:, k, c * 64:(c + 1) * 64], rhs=wab[:, k, :],
                                                               start=(k == 0), stop=(k == KD - 1)),
                      reads=[t_wab] + t_hT[c // 2:c // 2 + 1], writes=[t_pz[b]], sig=(k == KD - 1))
        pv = pz[b][0:64, :].rearrange("p (c d b h) -> p c d b h", c=4, d=2, b=2)
        fw.op("act", lambda e: e.activation(out=betaS[:, c4 * 4:(c4 + 1) * 4, :, :], in_=pv[:, :, :, 0, :], func=AF.Sigmoid), reads=[t_pz[b]], writes=[t_bg])
        dtv = dtb[:].rearrange("p (d h) -> p d h", d=2).unsqueeze(1).to_broadcast([64, 4, 2, 32])
        ngv = nega[:].rearrange("p (d h) -> p d h", d=2).unsqueeze(1).to_broadcast([64, 4, 2, 32])
        fw.op("dve", lambda e: e.tensor_tensor(out=sp_t[:], in0=pv[:, :, :, 1, :], in1=dtv, op=ALU.add), reads=[t_pz[b], t_dtb], writes=[t_sp])
        fw.op("act", lambda e: e.activation(out=sp_t[:], in_=sp_t[:], func=AF.Exp), reads=[t_sp], writes=[t_sp])
        fw.op("act", lambda e: e.activation(out=sp_t[:], in_=sp_t[:], func=AF.Ln, bias=1.0, scale=1.0), reads=[t_sp], writes=[t_sp])
        fw.op("dve", lambda e: e.tensor_tensor(out=gS[:, c4 * 4:(c4 + 1) * 4, :, :], in0=sp_t[:], in1=ngv, op=ALU.mult), reads=[t_sp, t_dtb], writes=[t_bg])
    ph.close()
    ph0.close()
    ph = Phase(fw)
    NH = 16
    NK = 8
    msk = ph.sb([64, 6, 64], F32, "msk")
    t_msk = T("msk")
    specs = [(1.0, [[1, 64]], -1, ALU.is_ge, 0.0), (1.0, [[-1, 64]], 1, ALU.is_ge, 0.0),
             (0.0, [[-1, 64]], 1, ALU.is_ge, -30000.0), (0.0, [[1, 64]], -1, ALU.is_ge, -30000.0),
             (1.0, [[-1, 64]], 1, ALU.is_gt, 0.0), (1.0, [[1, 64]], -1, ALU.is_gt, 0.0)]
    for mi, (init, pat, cm, cmp_, fill) in enumerate(specs):
        fw.op("pool", lambda e: e.memset(msk[:, mi, :], init), reads=[t_msk], writes=[t_msk])
        fw.op("pool", lambda e: e.affine_select(out=msk[:, mi, :], in_=msk[:, mi, :], compare_op=cmp_, fill=fill, base=0, pattern=pat, channel_multiplier=cm),
              reads=[t_msk], writes=[t_msk])
    U1, L1, NL, NU, SL, SU = [msk[:, i, :] for i in range(6)]
    identf64 = self.ident_f[0:64, 0:64]
    PS = ph.ps([128, 4096], F32, "gPS")
    t_bank = [T("bank%d" % i) for i in range(8)]
    ring = [0]

    def psget(nb):
        s = ring[0]
        if s + nb > 5:
            s = 0
        ring[0] = (s + nb) % 5
        return s, t_bank[s:s + nb]

    def psgetB(role):
        return 5 + role, t_bank[5 + role:6 + role]

    qc = [ph.sb([128, NK, 64], F32, "qc") for _ in range(2)]
    kc = [ph.sb([128, NK, 64], F32, "kc") for _ in range(2)]
    vc = [ph.sb([128, NH, 64], F32, "vc") for _ in range(2)]
    t_in = [T("in0"), T("in1")]
    S = ph.sb([128, NH, 128], F32, "S")
    t_S = [T("S%d" % i) for i in range(4)]
    ktok = ph.sb([64, NK, 128], F32, "ktok")
    vtok = ph.sb([64, NH, 128], F32, "vtok")
    vb = ph.sb([64, NH, 128], BF16, "vb")
    ost = ph.sb([64, NH, 128], F32, "ost")
    t_ost = T("ost")
    kbg = ph.sb([64, NH, 128], BF16, "kbg")
    kdec = ph.sb([64, NH, 128], BF16, "kdec")
    kcb = ph.sb([128, NK, 64], BF16, "kcb")
    qcb = ph.sb([128, NK, 64], BF16, "qcb")
    t_kcb = T("kcb")
    Sbf = ph.sb([128, NH, 128], BF16, "Sbf")
    t_ktok, t_vtok, t_vb, t_kbg, t_kdec = T("ktok"), T("vtok"), T("vb"), T("kbg"), T("kdec")
    sm = ph.sb([128, 8, NH], F32, "sm")
    t_sm = T("sm")
    tmpA = ph.sb([64, NH, 64], F32, "tmpA")
    betaR = ph.sb([64, NH, 64], F32, "betaR")
    decay = ph.sb([64, NH, 64], F32, "decay")
    decayT = ph.sb([64, NH, 64], F32, "decayT")
    qgT = ph.sb([128, NH, 64], BF16, "qgT")
    qgF = ph.sb([128, NH, 64], F32, "qgF")
    t_qgF = T("qgF")
    qkT = ph.sb([64, NH, 64], BF16, "qkT")
    wT = ph.sb([128, NH, 64], BF16, "wT")
    t_tmpA, t_betaR, t_decay, t_decayT, t_qgT, t_qkT, t_wT = T("tmpA"), T("betaR"), T("decay"), T("decayT"), T("qgT"), T("qkT"), T("wT")
    Pb = [ph.sb([64, NH, 64], F32, "Pb")]
    Qb = [ph.sb([64, NH, 64], F32, "Qb")]
    Ps = ph.sb([128, 2, NK, 64], F32, "Ps")
    Qs = ph.sb([128, 2, NK, 64], F32, "Qs")
    Ys = ph.sb([128, 2, NK, 64], F32, "Ys")
    t_Ps, t_Qs, t_Ys = [T("Ps0"), T("Ps1")], [T("Qs0"), T("Qs1")], [T("Ys0"), T("Ys1")]
    I2 = ph.sb([128, 64], F32, "I2")
    t_I2 = T("I2")
    fw.op("act", lambda e: e.activation(out=I2[0:64, :], in_=self.ident_f[0:64, 0:64], func=AF.Copy), reads=[self.t_c], writes=[t_I2])
    fw.op("act", lambda e: e.activation(out=I2[64:128, :], in_=self.ident_f[64:128, 64:128], func=AF.Copy), reads=[self.t_c], writes=[t_I2])
    Ybf = ph.sb([64, NH, 64], BF16, "Ybf")
    t_Ybf = T("Ybf")
    t_Pb, t_Qb = [T("P0")], [T("Q0")]
    vnew = [ph.sb([64, 4, 128], BF16, "vnew") for _ in range(2)]
    t_vnew = [T("vnew0"), T("vnew1")]

    def bc_pairs(ap3):
        return ap3.unsqueeze(2).to_broadcast([ap3.shape[0], NK, 2, ap3.shape[2]])

    def v4(ap3):
        return ap3.rearrange("p (a b) f -> p a b f", b=2)

    def bc_h(ap2, n=NH):
        return ap2.unsqueeze(1).to_broadcast([ap2.shape[0], n, ap2.shape[1]])

    def bc_f(ap2, f):
        return ap2.unsqueeze(2).to_broadcast([ap2.shape[0], ap2.shape[1], f])

    iters = []
    for d in range(2):
        order = [32, 33, 34, 35] + list(range(32)) if d == 0 else [35, 34, 33, 32] + list(range(31, -1, -1))
        for hh in range(2):
            for si, c in enumerate(order):
                iters.append((d, hh, c, si == 0))

    def load_in(ii):
        d, hh, c, _ = iters[ii]
        b = ii % 2
        fw.dma("sp", qc[b][:], self.gq_s[c, :, hh * NK:(hh + 1) * NK, :], reads=self.t_gq[hh * NK:(hh + 1) * NK], writes=[t_in[b]])
        fw.dma("sp", kc[b][:], self.gk_s[c, :, hh * NK:(hh + 1) * NK, :], reads=self.t_gk[hh * NK:(hh + 1) * NK], writes=[t_in[b]])
        fw.dma("sp", vc[b][:], self.gv_s[c, :, hh * NH:(hh + 1) * NH, :], reads=self.t_gv[hh * NH:(hh + 1) * NH], writes=[t_in[b]])

    qgT2, t_qgT2 = [qgT, ph.sb([128, NH, 64], BF16, "qgTb")], [t_qgT, T("qgTb")]
    qkT2, t_qkT2 = [qkT, ph.sb([64, NH, 64], BF16, "qkTb")], [t_qkT, T("qkTb")]
    kdec2, t_kdec2 = [kdec, ph.sb([64, NH, 128], BF16, "kdecb")], [t_kdec, T("kdecb")]
    wT2, t_wT2 = [wT, ph.sb([128, NH, 64], BF16, "wTb")], [t_wT, T("wTb")]
    sm2, t_sm2 = [sm, ph.sb([128, 8, NH], F32, "smb")], [t_sm, T("smb")]
    u2, t_u2 = [ph.sb([64, NH, 128], F32, "u") for _ in range(2)], [T("u0"), T("u1")]

    def stageA(ii):
        d, hh, c, first = iters[ii]
        if ii + 1 < len(iters):
            load_in(ii + 1)
        b = ii % 2
        qgT, t_qgT, qkT, t_qkT, kdec, t_kdec, wT, t_wT, sm, t_sm = qgT2[b], t_qgT2[b], qkT2[b], t_qkT2[b], kdec2[b], t_kdec2[b], wT2[b], t_wT2[b], sm2[b], t_sm2[b]
        u, t_u = u2[b], t_u2[b]
        q_c, k_c, v_c, tin = qc[b], kc[b], vc[b], t_in[b]
        Mc = U1 if d == 0 else L1
        NEG, NEGT = (NL, NU) if d == 0 else (NU, NL)
        ST, STT_ = (SL, SU) if d == 0 else (SU, SL)
        g_c = gS[:, c, d, hh * NH:(hh + 1) * NH]
        b_c = betaS[:, c, d, hh * NH:(hh + 1) * NH]
        fw.op("pool", lambda e: e.tensor_copy(out=kcb[:], in_=k_c[:]), reads=[tin], writes=[t_kcb])
        fw.op("pool", lambda e: e.tensor_copy(out=qcb[:], in_=q_c[:]), reads=[tin], writes=[t_kcb])
        yield
        s, tb = psget(1)
        fw.op("pe", lambda e: e.matmul(PS[0:64, s * 512:s * 512 + NH], lhsT=Mc, rhs=g_c, start=True, stop=True), reads=[t_msk, t_bg], writes=tb)
        fw.op("pe", lambda e: e.matmul(PS[:, s * 512 + 32:s * 512 + 32 + NH], lhsT=self.ones_f[0:64, :], rhs=g_c, start=True, stop=True), reads=[self.t_c, t_bg], writes=tb)
        gam, glast, eg, gle, dgl, be = [sm[:, i, :] for i in range(6)]
        fw.op("dve", lambda e: e.tensor_copy(out=gam[0:64], in_=PS[0:64, s * 512:s * 512 + NH]), reads=tb, writes=[t_sm])
        fw.op("dve", lambda e: e.tensor_copy(out=glast, in_=PS[:, s * 512 + 32:s * 512 + 32 + NH]), reads=tb, writes=[t_sm])
        fw.op("act", lambda e: e.activation(out=eg[0:64], in_=gam[0:64], func=AF.Exp), reads=[t_sm], writes=[t_sm])
        fw.op("act", lambda e: e.activation(out=gle, in_=glast, func=AF.Exp), reads=[t_sm], writes=[t_sm])
        fw.op("dve", lambda e: e.tensor_tensor(out=dgl[0:64], in0=glast[0:64], in1=gam[0:64], op=ALU.subtract), reads=[t_sm], writes=[t_sm])
        fw.op("act", lambda e: e.activation(out=dgl[0:64], in_=dgl[0:64], func=AF.Exp), reads=[t_sm], writes=[t_sm])
        fw.op("dve", lambda e: e.tensor_tensor(out=be[0:64], in0=b_c, in1=eg[0:64], op=ALU.mult), reads=[t_sm, t_bg], writes=[t_sm])
        yield
        fw.op("dve", lambda e: e.tensor_tensor(out=tmpA[:], in0=bc_f(b_c, 64), in1=bc_h(identf64), op=ALU.mult), reads=[t_bg, self.t_c], writes=[t_tmpA])
        s, tb = psget(2)
        for x in range(2):
            fw.op("pe", lambda e, x=x: e.matmul(PS[0:64, (s + x) * 512:(s + x + 1) * 512], lhsT=self.ones_f[0:64, 0:64], rhs=tmpA[:, x * 8:(x + 1) * 8, :].rearrange("p h f -> p (h f)"),
                                              start=True, stop=True), reads=[t_tmpA, self.t_c], writes=tb)
        fw.op("act", lambda e: e.activation(out=betaR[:].rearrange("p h f -> p (h f)"), in_=PS[0:64, s * 512:(s + 2) * 512], func=AF.Copy), reads=tb, writes=[t_betaR])
        yield
        fw.op("dve", lambda e: e.tensor_tensor(out=tmpA[:], in0=bc_f(g_c, 64), in1=bc_h(Mc), op=ALU.mult), reads=[t_bg, t_msk], writes=[t_tmpA])
        s, tb = psget(2)
        for x in range(2):
            fw.op("pe", lambda e, x=x: e.matmul(PS[:, (s + x) * 512:(s + x + 1) * 512], lhsT=self.ones_f[0:64, :], rhs=tmpA[:, x * 8:(x + 1) * 8, :].rearrange("p h f -> p (h f)"),
                                              start=True, stop=True), reads=[t_tmpA, self.t_c], writes=tb)
        R64 = PS[0:64, s * 512:(s + 2) * 512].rearrange("p (h f) -> p h f", h=NH)
        R128 = PS[:, s * 512:(s + 2) * 512].rearrange("p (h f) -> p h f", h=NH)
        fw.op("dve", lambda e: e.tensor_tensor(out=decay[:], in0=bc_f(gam[0:64], 64), in1=R64, op=ALU.subtract), reads=tb + [t_sm], writes=[t_decay])
        fw.op("pool", lambda e: e.tensor_tensor(out=decay[:], in0=decay[:], in1=bc_h(NEG), op=ALU.add), reads=[t_decay, t_msk], writes=[t_decay])
        fw.op("act", lambda e: e.activation(out=decay[:], in_=decay[:], func=AF.Exp), reads=[t_decay], writes=[t_decay])
        fw.op("dve", lambda e: e.tensor_tensor(out=decayT[:], in0=R64, in1=bc_f(gam[0:64], 64), op=ALU.subtract), reads=tb + [t_sm], writes=[t_decayT])
        fw.op("pool", lambda e: e.tensor_tensor(out=decayT[:], in0=decayT[:], in1=bc_h(NEGT), op=ALU.add), reads=[t_decayT, t_msk], writes=[t_decayT])
        fw.op("act", lambda e: e.activation(out=decayT[:], in_=decayT[:], func=AF.Exp), reads=[t_decayT], writes=[t_decayT])
        fw.op("act", lambda e: e.activation(out=qgF[:], in_=R128, func=AF.Exp), reads=tb, writes=[t_qgF])
        fw.op("dve", lambda e: e.tensor_tensor(out=v4(qgT[:]), in0=v4(qgF[:]), in1=bc_pairs(q_c[:]), op=ALU.mult), reads=[t_qgF, tin], writes=[t_qgT])
        yield
        s, tb = psget(1)
        for h in range(NK):
            fw.op("pe", lambda e, h=h: e.matmul(PS[0:64, s * 512 + h * 64:s * 512 + (h + 1) * 64], lhsT=kcb[:, h, :], rhs=kcb[:, h, :], start=True, stop=True),
                  reads=[t_kcb], writes=tb)
        kkv = bc_pairs(PS[0:64, s * 512:(s + 1) * 512].rearrange("p (h f) -> p h f", h=NK))
        P, Q = Pb[0], Qb[0]
        tP, tQ = t_Pb[0], t_Qb[0]
        fw.op("dve", lambda e: e.tensor_tensor(out=v4(P[:]), in0=v4(decay[:]), in1=kkv, op=ALU.mult), reads=tb + [t_decay], writes=[tP])
        fw.op("dve", lambda e: e.scalar_tensor_tensor(out=P[:], in0=P[:], scalar=-1.0, in1=bc_f(b_c, 64), op0=ALU.mult, op1=ALU.mult), reads=[tP, t_bg], writes=[tP])
        for par in range(2):
            fw.op("dve", lambda e, par=par: e.tensor_tensor(out=Ps[par * 64:(par + 1) * 64, 0, :, :], in0=v4(P[:])[:, :, par, :], in1=bc_h(ST, NK), op=ALU.mult),
                  reads=[tP, t_msk], writes=[t_Ps[0]])
        fw.op("dve", lambda e: e.tensor_tensor(out=v4(Q[:]), in0=v4(decayT[:]), in1=kkv, op=ALU.mult), reads=tb + [t_decayT], writes=[tQ])
        fw.op("dve", lambda e: e.scalar_tensor_tensor(out=Q[:], in0=Q[:], scalar=-1.0, in1=betaR[:], op0=ALU.mult, op1=ALU.mult), reads=[tQ, t_betaR], writes=[tQ])
        for par in range(2):
            fw.op("dve", lambda e, par=par: e.tensor_tensor(out=Qs[par * 64:(par + 1) * 64, 0, :, :], in0=v4(Q[:])[:, :, par, :], in1=bc_h(STT_, NK), op=ALU.mult),
                  reads=[tQ, t_msk], writes=[t_Qs[0]])
        fw.op("pool", lambda e: e.tensor_tensor(out=Ys[:, 0, :, :], in0=Qs[:, 0, :, :], in1=I2[:].unsqueeze(1).to_broadcast([128, NK, 64]), op=ALU.add),
              reads=[t_Qs[0], t_I2], writes=[t_Ys[0]])
        yield
        s, tb = psget(1)
        for h in range(NK):
            fw.op("pe", lambda e, h=h: e.matmul(PS[0:64, s * 512 + h * 64:s * 512 + (h + 1) * 64], lhsT=kcb[:, h, :], rhs=qcb[:, h, :], start=True, stop=True),
                  reads=[t_kcb], writes=tb)
        kqv = bc_pairs(PS[0:64, s * 512:(s + 1) * 512].rearrange("p (h f) -> p h f", h=NK))
        fw.op("dve", lambda e: e.tensor_tensor(out=v4(qkT[:]), in0=v4(decayT[:]), in1=kqv, op=ALU.mult), reads=tb + [t_decayT], writes=[t_qkT])
        yield
        s, tb = psget(2)
        for h in range(NK):
            fw.op("pe", lambda e, h=h: e.transpose(out=PS[0:64, s * 512 + h * 128:s * 512 + (h + 1) * 128], in_=k_c[:, h, :], identity=self.ident_f[:]),
                  reads=[tin, self.t_c], writes=tb, sig=(h == NK - 1))
        fw.op("act", lambda e: e.activation(out=ktok[:].rearrange("p h f -> p (h f)"), in_=PS[0:64, s * 512:(s + 2) * 512], func=AF.Copy), reads=tb, writes=[t_ktok])
        s, tb = psget(4)
        for h in range(NH):
            fw.op("pe", lambda e, h=h: e.transpose(out=PS[0:64, s * 512 + h * 128:s * 512 + (h + 1) * 128], in_=v_c[:, h, :], identity=self.ident_f[:]),
                  reads=[tin, self.t_c], writes=tb, sig=(h == NH - 1))
        fw.op("act", lambda e: e.activation(out=vtok[:].rearrange("p h f -> p (h f)"), in_=PS[0:64, s * 512:(s + 4) * 512], func=AF.Copy), reads=tb, writes=[t_vtok])
        yield
        fw.op("dve", lambda e: e.tensor_tensor(out=vb[:], in0=vtok[:], in1=bc_f(b_c, 128), op=ALU.mult), reads=[t_vtok, t_bg], writes=[t_vb])
        fw.op("dve", lambda e: e.tensor_tensor(out=v4(kbg[:]), in0=bc_pairs(ktok[:]), in1=v4(bc_f(be[0:64], 128)), op=ALU.mult), reads=[t_ktok, t_sm], writes=[t_kbg])
        fw.op("pool", lambda e: e.tensor_tensor(out=v4(kdec[:]), in0=bc_pairs(ktok[:]), in1=v4(bc_f(dgl[0:64], 128)), op=ALU.mult), reads=[t_ktok, t_sm], writes=[t_kdec])
        yield
        cur = 0

        def quad_mm(sb_, lhs, rhs, tl, tr, tb):
            for m in range(NK):
                for par in range(2):
                    pr = slice(par * 64, (par + 1) * 64)
                    fw.op("pe", lambda e, m=m, pr=pr: e.matmul(PS[pr, sb_ * 512 + m * 64:sb_ * 512 + (m + 1) * 64], lhsT=lhs[pr, m, :], rhs=rhs[pr, m, :], start=True, stop=True),
                          reads=[tl, tr], writes=tb, sig=(m == NK - 1 and par == 1))

        for lev in range(1, 6):
            nxt = 1 - cur
            P_, Q_, Y_ = Ps[:, cur], Qs[:, cur], Ys[:, cur]
            s, tb = psget(1)
            quad_mm(s, Q_, P_, t_Qs[cur], t_Ps[cur], tb)
            fw.op("act", lambda e: e.activation(out=Ps[:, nxt].rearrange("p h f -> p (h f)"), in_=PS[:, s * 512:(s + 1) * 512], func=AF.Copy), reads=tb, writes=[t_Ps[nxt]])
            yield
            if lev < 5:
                s, tb = psget(1)
                quad_mm(s, P_, Q_, t_Ps[cur], t_Qs[cur], tb)
                fw.op("pool" if False else "act", lambda e: e.activation(out=Qs[:, nxt].rearrange("p h f -> p (h f)"), in_=PS[:, s * 512:(s + 1) * 512], func=AF.Copy),
                      reads=tb, writes=[t_Qs[nxt]])
                yield
            s, tb = psget(1)
            quad_mm(s, Ps[:, nxt], Y_, t_Ps[nxt], t_Ys[cur], tb)
            fw.op("dve", lambda e: e.tensor_tensor(out=Ys[:, nxt].rearrange("p h f -> p (h f)"), in0=Y_.rearrange("p h f -> p (h f)"), in1=PS[:, s * 512:(s + 1) * 512], op=ALU.add),
                  reads=tb + [t_Ys[cur]], writes=[t_Ys[nxt]])
            cur = nxt
            yield
        fw.op("act", lambda e: e.activation(out=v4(Ybf[:])[:, :, 0, :], in_=Ys[0:64, cur], func=AF.Copy), reads=[t_Ys[cur]], writes=[t_Ybf])
        fw.op("dve", lambda e: e.tensor_copy(out=v4(Ybf[:])[:, :, 1, :], in_=Ys[64:128, cur]), reads=[t_Ys[cur]], writes=[t_Ybf])
        Y, tY = Ybf, t_Ybf
        yield
        s, tb = psget(4)
        for h in range(NH):
            fw.op("pe", lambda e, h=h: e.matmul(PS[0:64, s * 512 + h * 128:s * 512 + (h + 1) * 128], lhsT=Y[:, h, :], rhs=vb[:, h, :], start=True, stop=True),
                  reads=[tY, t_vb], writes=tb)
        fw.op("act", lambda e: e.activation(out=u[:].rearrange("p h f -> p (h f)"), in_=PS[0:64, s * 512:(s + 4) * 512], func=AF.Copy), reads=tb, writes=[t_u])
        s, tb = psget(2)
        for h in range(NH):
            fw.op("pe", lambda e, h=h: e.matmul(PS[:, s * 512 + h * 64:s * 512 + (h + 1) * 64], lhsT=kbg[:, h, :], rhs=Y[:, h, :], start=True, stop=True),
                  reads=[tY, t_kbg], writes=tb)
        fw.op("act", lambda e: e.activation(out=wT[:].rearrange("p h f -> p (h f)"), in_=PS[:, s * 512:(s + 2) * 512], func=AF.Copy), reads=tb, writes=[t_wT])

    def stageB(ii):
        d, hh, c, first = iters[ii]
        b = ii % 2
        qgT, t_qgT, qkT, t_qkT, kdec, t_kdec, wT, t_wT, sm, t_sm = qgT2[b], t_qgT2[b], qkT2[b], t_qkT2[b], kdec2[b], t_kdec2[b], wT2[b], t_wT2[b], sm2[b], t_sm2[b]
        u, t_u = u2[b], t_u2[b]
        gam, glast, eg, gle, dgl, be = [sm[:, i, :] for i in range(6)]
        if first:
            fw.op("pool", lambda e: e.memset(S[:], 0.0), writes=t_S)
            fw.op("pool", lambda e: e.memset(Sbf[:], 0.0), writes=t_S)
        for hg in range(4):
            vn, tvn = vnew[hg % 2], t_vnew[hg % 2]
            tS = [t_S[hg]]
            s, tb = psgetB(0)
            for hl in range(4):
                h = hg * 4 + hl
                fw.op("pe", lambda e, h=h, hl=hl: e.matmul(PS[0:64, s * 512 + hl * 128:s * 512 + (hl + 1) * 128], lhsT=wT[:, h, :], rhs=Sbf[:, h, :], start=True, stop=True),
                      reads=[t_wT] + tS, writes=tb)
            yield
            fw.op("dve", lambda e: e.tensor_tensor(out=vn[:].rearrange("p h f -> p (h f)"), in0=u[:, hg * 4:(hg + 1) * 4, :].rearrange("p h f -> p (h f)"),
                                                   in1=PS[0:64, s * 512:(s + 1) * 512], op=ALU.subtract), reads=tb + [t_u], writes=[tvn])
            s, tb = psgetB(1)
            for hl in range(4):
                h = hg * 4 + hl
                fw.op("pe", lambda e, h=h, hl=hl: e.matmul(PS[0:64, s * 512 + hl * 128:s * 512 + (hl + 1) * 128], lhsT=qgT[:, h, :], rhs=Sbf[:, h, :], start=True, stop=False),
                      reads=[t_qgT] + tS, writes=tb)
                fw.op("pe", lambda e, h=h, hl=hl: e.matmul(PS[0:64, s * 512 + hl * 128:s * 512 + (hl + 1) * 128], lhsT=qkT[:, h, :], rhs=vn[:, hl, :], start=False, stop=True),
                      reads=[t_qkT, tvn], writes=tb)
            yield
            fw.op("act", lambda e: e.activation(out=ost[:, hg * 4:(hg + 1) * 4, :].rearrange("p h f -> p (h f)"), in_=PS[0:64, s * 512:(s + 1) * 512], func=AF.Copy),
                  reads=tb, writes=[t_ost])
            s, tb = psgetB(2)
            for hl in range(4):
                h = hg * 4 + hl
                fw.op("pe", lambda e, h=h, hl=hl: e.matmul(PS[:, s * 512 + hl * 128:s * 512 + (hl + 1) * 128], lhsT=kdec[:, h, :], rhs=vn[:, hl, :], start=True, stop=True),
                      reads=[t_kdec, tvn], writes=tb)
            yield
            for hl in range(4):
                h = hg * 4 + hl
                fw.op("dve", lambda e, h=h, hl=hl: e.scalar_tensor_tensor(out=S[:, h, :], in0=S[:, h, :], scalar=gle[:, h:h + 1], in1=PS[:, s * 512 + hl * 128:s * 512 + (hl + 1) * 128],
                                                                        op0=ALU.mult, op1=ALU.add), reads=tb + tS + [t_sm], writes=tS)
            yield
            fw.op("act", lambda e: e.activation(out=Sbf[:, hg * 4:(hg + 1) * 4, :], in_=S[:, hg * 4:(hg + 1) * 4, :], func=AF.Copy), reads=tS, writes=tS)
        fw.dma("sp", self.o_s[d, c * 64:(c + 1) * 64, hh * 2048:(hh + 1) * 2048], ost[:].rearrange("p h f -> p (h f)"), reads=[t_ost], writes=[self.t_os[d][c]])

    def drive(ga, gb, ra=2):
        alive_a, alive_b = ga is not None, gb is not None
        while alive_a or alive_b:
            if alive_a:
                for _ in range(ra):
                    try:
                        next(ga)
                    except StopIteration:
                        alive_a = False
                        break
            if alive_b:
                try:
                    next(gb)
                except StopIteration:
                    alive_b = False

    load_in(0)
    drive(stageA(0), None)
    for ii in range(len(iters)):
        drive(stageA(ii + 1) if ii + 1 < len(iters) else None, stageB(ii))
    ph.close()
    phA.close()
    ph = Phase(fw)
    otiles = list(range(16)) if last else tiles
    gate = [ph.sb([128, D], F32, "ggate") for _ in range(2)]
    t_gate = T("ggate")
    for r in range(2):
        self.load_bc(gate[r][:], t_gate, layer, r, 2)
    ng = ph.sb([128, 128], F32, "ng")
    t_ng = T("ng")
    fw.dma("sp", ng[:], I["gdn_norm_gain"][j].partition_broadcast(128), writes=[t_ng])
    wo = ph.sb([128, 32, D], BF16, "gwo")
    t_wo = [T("gwo%d" % i) for i in range(4)]
    for n in range(4):
        for kh in range(2):
            fw.dma("pool", wo[:, kh * 16:(kh + 1) * 16, n * 512:(n + 1) * 512],
                   I["gdn_w_o"][j, kh * 2048:(kh + 1) * 2048, n * 512:(n + 1) * 512].rearrange("(k p) n -> p k n", p=128), writes=[t_wo[n]])
    HC = D
    of2 = [ph.sb([128, HC], F32, "of") for _ in range(2)]
    ob2 = [ph.sb([128, HC], F32, "ob") for _ in range(2)]
    zt2 = [ph.sb([128, HC], BF16, "zt") for _ in range(2)]
    t_of2, t_ob2, t_zt2 = [T("of0"), T("of1")], [T("ob0"), T("ob1")], [T("zt0"), T("zt1")]
    ssn2 = [ph.sb([128, 3, 16], F32, "ssn") for _ in range(2)]
    t_ssn2 = [T("ssn0"), T("ssn1")]
    yT2 = [ph.sb([128, 16, 128], BF16, "yT") for _ in range(2)]
    t_yT2 = [T("yT0"), T("yT1")]
    xsl = [ph.sb([128, 512], F32, "gxsl") for _ in range(4)]
    t_xsl = [T("gxsl%d" % i) for i in range(4)]
    tmp = ph.sb([128, 512], F32, "gtmp")
    t_tmp = T("gtmp")
    pt = [ph.ps([128, 1024], BF16, "gpt") for _ in range(2)]
    t_pt = [T("gpt0"), T("gpt1")]
    po = [ph.ps([128, 512], F32, "gpo") for _ in range(4)]
    t_po = [T("gpo%d" % i) for i in range(4)]
    steps = [(ti, tt, hf) for ti, tt in enumerate(otiles) for hf in range(2)]

    def chain(si):
        ti, tt, hf = steps[si]
        sb = si % 2
        of, ob, zt, ssn = of2[sb], ob2[sb], zt2[sb], ssn2[sb]
        t_of, t_ob, t_zt, t_ssn = t_of2[sb], t_ob2[sb], t_zt2[sb], t_ssn2[sb]
        rows = slice(tt * 128, (tt + 1) * 128)
        cols = slice(hf * HC, (hf + 1) * HC)
        fw.dma("sp", of[:], self.o_s[0, rows, cols], reads=self.t_os[0][2 * tt:2 * tt + 2], writes=[t_of])
        fw.dma("sp", ob[:], self.o_s[1, rows, cols], reads=self.t_os[1][2 * tt:2 * tt + 2], writes=[t_ob])
        fw.dma("sp", zt[:], self.z_s[rows, cols], reads=self.t_z, writes=[t_zt])
        fw.op("pool", lambda e: e.tensor_tensor(out=of[:], in0=of[:], in1=ob[:], op=ALU.add), reads=[t_of, t_ob], writes=[t_of])
        fw.op("act", lambda e: e.activation(out=ob[:], in_=of[:], func=AF.Square), reads=[t_of], writes=[t_ob])
        fw.op("dve", lambda e: e.tensor_reduce(out=ssn[:, 0, :], in_=ob[:].rearrange("p (h f) -> p h f", h=16), axis=AX.X, op=ALU.add), reads=[t_ob], writes=[t_ssn])
        fw.op("act", lambda e: e.activation(out=ssn[:, 1, :], in_=ssn[:, 0, :], func=AF.Sqrt, scale=1.0 / 128, bias=EPS), reads=[t_ssn], writes=[t_ssn])
        fw.op("dve", lambda e: e.reciprocal(out=ssn[:, 2, :], in_=ssn[:, 1, :]), reads=[t_ssn], writes=[t_ssn])
        o3 = of[:].rearrange("p (h f) -> p h f", h=16)
        fw.op("dve", lambda e: e.tensor_tensor(out=o3, in0=o3, in1=ssn[:, 2, :].unsqueeze(2).to_broadcast([128, 16, 128]), op=ALU.mult), reads=[t_of, t_ssn], writes=[t_of])
        fw.op("pool", lambda e: e.tensor_tensor(out=o3, in0=o3, in1=ng[:].unsqueeze(1).to_broadcast([128, 16, 128]), op=ALU.mult), reads=[t_of, t_ng], writes=[t_of])
        fw.op("dve", lambda e: e.tensor_tensor(out=zt[:], in0=of[:], in1=zt[:], op=ALU.mult), reads=[t_of, t_zt], writes=[t_zt])

    def trans(si):
        sb = si % 2
        yb, t_yb, yT, t_yT = zt2[sb], t_zt2[sb], yT2[sb], t_yT2[sb]
        for q4 in range(2):
            p_ = pt[q4]
            for kk in range(8):
                k = q4 * 8 + kk
                fw.op("pe", lambda e, k=k, kk=kk: e.transpose(out=p_[:, kk * 128:(kk + 1) * 128], in_=yb[:, k * 128:(k + 1) * 128], identity=self.ident_bf[:]),
                      reads=[t_yb, self.t_c], writes=[t_pt[q4]], sig=(kk == 7))
            fw.op("act", lambda e: e.activation(out=yT[:, q4 * 8:(q4 + 1) * 8, :], in_=p_[:].rearrange("p (k t) -> p k t", k=8), func=AF.Copy), reads=[t_pt[q4]], writes=[t_yT])

    def mm(si):
        ti, tt, hf = steps[si]
        sb = si % 2
        yT, t_yT = yT2[sb], t_yT2[sb]
        r = 0 if tt < 16 else 1
        if hf == 1:
            for n in range(4):
                fw.dma("sp", xsl[n][:], self.xs[tt * 128:(tt + 1) * 128, n * 512:(n + 1) * 512], reads=[self.t_xs[tt]], writes=[t_xsl[n]])
        for n in range(4):
            p_, tp = po[n], t_po[n]
            for kk in range(16):
                k = hf * 16 + kk
                fw.op("pe", lambda e, k=k, kk=kk, n=n: e.matmul(p_[:], lhsT=yT[:, kk, :], rhs=wo[:, k, n * 512:(n + 1) * 512], start=(k == 0), stop=(k == 31)),
                      reads=[t_yT, t_wo[n]], writes=[tp], sig=(kk == 15))
            if hf == 1:
                fw.op("dve", lambda e, n=n: e.tensor_tensor(out=tmp[:], in0=p_[:], in1=gate[r][:, n * 512:(n + 1) * 512], op=ALU.mult), reads=[tp, t_gate], writes=[t_tmp])
                fw.op("dve", lambda e, n=n: e.tensor_tensor(out=xsl[n][:], in0=xsl[n][:], in1=tmp[:], op=ALU.add), reads=[t_xsl[n], t_tmp], writes=[t_xsl[n]])
                fw.dma("sp", self.xs[tt * 128:(tt + 1) * 128, n * 512:(n + 1) * 512], xsl[n][:], reads=[t_xsl[n]], writes=[self.t_xs[tt]])

    chain(0)
    for si in range(len(steps)):
        trans(si)
        if si + 1 < len(steps):
            chain(si + 1)
        mm(si)
    ph.close()


Prog.gdn_layer = _gdn_layer
```

```python
import math
from contextlib import ExitStack

import numpy as np
import concourse.bass as bass
import concourse.mybir as mybir
from concourse.bass_utils import run_bass_kernel_spmd

F32 = mybir.dt.float32
BF16 = mybir.dt.bfloat16
AF = mybir.ActivationFunctionType
ALU = mybir.AluOpType
AX = mybir.AxisListType

D = 2048
T_LAT = 2048
T_CTX = 256
T_ALL = T_LAT + T_CTX
NT = T_ALL // 128
KD = D // 128
DEPTH = 4
EPS = 1e-6
D_FF = 5504
NFF = D_FF // 128
GDN_IN_W = 12416
NQ = 12


class T:
    __slots__ = ("name", "w", "r")

    def __init__(self, name=""):
        self.name = name
        self.w = None
        self.r = []


class Op:
    __slots__ = ("eng", "sig")

    def __init__(self, eng):
        self.eng = eng
        self.sig = None


class FW:
    CE = ("pe", "act", "dve", "pool")

    def __init__(self, nc, stack):
        self.nc = nc
        self.h = {"pe": nc.tensor, "act": nc.scalar, "dve": nc.vector, "pool": nc.gpsimd, "sp": nc.sync}
        self.sem = {e: stack.enter_context(nc.semaphore("s_" + e)) for e in self.CE}
        self.cnt = {e: 0 for e in self.CE}
        self.pending = {e: [] for e in self.CE}
        self.waited = {e: {} for e in self.h}
        self.dsem, self.dcnt, self.drr, self.dlast = {}, {}, {}, {}
        for q in ("sp", "pool"):
            self.dsem[q] = [stack.enter_context(nc.semaphore("d_%s%d" % (q, i))) for i in range(NQ)]
            self.dcnt[q] = [0] * NQ
            self.drr[q] = 0
            self.dlast[q] = [None] * NQ
        self.n_inst = 0
        self.n_wait = 0
        self.uid = 0

    def name(self, p):
        self.uid += 1
        return "%s_%d" % (p, self.uid)

    def _deps(self, reads, writes):
        deps = []
        for t in reads:
            if t.w is not None:
                deps.append((t.w, True))
        for t in writes:
            if t.w is not None:
                deps.append((t.w, False))
            for r in t.r:
                deps.append((r, False))
        return deps

    def _update(self, op, reads, writes):
        for t in writes:
            t.w = op
            t.r = []
        for t in reads:
            if t.w is not op:
                t.r.append(op)

    def _wait(self, eng, sem, val):
        w = self.waited[eng]
        k = id(sem)
        if w.get(k, 0) >= val:
            return
        self.h[eng].wait_ge(sem, val)
        self.n_wait += 1
        w[k] = val

    def _emit_waits(self, eng, deps, extra=()):
        need = {}
        for (d, raw) in deps:
            if d.eng == eng and eng in self.CE:
                if not raw or eng == "pe":
                    continue
            if d.sig is None:
                raise RuntimeError("dependency on unsignaled op")
            sem, val = d.sig
            k = id(sem)
            if k not in need or need[k][1] < val:
                need[k] = (sem, val)
        for d in extra:
            sem, val = d.sig
            k = id(sem)
            if k not in need or need[k][1] < val:
                need[k] = (sem, val)
        for k, (sem, val) in need.items():
            self._wait(eng, sem, val)

    def op(self, eng, fn, reads=(), writes=(), sig=True):
        deps = self._deps(reads, writes)
        self._emit_waits(eng, deps)
        inst = fn(self.h[eng])
        o = Op(eng)
        self.n_inst += 1
        if sig:
            self.cnt[eng] += 1
            inst.then_inc(self.sem[eng], 1)
            o.sig = (self.sem[eng], self.cnt[eng])
            for p in self.pending[eng]:
                p.sig = o.sig
            self.pending[eng] = []
        else:
            self.pending[eng].append(o)
        self._update(o, reads, writes)
        return o

    def dma(self, q, out, in_, reads=(), writes=(), **kw):
        deps = self._deps(reads, writes)
        i = self.drr[q]
        self.drr[q] = (i + 1) % NQ
        extra = [self.dlast[q][i]] if self.dlast[q][i] is not None else []
        self._emit_waits(q, deps, extra)
        self.dcnt[q][i] += 16
        self.h[q].dma_start(out=out, in_=in_, **kw).then_inc(self.dsem[q][i], 16)
        o = Op("dma_" + q)
        o.sig = (self.dsem[q][i], self.dcnt[q][i])
        self.dlast[q][i] = o
        self.n_inst += 1
        self._update(o, reads, writes)
        return o

    def barrier(self):
        for e in self.CE:
            assert not self.pending[e], "unsignaled ops pending at barrier"
        for eng in self.h:
            for e in self.CE:
                if e != eng and self.cnt[e] > 0:
                    self._wait(eng, self.sem[e], self.cnt[e])
            if eng in self.CE and eng != "pe" and self.cnt[eng] > 0:
                self._wait(eng, self.sem[eng], self.cnt[eng])
            for q in self.dsem:
                for i in range(NQ):
                    if self.dcnt[q][i] > 0:
                        self._wait(eng, self.dsem[q][i], self.dcnt[q][i])


class Phase:
    def __init__(self, fw):
        self.fw = fw
        self.st = ExitStack()

    def sb(self, shape, dtype, name="sb"):
        t = self.st.enter_context(self.fw.nc.sbuf_tensor(self.fw.name(name), list(shape), dtype))
        return t

    def ps(self, shape, dtype, name="ps"):
        t = self.st.enter_context(self.fw.nc.psum_tensor(self.fw.name(name), list(shape), dtype))
        return t

    def close(self):
        self.fw.barrier()
        self.st.close()


class Prog:
    def __init__(self, nc, fw, n_layers, dbg):
        self.nc, self.fw, self.n_layers, self.dbg = nc, fw, n_layers, dbg
        dt = nc.dram_tensor
        I = {}
        I["x"] = dt("x", [T_LAT, D], F32, kind="ExternalInput").ap()
        I["ctx"] = dt("ctx", [T_CTX, D], F32, kind="ExternalInput").ap()
        I["cT"] = dt("cT", [128, KD, 2], F32, kind="ExternalInput").ap()
        I["w_mod"] = dt("w_mod", [DEPTH, D, 6 * D], F32, kind="ExternalInput").ap()
        I["b_mod"] = dt("b_mod", [DEPTH, 6 * D], F32, kind="ExternalInput").ap()
        I["norm_mix"] = dt("norm_mix", [DEPTH, D], F32, kind="ExternalInput").ap()
        I["norm_ffn"] = dt("norm_ffn", [DEPTH, D], F32, kind="ExternalInput").ap()
        I["da_w_qkv"] = dt("da_w_qkv", [2, D, 3 * D], F32, kind="ExternalInput").ap()
        I["da_lambda"] = dt("da_lambda", [2, 4, 128], F32, kind="ExternalInput").ap()
        I["da_gainT"] = dt("da_gainT", [2, 128, 2], F32, kind="ExternalInput").ap()
        I["da_w_o"] = dt("da_w_o", [2, D, D], F32, kind="ExternalInput").ap()
        I["gdn_w_in"] = dt("gdn_w_in", [2, D, GDN_IN_W], F32, kind="ExternalInput").ap()
        I["gdn_convT"] = dt("gdn_convT", [2, 128, 64, 5], F32, kind="ExternalInput").ap()
        I["gdn_a_log"] = dt("gdn_a_log", [2, 64], F32, kind="ExternalInput").ap()
        I["gdn_dt_bias"] = dt("gdn_dt_bias", [2, 64], F32, kind="ExternalInput").ap()
        I["gdn_norm_gain"] = dt("gdn_norm_gain", [2, 128], F32, kind="ExternalInput").ap()
        I["gdn_w_o"] = dt("gdn_w_o", [2, 2 * D, D], F32, kind="ExternalInput").ap()
        I["ffn_w_up"] = dt("ffn_w_up", [DEPTH, D, 2 * D_FF], F32, kind="ExternalInput").ap()
        I["ffn_convT"] = dt("ffn_convT", [DEPTH, 128, 2 * NFF, 3], F32, kind="ExternalInput").ap()
        I["ffn_w_down"] = dt("ffn_w_down", [DEPTH, D_FF, D], F32, kind="ExternalInput").ap()
        I["final_norm"] = dt("final_norm", [D], F32, kind="ExternalInput").ap()
        I["rope"] = dt("rope", [128, 16, 2, 64], F32, kind="ExternalInput").ap()
        self.I = I
        self.out = dt("out", [T_LAT, D], F32, kind="ExternalOutput").ap()
        self.xs = dt("xs", [T_ALL, D], F32, kind="Internal").ap()
        self.mods = dt("mods", [DEPTH, 2, 6 * D], F32, kind="Internal").ap()
        self.qT_s = dt("qT_s", [16, 128, T_ALL], BF16, kind="Internal").ap()
        self.kT_s = dt("kT_s", [16, 128, T_ALL], BF16, kind="Internal").ap()
        self.v_s = dt("v_s", [T_ALL, D], BF16, kind="Internal").ap()
        self.g_s = dt("g_s", [NFF, 128, T_ALL], BF16, kind="Internal").ap()
        self.t_xs = [T("xs%d" % i) for i in range(NT)]
        self.t_mods = [T("mods%d" % i) for i in range(DEPTH)]
        self.t_qT = [T("qT%d" % i) for i in range(16)]
        self.t_kT = [T("kT%d" % i) for i in range(16)]
        self.t_v = [T("v%d" % i) for i in range(4)]
        self.t_g = [T("g%d" % i) for i in range(NFF)]
        self.t_out = T("out")
        self.t_none = T("const")

    def consts(self, ph):
        fw = self.fw
        self.ident_bf = ph.sb([128, 128], BF16, "identb")
        self.ident_f = ph.sb([128, 128], F32, "identf")
        self.ones_bf = ph.sb([128, 128], BF16, "onesb")
        self.ones_f = ph.sb([128, 128], F32, "onesf")
        tc_ = T("consts")
        self.t_c = tc_

        fw.op("pool", lambda e: e.memset(self.ident_f[:], 0.0), writes=[tc_])
        fw.op("pool", lambda e: e.affine_select(out=self.ident_f[:], in_=self.ident_f[:], compare_op=ALU.not_equal, fill=1.0, base=0,
                                                pattern=[[-1, 128]], channel_multiplier=1), reads=[tc_], writes=[tc_])
        fw.op("pool", lambda e: e.memset(self.ones_f[:], 1.0), writes=[tc_])
        fw.op("pool", lambda e: e.memset(self.ones_bf[:], 1.0), writes=[tc_])
        fw.op("dve", lambda e: e.tensor_copy(out=self.ident_bf[:], in_=self.ident_f[:]), reads=[tc_], writes=[tc_])

    def init_xs(self, ph):
        fw = self.fw
        buf = [ph.sb([128, D], F32, "cp") for _ in range(3)]
        tb = [T("cp%d" % i) for i in range(3)]
        for tt in range(NT):
            src = self.I["x"][tt * 128:(tt + 1) * 128, :] if tt < 16 else self.I["ctx"][(tt - 16) * 128:(tt - 15) * 128, :]
            b = tt % 3
            fw.dma("sp", buf[b][:], src, writes=[tb[b]])
            fw.dma("sp", self.xs[tt * 128:(tt + 1) * 128, :], buf[b][:], reads=[tb[b]], writes=[self.t_xs[tt]])

    def mods_phase(self, ph):
        fw, I = self.fw, self.I
        cT = ph.sb([128, KD, 2], F32, "cT")
        sc = ph.sb([128, KD, 2], F32, "scT")
        t_c, t_sc = T("cT"), T("scT")
        fw.dma("sp", cT[:], I["cT"], writes=[t_c])
        fw.op("act", lambda e: e.activation(out=sc[:], in_=cT[:], func=AF.Silu), reads=[t_c], writes=[t_sc])
        wb = [ph.sb([128, KD, 512], F32, "wmod") for _ in range(2)]
        t_wb = [T("wmod0"), T("wmod1")]
        m_sb = ph.sb([2, 6 * D], F32, "m_sb")
        b2 = ph.sb([2, 6 * D], F32, "b2")
        nm = ph.sb([2, 2, D], F32, "nm")
        t_m, t_b2, t_nm = T("m_sb"), T("b2"), T("nm")
        pp = [ph.ps([2, 512], F32, "modps") for _ in range(2)]
        t_pp = [T("modps0"), T("modps1")]
        cnt = 0
        for i in range(self.n_layers):
            fw.dma("sp", b2[:], I["b_mod"][i, :].partition_broadcast(2), reads=[], writes=[t_b2])
            fw.dma("sp", nm[:, 0, :], I["norm_mix"][i, :].partition_broadcast(2), writes=[t_nm])
            fw.dma("sp", nm[:, 1, :], I["norm_ffn"][i, :].partition_broadcast(2), writes=[t_nm])
            for n in range(24):
                b = cnt % 2
                cnt += 1
                w = wb[b]
                fw.dma("sp", w[:], I["w_mod"][i, :, n * 512:(n + 1) * 512].rearrange("(k p) n -> p k n", p=128),
                       writes=[t_wb[b]])
                p_ = pp[b]
                for k in range(KD):
                    fw.op("pe", lambda e, k=k: e.matmul(p_[:], lhsT=sc[:, k, :], rhs=w[:, k, :], start=(k == 0), stop=(k == KD - 1)),
                          reads=[t_sc, t_wb[b]], writes=[t_pp[b]], sig=(k == KD - 1))
                fw.op("dve", lambda e: e.tensor_tensor(out=m_sb[:, n * 512:(n + 1) * 512], in0=p_[:], in1=b2[:, n * 512:(n + 1) * 512], op=ALU.add),
                      reads=[t_pp[b], t_b2], writes=[t_m])
            for s, r in ((1, 0), (4, 1)):
                fw.op("dve", lambda e, s=s, r=r: e.scalar_tensor_tensor(out=m_sb[:, s * D:(s + 1) * D], in0=m_sb[:, s * D:(s + 1) * D], scalar=1.0,
                                                                   in1=nm[:, r, :], op0=ALU.add, op1=ALU.mult),
                      reads=[t_m, t_nm], writes=[t_m])
            fw.dma("sp", self.mods[i], m_sb[:], reads=[t_m], writes=[self.t_mods[i]])

    def load_bc(self, dst, t_dst, layer, row, slot):
        self.fw.dma("sp", dst, self.mods[layer, row, slot * D:(slot + 1) * D].partition_broadcast(128),
                    reads=[self.t_mods[layer]], writes=[t_dst])

    def norm_phase(self, ph, layer, gslot, sslot, hT, t_hT, tiles):
        fw = self.fw
        G = [ph.sb([128, D], F32, "G") for _ in range(2)]
        S = [ph.sb([128, D], F32, "S") for _ in range(2)]
        t_G = T("G")
        for r in range(2):
            self.load_bc(G[r][:], t_G, layer, r, gslot)
            self.load_bc(S[r][:], t_G, layer, r, sslot)
        xb = [ph.sb([128, D], F32, "xb") for _ in range(2)]
        t_xb = [T("xb0"), T("xb1")]
        junk = ph.sb([128, D], BF16, "junk")
        t_junk = T("junk")
        ss2 = [ph.sb([128, 4], F32, "ss") for _ in range(2)]
        t_ss2 = [T("ss0"), T("ss1")]
        hf2 = [ph.sb([128, D], F32, "hf") for _ in range(2)]
        hb2 = [ph.sb([128, D], BF16, "hb") for _ in range(2)]
        t_hf2, t_hb2 = [T("hf0"), T("hf1")], [T("hb0"), T("hb1")]
        pt = [ph.ps([128, 1024], BF16, "pt") for _ in range(2)]
        t_pt = [T("pt0"), T("pt1")]

        def load(i):
            tt = tiles[i]
            fw.dma("sp", xb[i % 2][:], self.xs[tt * 128:(tt + 1) * 128, :], reads=[self.t_xs[tt]], writes=[t_xb[i % 2]])

        def stats(i):
            x_, tx, ss, t_ss = xb[i % 2], t_xb[i % 2], ss2[i % 2], t_ss2[i % 2]
            fw.op("act", lambda e: e.activation(out=junk[:], in_=x_[:], func=AF.Square, accum_out=ss[:, 0:1]),
                  reads=[tx], writes=[t_junk, t_ss])
            fw.op("act", lambda e: e.activation(out=ss[:, 1:2], in_=ss[:, 0:1], func=AF.Sqrt, scale=1.0 / D, bias=EPS),
                  reads=[t_ss], writes=[t_ss])
            fw.op("dve", lambda e: e.reciprocal(out=ss[:, 2:3], in_=ss[:, 1:2]), reads=[t_ss], writes=[t_ss])

        def rest(i):
            tt = tiles[i]
            x_, tx, ss, t_ss = xb[i % 2], t_xb[i % 2], ss2[i % 2], t_ss2[i % 2]
            hf, hb, t_hf, t_hb = hf2[i % 2], hb2[i % 2], t_hf2[i % 2], t_hb2[i % 2]
            r = 0 if tt < 16 else 1
            fw.op("dve", lambda e: e.scalar_tensor_tensor(out=hf[:], in0=x_[:], scalar=ss[:, 2:3], in1=G[r][:], op0=ALU.mult, op1=ALU.mult),
                  reads=[tx, t_ss, t_G], writes=[t_hf])
            fw.op("dve", lambda e: e.tensor_tensor(out=hb[:], in0=hf[:], in1=S[r][:], op=ALU.add), reads=[t_hf, t_G], writes=[t_hb])
            for half in range(2):
                p_ = pt[half]
                for kk in range(8):
                    k = half * 8 + kk
                    fw.op("pe", lambda e, k=k, kk=kk: e.transpose(out=p_[:, kk * 128:(kk + 1) * 128], in_=hb[:, k * 128:(k + 1) * 128], identity=self.ident_bf[:]),
                          reads=[t_hb, self.t_c], writes=[t_pt[half]], sig=(kk == 7))
                dst = hT[:, half * 8:(half + 1) * 8, tt * 128:(tt + 1) * 128]
                src = p_[:].rearrange("p (k t) -> p k t", k=8)
                if half == 0:
                    fw.op("act", lambda e: e.activation(out=dst, in_=src, func=AF.Copy), reads=[t_pt[half]], writes=[t_hT[tt]])
                else:
                    fw.op("dve", lambda e: e.tensor_copy(out=dst, in_=src), reads=[t_pt[half]], writes=[t_hT[tt]])

        load(0)
        stats(0)
        for i in range(len(tiles)):
            if i + 1 < len(tiles):
                load(i + 1)
                stats(i + 1)
            rest(i)

    def ffn(self, layer, tiles):
        fw, I = self.fw, self.I
        has_ctx = len(tiles) == NT
        ph0 = Phase(fw)
        hT = ph0.sb([128, KD, T_ALL], BF16, "h2T")
        t_hT = [T("h2T%d" % i) for i in range(NT)]
        ph = Phase(fw)
        self.norm_phase(ph, layer, 4, 3, hT, t_hT, tiles)
        ph.close()
        ph = Phase(fw)
        WP = 2308
        wu = [ph.sb([128, KD, 256], BF16, "wu") for _ in range(3)]
        t_wu = [T("wu%d" % i) for i in range(3)]
        cw = ph.sb([128, 2 * NFF, 3], F32, "cw")
        t_cw = T("cw")
        fw.dma("sp", cw[:], I["ffn_convT"][layer], writes=[t_cw])
        U = [[ph.sb([128, WP], F32, "U") for _ in range(2)] for _ in range(2)]
        t_U = [[T("U") for _ in range(2)] for _ in range(2)]
        for gv in range(2):
            for b in range(2):
                fw.op("pool", lambda e, gv=gv, b=b: e.memset(U[gv][b][:], 0.0), writes=[t_U[gv][b]])
        cg = ph.sb([128, WP], F32, "cg")
        cv = ph.sb([128, WP], F32, "cv")
        ctmp = ph.sb([128, WP], F32, "ctmp")
        sg = ph.sb([128, WP], F32, "sg")
        t_cg, t_cv, t_ctmp, t_sg = T("cg"), T("cv"), T("ctmp"), T("sg")
        gst = [ph.sb([128, WP], BF16, "gst") for _ in range(2)]
        t_gst = [T("gst0"), T("gst1")]
        PA = [ph.ps([128, 2048], F32, "PA"), ph.ps([128, 2048], F32, "PB")]
        t_PA = [T("PA"), T("PB")]

        def load_w(j):
            b = j % 3
            for gv in range(2):
                c0 = gv * D_FF + j * 128
                fw.dma("pool", wu[b][:, :, gv * 128:(gv + 1) * 128],
                       I["ffn_w_up"][layer, :, c0:c0 + 128].rearrange("(k p) n -> p k n", p=128), writes=[t_wu[b]])

        load_w(0)
        load_w(1)
        for j in range(NFF):
            if j + 2 < NFF:
                load_w(j + 2)
            w = wu[j % 3]
            tw = t_wu[j % 3]
            ub = j % 2
            for gv in range(2):
                for blk in range(4):
                    for k in range(KD):
                        fw.op("pe", lambda e, gv=gv, blk=blk, k=k: e.matmul(PA[gv][:, blk * 512:(blk + 1) * 512], lhsT=w[:, k, gv * 128:(gv + 1) * 128],
                                                                          rhs=hT[:, k, blk * 512:(blk + 1) * 512], start=(k == 0), stop=(k == KD - 1)),
                              reads=[tw] + t_hT[blk * 4:(blk + 1) * 4], writes=[t_PA[gv]], sig=(k == KD - 1))
                eng = "act" if gv == 0 else "dve"
                if eng == "act":
                    fw.op("act", lambda e, gv=gv: e.activation(out=U[gv][ub][:, 1:2049], in_=PA[gv][:], func=AF.Copy), reads=[t_PA[gv]], writes=[t_U[gv][ub]])
                else:
                    fw.op("dve", lambda e, gv=gv: e.tensor_copy(out=U[gv][ub][:, 1:2049], in_=PA[gv][:]), reads=[t_PA[gv]], writes=[t_U[gv][ub]])
            if has_ctx:
                for gv in range(2):
                    for k in range(KD):
                        fw.op("pe", lambda e, gv=gv, k=k: e.matmul(PA[gv][:, 0:256], lhsT=w[:, k, gv * 128:(gv + 1) * 128], rhs=hT[:, k, 2048:2304],
                                                                  start=(k == 0), stop=(k == KD - 1)),
                              reads=[tw] + t_hT[16:18], writes=[t_PA[gv]], sig=(k == KD - 1))
                    fw.op("act", lambda e, gv=gv: e.activation(out=U[gv][ub][:, 2051:2307], in_=PA[gv][:, 0:256], func=AF.Copy), reads=[t_PA[gv]], writes=[t_U[gv][ub]])
            Ug, Uv = U[0][ub], U[1][ub]
            jg, jv = j, NFF + j
            fw.op("dve", lambda e: e.tensor_scalar(out=cg[:, 1:2307], in0=Ug[:, 1:2307], scalar1=cw[:, jg, 1:2], scalar2=None, op0=ALU.mult),
                  reads=[t_U[0][ub], t_cw], writes=[t_cg])
            fw.op("dve", lambda e: e.scalar_tensor_tensor(out=cg[:, 1:2307], in0=Ug[:, 0:2306], scalar=cw[:, jg, 0:1], in1=cg[:, 1:2307], op0=ALU.mult, op1=ALU.add),
                  reads=[t_U[0][ub], t_cw, t_cg], writes=[t_cg])
            fw.op("dve", lambda e: e.scalar_tensor_tensor(out=cg[:, 1:2307], in0=Ug[:, 2:2308], scalar=cw[:, jg, 2:3], in1=cg[:, 1:2307], op0=ALU.mult, op1=ALU.add),
                  reads=[t_U[0][ub], t_cw, t_cg], writes=[t_cg])
            fw.op("act", lambda e: e.activation(out=sg[:, 1:2307], in_=cg[:, 1:2307], func=AF.Silu), reads=[t_cg], writes=[t_sg])
            fw.op("pool", lambda e: e.tensor_scalar(out=cv[:, 1:2307], in0=Uv[:, 1:2307], scalar1=cw[:, jv, 1:2], scalar2=0.0, op0=ALU.mult, op1=ALU.add),
                  reads=[t_U[1][ub], t_cw], writes=[t_cv])
            fw.op("pool", lambda e: e.tensor_scalar(out=ctmp[:, 1:2307], in0=Uv[:, 0:2306], scalar1=cw[:, jv, 0:1], scalar2=0.0, op0=ALU.mult, op1=ALU.add),
                  reads=[t_U[1][ub], t_cw], writes=[t_ctmp])
            fw.op("pool", lambda e: e.tensor_tensor(out=cv[:, 1:2307], in0=cv[:, 1:2307], in1=ctmp[:, 1:2307], op=ALU.add),
                  reads=[t_cv, t_ctmp], writes=[t_cv])
            fw.op("pool", lambda e: e.tensor_scalar(out=ctmp[:, 1:2307], in0=Uv[:, 2:2308], scalar1=cw[:, jv, 2:3], scalar2=0.0, op0=ALU.mult, op1=ALU.add),
                  reads=[t_U[1][ub], t_cw], writes=[t_ctmp])
            fw.op("pool", lambda e: e.tensor_tensor(out=cv[:, 1:2307], in0=cv[:, 1:2307], in1=ctmp[:, 1:2307], op=ALU.add),
                  reads=[t_cv, t_ctmp], writes=[t_cv])
            gs_ = gst[j % 2]
            fw.op("dve", lambda e: e.tensor_tensor(out=gs_[:, 1:2307], in0=sg[:, 1:2307], in1=cv[:, 1:2307], op=ALU.mult),
                  reads=[t_sg, t_cv], writes=[t_gst[j % 2]])
            fw.dma("sp", self.g_s[j, :, 0:2048], gs_[:, 1:2049], reads=[t_gst[j % 2]], writes=[self.t_g[j]])
            if has_ctx:
                fw.dma("sp", self.g_s[j, :, 2048:2304], gs_[:, 2051:2307], reads=[t_gst[j % 2]], writes=[self.t_g[j]])
        ph.close()
        ph0.close()
        ph = Phase(fw)
        gate = [ph.sb([128, D], F32, "gate2") for _ in range(2)]
        t_gate = T("gate2")
        for r in range(2):
            self.load_bc(gate[r][:], t_gate, layer, r, 5)
        wd = [ph.sb([128, NFF, 512], BF16, "wd") for _ in range(2)]
        t_wd = [T("wd0"), T("wd1")]
        gb = [ph.sb([128, NFF, 512], BF16, "gb") for _ in range(2)]
        t_gb = [T("gb0"), T("gb1")]
        xt = [ph.sb([128, 512], F32, "xt") for _ in range(3)]
        t_xt = [T("xt%d" % i) for i in range(3)]
        tmp = ph.sb([128, 512], F32, "tmp")
        t_tmp = T("tmp")
        pd = [ph.ps([128, 512], F32, "pd") for _ in range(2)]
        t_pd = [T("pd0"), T("pd1")]
        blocks = [(0, 512), (512, 512), (1024, 512), (1536, 512)] + ([(2048, 256)] if has_ctx else [])

        def load_wd(n):
            b = n % 2
            for c in range(0, NFF, 11):
                c1 = min(NFF, c + 11)
                fw.dma("pool", wd[b][:, c:c1, :],
                       I["ffn_w_down"][layer, c * 128:c1 * 128, n * 512:(n + 1) * 512].rearrange("(j p) n -> p j n", p=128), writes=[t_wd[b]])

        seq = [(n, bi) for n in range(4) for bi in range(len(blocks))]

        def load_gb(si):
            n, bi = seq[si]
            t0, tl = blocks[bi]
            fw.dma("sp", gb[si % 2][:, :, 0:tl], self.g_s[:, :, t0:t0 + tl].rearrange("j p t -> p j t"), reads=self.t_g, writes=[t_gb[si % 2]])

        load_wd(0)
        load_gb(0)
        xi = 0
        for si, (n, bi) in enumerate(seq):
            if bi == 0 and n + 1 < 4:
                load_wd(n + 1)
            if si + 1 < len(seq):
                load_gb(si + 1)
            t0, tl = blocks[bi]
            g_ = gb[si % 2]
            w_ = wd[n % 2]
            for lt in range(tl // 128):
                tt = (t0 + lt * 128) // 128
                r = 0 if tt < 16 else 1
                xb_ = xt[xi % 3]
                txb = t_xt[xi % 3]
                xi += 1
                fw.dma("sp", xb_[:], self.xs[tt * 128:(tt + 1) * 128, n * 512:(n + 1) * 512], reads=[self.t_xs[tt]], writes=[txb])
                p_ = pd[lt % 2]
                tp = t_pd[lt % 2]
                for j in range(NFF):
                    fw.op("pe", lambda e, j=j, lt=lt: e.matmul(p_[:], lhsT=g_[:, j, lt * 128:(lt + 1) * 128], rhs=w_[:, j, :], start=(j == 0), stop=(j == NFF - 1)),
                          reads=[t_gb[si % 2], t_wd[n % 2]], writes=[tp], sig=(j == NFF - 1))
                fw.op("dve", lambda e: e.tensor_tensor(out=tmp[:], in0=p_[:], in1=gate[r][:, n * 512:(n + 1) * 512], op=ALU.mult),
                      reads=[tp, t_gate], writes=[t_tmp])
                fw.op("dve", lambda e: e.tensor_tensor(out=xb_[:], in0=xb_[:], in1=tmp[:], op=ALU.add), reads=[txb, t_tmp], writes=[txb])
                fw.dma("sp", self.xs[tt * 128:(tt + 1) * 128, n * 512:(n + 1) * 512], xb_[:], reads=[txb], writes=[self.t_xs[tt]])
        ph.close()

    def da_layer(self, layer):
        fw, I = self.fw, self.I
        j = layer // 2
        li = 0.8 - 0.6 * math.exp(-0.3 * layer)
        tiles = list(range(NT))
        ph0 = Phase(fw)
        hT = ph0.sb([128, KD, T_ALL], BF16, "h1T")
        t_hT = [T("h1T%d" % i) for i in range(NT)]
        ph = Phase(fw)
        self.norm_phase(ph, layer, 1, 0, hT, t_hT, tiles)
        ph.close()
        ph = Phase(fw)
        wq = [ph.sb([128, KD, 512], BF16, "wq") for _ in range(2)]
        t_wq = [T("wq0"), T("wq1")]
        rope = ph.sb([128, 16, 2, 64], F32, "rope")
        t_rope = T("rope")
        fw.dma("sp", rope[:], I["rope"], writes=[t_rope])
        pq = [ph.ps([128, 512], F32, "pq") for _ in range(2)]
        t_pq = [T("pq0"), T("pq1")]
        ptr = [ph.ps([128, 512], BF16, "ptr") for _ in range(2)]
        t_ptr = [T("ptr0"), T("ptr1")]
        qf = [ph.sb([128, 512], F32, "qf") for _ in range(2)]
        t_qf = [T("qf0"), T("qf1")]
        r1 = ph.sb([128, 256], F32, "r1")
        r2 = ph.sb([128, 256], F32, "r2")
        t_r1, t_r2 = T("r1"), T("r2")
        qr = [ph.sb([128, 512], BF16, "qr") for _ in range(2)]
        t_qr = [T("qr0"), T("qr1")]
        stT = [ph.sb([128, 4, T_ALL], BF16, "stT") for _ in range(2)]
        t_stT = [T("stT0"), T("stT1")]
        stV = [ph.sb([128, NT, 512], BF16, "stV") for _ in range(2)]
        t_stV = [T("stV0"), T("stV1")]

        def load_wq(n):
            fw.dma("pool", wq[n % 2][:], I["da_w_qkv"][j, :, n * 512:(n + 1) * 512].rearrange("(k p) n -> p k n", p=128), writes=[t_wq[n % 2]])

        load_wq(0)
        it = 0
        pend_tr = []
        for n in range(12):
            if n + 1 < 12:
                load_wq(n + 1)
            w_ = wq[n % 2]
            tw = t_wq[n % 2]
            for tt in tiles:
                b = it % 2
                it += 1
                p_ = pq[b]
                for k in range(KD):
                    fw.op("pe", lambda e, k=k, tt=tt: e.matmul(p_[:], lhsT=hT[:, k, tt * 128:(tt + 1) * 128], rhs=w_[:, k, :], start=(k == 0), stop=(k == KD - 1)),
                          reads=[tw, t_hT[tt]], writes=[t_pq[b]], sig=(k == KD - 1))
                while pend_tr:
                    pend_tr.pop(0)()
                if n >= 8:
                    sv = stV[n % 2]
                    fw.op("act", lambda e, tt=tt: e.activation(out=sv[:, tt, :], in_=p_[:], func=AF.Copy), reads=[t_pq[b]], writes=[t_stV[n % 2]])
                    continue
                q_ = qr[b]
                if tt < 16:
                    f_ = qf[b]
                    fw.op("act", lambda e: e.activation(out=f_[:], in_=p_[:], func=AF.Copy), reads=[t_pq[b]], writes=[t_qf[b]])
                    fv = f_[:].rearrange("p (h a two f) -> p h a two f", h=4, a=2, two=2)
                    qv = q_[:].rearrange("p (h a two f) -> p h a two f", h=4, a=2, two=2)
                    x1, x2 = fv[:, :, :, 0, :], fv[:, :, :, 1, :]
                    cosv = rope[:, tt, 0, :].rearrange("p (a f) -> p a f", a=2).unsqueeze(1).to_broadcast([128, 4, 2, 32])
                    sinv = rope[:, tt, 1, :].rearrange("p (a f) -> p a f", a=2).unsqueeze(1).to_broadcast([128, 4, 2, 32])
                    r1v = r1[:].rearrange("p (h a f) -> p h a f", h=4, a=2)
                    r2v = r2[:].rearrange("p (h a f) -> p h a f", h=4, a=2)
                    fw.op("dve", lambda e: e.tensor_tensor(out=r1v, in0=x1, in1=cosv, op=ALU.mult), reads=[t_qf[b], t_rope], writes=[t_r1])
                    fw.op("dve", lambda e: e.tensor_tensor(out=r2v, in0=x2, in1=sinv, op=ALU.mult), reads=[t_qf[b], t_rope], writes=[t_r2])
                    fw.op("dve", lambda e: e.tensor_tensor(out=qv[:, :, :, 0, :], in0=r1v, in1=r2v, op=ALU.subtract), reads=[t_r1, t_r2], writes=[t_qr[b]])
                    fw.op("dve", lambda e: e.tensor_tensor(out=r1v, in0=x2, in1=cosv, op=ALU.mult), reads=[t_qf[b], t_rope], writes=[t_r1])
                    fw.op("dve", lambda e: e.tensor_tensor(out=r2v, in0=x1, in1=sinv, op=ALU.mult), reads=[t_qf[b], t_rope], writes=[t_r2])
                    fw.op("dve", lambda e: e.tensor_tensor(out=qv[:, :, :, 1, :], in0=r1v, in1=r2v, op=ALU.add), reads=[t_r1, t_r2], writes=[t_qr[b]])
                else:
                    fw.op("act", lambda e: e.activation(out=q_[:], in_=p_[:], func=AF.Copy), reads=[t_pq[b]], writes=[t_qr[b]])
                def tr_step(b=b, q_=q_, tt=tt, n=n):
                    pt_ = ptr[b]
                    for c in range(4):
                        fw.op("pe", lambda e, c=c: e.transpose(out=pt_[:, c * 128:(c + 1) * 128], in_=q_[:, c * 128:(c + 1) * 128], identity=self.ident_bf[:]),
                              reads=[t_qr[b], self.t_c], writes=[t_ptr[b]], sig=(c == 3))
                    st_ = stT[n % 2]
                    fw.op("act", lambda e: e.activation(out=st_[:, :, tt * 128:(tt + 1) * 128], in_=pt_[:].rearrange("p (c t) -> p c t", c=4), func=AF.Copy),
                          reads=[t_ptr[b]], writes=[t_stT[n % 2]])
                pend_tr.append(tr_step)
            while pend_tr:
                pend_tr.pop(0)()
            if n < 8:
                dst = self.qT_s if n < 4 else self.kT_s
                tl = self.t_qT if n < 4 else self.t_kT
                m = n % 4
                fw.dma("sp", dst[m * 4:(m + 1) * 4].rearrange("c p t -> p c t"), stT[n % 2][:], reads=[t_stT[n % 2]], writes=tl[m * 4:(m + 1) * 4])
            else:
                m = n - 8
                fw.dma("sp", self.v_s[:, m * 512:(m + 1) * 512].rearrange("(t p) d -> p t d", p=128), stV[n % 2][:], reads=[t_stV[n % 2]], writes=[self.t_v[m]])
        ph.close()
        ph0.close()
        ph0 = Phase(fw)
        aoT = ph0.sb([128, 16, T_ALL], BF16, "aoT")
        t_ao = [T("ao%d" % i) for i in range(5)]
        ph = Phase(fw)
        lamb = ph.sb([128, 4, 128], F32, "lamb")
        lw = ph.sb([128, 2, 128], F32, "lw")
        lsc = ph.sb([128, 8], F32, "lsc")
        t_lam = T("lam")
        gcol = ph.sb([128, 2], F32, "gcol")
        fw.dma("sp", gcol[:], I["da_gainT"][j], writes=[t_lam])
        fw.dma("sp", lamb[:].rearrange("p a d -> p (a d)"), I["da_lambda"][j].rearrange("a d -> (a d)").partition_broadcast(128), writes=[t_lam])
        fw.op("dve", lambda e: e.tensor_tensor(out=lw[:, 0, :], in0=lamb[:, 0, :], in1=lamb[:, 1, :], op=ALU.mult), reads=[t_lam], writes=[t_lam])
        fw.op("dve", lambda e: e.tensor_tensor(out=lw[:, 1, :], in0=lamb[:, 2, :], in1=lamb[:, 3, :], op=ALU.mult), reads=[t_lam], writes=[t_lam])
        fw.op("dve", lambda e: e.tensor_reduce(out=lsc[:, 0:2], in_=lw[:], axis=AX.X, op=ALU.add), reads=[t_lam], writes=[t_lam])
        fw.op("act", lambda e: e.activation(out=lsc[:, 2:4], in_=lsc[:, 0:2], func=AF.Exp), reads=[t_lam], writes=[t_lam])
        fw.op("dve", lambda e: e.tensor_tensor(out=lsc[:, 4:5], in0=lsc[:, 3:4], in1=lsc[:, 2:3], op=ALU.subtract), reads=[t_lam], writes=[t_lam])
        fw.op("dve", lambda e: e.tensor_scalar(out=lsc[:, 5:6], in0=lsc[:, 4:5], scalar1=-li, scalar2=None, op0=ALU.add), reads=[t_lam], writes=[t_lam])
        neglam = lsc[:, 5:6]
        qh = [ph.sb([128, 2, T_ALL], BF16, "qh") for _ in range(2)]
        kh = [ph.sb([128, 2, T_ALL], BF16, "kh") for _ in range(2)]
        vh = [ph.sb([128, NT, 256], BF16, "vh") for _ in range(2)]
        t_qh, t_kh, t_vh = [T("qh0"), T("qh1")], [T("kh0"), T("kh1")], [T("vh0"), T("vh1")]
        NE = 4
        eb = [ph.sb([128, 512], BF16, "eb") for _ in range(NE)]
        t_eb = [T("eb%d" % i) for i in range(NE)]
        ps_s = [ph.ps([128, 512], F32, "ps_s") for _ in range(2)]
        t_ps_s = [T("ps_s0"), T("ps_s1")]
        ps_sum2 = [ph.ps([128, 512], F32, "ps_sum") for _ in range(2)]
        ps_o2 = [[ph.ps([128, 512], F32, "ps_o") for _ in range(2)] for _ in range(2)]
        t_ps_sum2, t_ps_o2 = [T("ps_sum0"), T("ps_sum1")], [[T("ps_o00"), T("ps_o01")], [T("ps_o10"), T("ps_o11")]]
        ps_ss, t_ps_ss = ps_s[0], t_ps_s[0]
        rc = ph.sb([128, 512], F32, "rc")
        t_rc = T("rc")
        oc = [[ph.sb([128, 512], F32, "oc") for _ in range(2)] for _ in range(2)]
        t_oc = [[T("oc") for _ in range(2)] for _ in range(2)]
        osum = [ph.sb([128, 512], F32, "osum") for _ in range(2)]
        t_osum = [T("osum0"), T("osum1")]
        sq = [ph.sb([128, 512], F32, "sq") for _ in range(2)]
        t_sq = [T("sq0"), T("sq1")]
        sd = ph.sb([128, 512], F32, "sd")
        rs = ph.sb([128, 512], F32, "rs")
        t_sd, t_rs = T("sd"), T("rs")
        qblocks = [(0, 512), (512, 512), (1024, 512), (1536, 512), (2048, 256)]
        scale = 128 ** -0.5

        def load_head(h):
            b = h % 2
            fw.dma("sp", qh[b][:], self.qT_s[2 * h:2 * h + 2].rearrange("c p t -> p c t"), reads=self.t_qT[2 * h:2 * h + 2], writes=[t_qh[b]])
            fw.dma("sp", kh[b][:], self.kT_s[2 * h:2 * h + 2].rearrange("c p t -> p c t"), reads=self.t_kT[2 * h:2 * h + 2], writes=[t_kh[b]])
            fw.dma("sp", vh[b][:], self.v_s[:, h * 256:(h + 1) * 256].rearrange("(t p) d -> p t d", p=128), reads=[self.t_v[h // 2]], writes=[t_vh[b]])

        load_head(0)
        ei = 0
        si = 0
        for h in range(8):
            if h + 1 < 8:
                load_head(h + 1)
            b = h % 2
            q_, k_, v_ = qh[b], kh[b], vh[b]
            for qi, (q0, Q) in enumerate(qblocks):
                keys = list(range(NT)) if q0 < T_LAT else [16, 17]
                for c in range(2):
                    ps_sum, t_ps_sum, ps_o, t_ps_o = ps_sum2[c], t_ps_sum2[c], ps_o2[c], t_ps_o2[c]

                    def av_step(ki, kc, e_, te):
                        first, last = ki == 0, ki == len(keys) - 1
                        fw.op("pe", lambda e: e.matmul(ps_sum[:, 0:Q], lhsT=self.ones_bf[:], rhs=e_[:, 0:Q], start=first, stop=last),
                              reads=[te, self.t_c], writes=[t_ps_sum], sig=last)
                        for half in range(2):
                            fw.op("pe", lambda e, half=half: e.matmul(ps_o[half][:, 0:Q], lhsT=v_[:, kc, half * 128:(half + 1) * 128], rhs=e_[:, 0:Q], start=first, stop=last),
                                  reads=[te, t_vh[b]], writes=[t_ps_o[half]], sig=last)
                    pend = None
                    for ki, kc in enumerate(keys):
                        p_ = ps_s[si % 2]
                        tp = t_ps_s[si % 2]
                        si += 1
                        fw.op("pe", lambda e, c=c, kc=kc: e.matmul(p_[:, 0:Q], lhsT=k_[:, c, kc * 128:(kc + 1) * 128], rhs=q_[:, c, q0:q0 + Q], start=True, stop=True),
                              reads=[t_kh[b], t_qh[b]], writes=[tp])
                        e_ = eb[ei % NE]
                        te = t_eb[ei % NE]
                        ei += 1
                        fw.op("act", lambda e: e.activation(out=e_[:, 0:Q], in_=p_[:, 0:Q], func=AF.Exp, scale=scale), reads=[tp], writes=[te])
                        if pend is not None:
                            av_step(*pend)
                        pend = (ki, kc, e_, te)
                    av_step(*pend)
                    fw.op("dve", lambda e: e.reciprocal(out=rc[:, 0:Q], in_=ps_sum[:, 0:Q]), reads=[t_ps_sum], writes=[t_rc])
                    for half in range(2):
                        if c == 0:
                            fw.op("dve", lambda e, half=half: e.tensor_tensor(out=oc[0][half][:, 0:Q], in0=ps_o[half][:, 0:Q], in1=rc[:, 0:Q], op=ALU.mult),
                                  reads=[t_ps_o[half], t_rc], writes=[t_oc[0][half]])
                        else:
                            fw.op("dve", lambda e, half=half: e.scalar_tensor_tensor(out=oc[1][half][:, 0:Q], in0=ps_o[half][:, 0:Q], scalar=neglam, in1=rc[:, 0:Q],
                                                                                 op0=ALU.mult, op1=ALU.mult),
                                  reads=[t_ps_o[half], t_rc, t_lam], writes=[t_oc[1][half]])
                for half in range(2):
                    fw.op("pool", lambda e, half=half: e.tensor_tensor(out=osum[half][:, 0:Q], in0=oc[0][half][:, 0:Q], in1=oc[1][half][:, 0:Q], op=ALU.add),
                          reads=[t_oc[0][half], t_oc[1][half]], writes=[t_osum[half]])
                    fw.op("act", lambda e, half=half: e.activation(out=sq[half][:, 0:Q], in_=osum[half][:, 0:Q], func=AF.Square), reads=[t_osum[half]], writes=[t_sq[half]])
                    fw.op("pe", lambda e, half=half: e.matmul(ps_ss[:, 0:Q], lhsT=self.ones_f[:], rhs=sq[half][:, 0:Q], start=(half == 0), stop=(half == 1)),
                          reads=[t_sq[half], self.t_c], writes=[t_ps_ss], sig=(half == 1))
                a_ = 1.0 / (256.0 * (1 - li) ** 2)
                fw.op("act", lambda e: e.activation(out=sd[:, 0:Q], in_=ps_ss[:, 0:Q], func=AF.Sqrt, scale=a_, bias=EPS / (1 - li) ** 2), reads=[t_ps_ss], writes=[t_sd])
                fw.op("dve", lambda e: e.reciprocal(out=rs[:, 0:Q], in_=sd[:, 0:Q]), reads=[t_sd], writes=[t_rs])
                for half in range(2):
                    fw.op("dve", lambda e, half=half: e.scalar_tensor_tensor(out=aoT[:, 2 * h + half, q0:q0 + Q], in0=osum[half][:, 0:Q], scalar=gcol[:, half:half + 1],
                                                                         in1=rs[:, 0:Q], op0=ALU.mult, op1=ALU.mult),
                          reads=[t_osum[half], t_rs, t_lam], writes=[t_ao[qi]])
        ph.close()
        ph = Phase(fw)
        gate = [ph.sb([128, D], F32, "gate1") for _ in range(2)]
        t_gate = T("gate1")
        for r in range(2):
            self.load_bc(gate[r][:], t_gate, layer, r, 2)
        wo = ph.sb([128, KD, D], BF16, "wo")
        t_wo = [T("wo%d" % i) for i in range(4)]
        for n in range(4):
            fw.dma("pool", wo[:, :, n * 512:(n + 1) * 512], I["da_w_o"][j, :, n * 512:(n + 1) * 512].rearrange("(k p) n -> p k n", p=128), writes=[t_wo[n]])
        self.oproj(ph, tiles, gate, t_gate, wo, t_wo, KD, aoT, lambda tt: [t_ao[min(tt // 4, 4)]])
        ph.close()
        ph0.close()

    def oproj(self, ph, tiles, gate, t_gate, wo, t_wo, nk, aT, t_a_of):
        fw = self.fw
        xt = [ph.sb([128, D], F32, "xt") for _ in range(2)]
        t_xt = [T("xt0"), T("xt1")]
        tmp = ph.sb([128, 512], F32, "tmp")
        t_tmp = T("tmp")
        po = [ph.ps([128, 512], F32, "po") for _ in range(2)]
        t_po = [T("po0"), T("po1")]

        def load(i):
            tt = tiles[i]
            fw.dma("sp", xt[i % 2][:], self.xs[tt * 128:(tt + 1) * 128, :], reads=[self.t_xs[tt]], writes=[t_xt[i % 2]])

        load(0)
        pi = 0
        for i, tt in enumerate(tiles):
            if i + 1 < len(tiles):
                load(i + 1)
            x_, tx = xt[i % 2], t_xt[i % 2]
            r = 0 if tt < 16 else 1
            for n in range(4):
                p_, tp = po[pi % 2], t_po[pi % 2]
                pi += 1
                for k in range(nk):
                    fw.op("pe", lambda e, k=k, n=n: e.matmul(p_[:], lhsT=aT[:, k, tt * 128:(tt + 1) * 128], rhs=wo[:, k, n * 512:(n + 1) * 512], start=(k == 0), stop=(k == nk - 1)),
                          reads=t_a_of(tt) + [t_wo[n]], writes=[tp], sig=(k == nk - 1))
                fw.op("dve", lambda e, n=n: e.tensor_tensor(out=tmp[:], in0=p_[:], in1=gate[r][:, n * 512:(n + 1) * 512], op=ALU.mult), reads=[tp, t_gate], writes=[t_tmp])
                fw.op("dve", lambda e, n=n: e.tensor_tensor(out=x_[:, n * 512:(n + 1) * 512], in0=x_[:, n * 512:(n + 1) * 512], in1=tmp[:], op=ALU.add),
                      reads=[tx, t_tmp], writes=[tx])
            fw.dma("sp", self.xs[tt * 128:(tt + 1) * 128, :], x_[:], reads=[tx], writes=[self.t_xs[tt]])

    def final_phase(self, ph):
        fw, I = self.fw, self.I
        gw = ph.sb([128, D], F32, "gw")
        t_gw = T("gw")
        fw.dma("sp", gw[:], I["final_norm"].partition_broadcast(128), writes=[t_gw])
        xb = [ph.sb([128, D], F32, "xb") for _ in range(2)]
        t_xb = [T("xb0"), T("xb1")]
        ob = [ph.sb([128, D], F32, "ob") for _ in range(2)]
        t_ob = [T("ob0"), T("ob1")]
        junk = ph.sb([128, D], BF16, "junk")
        t_junk = T("junk")
        ss = ph.sb([128, 4], F32, "ss")
        t_ss = T("ss")
        for tt in range(16):
            b = tt % 2
            fw.dma("sp", xb[b][:], self.xs[tt * 128:(tt + 1) * 128, :], reads=[self.t_xs[tt]], writes=[t_xb[b]])
            fw.op("act", lambda e: e.activation(out=junk[:], in_=xb[b][:], func=AF.Square, accum_out=ss[:, 0:1]), reads=[t_xb[b]], writes=[t_junk, t_ss])
            fw.op("act", lambda e: e.activation(out=ss[:, 1:2], in_=ss[:, 0:1], func=AF.Sqrt, scale=1.0 / D, bias=EPS), reads=[t_ss], writes=[t_ss])
            fw.op("dve", lambda e: e.reciprocal(out=ss[:, 2:3], in_=ss[:, 1:2]), reads=[t_ss], writes=[t_ss])
            fw.op("dve", lambda e: e.scalar_tensor_tensor(out=ob[b][:], in0=xb[b][:], scalar=ss[:, 2:3], in1=gw[:], op0=ALU.mult, op1=ALU.mult),
                  reads=[t_xb[b], t_ss, t_gw], writes=[t_ob[b]])
            fw.dma("sp", self.out[tt * 128:(tt + 1) * 128, :], ob[b][:], reads=[t_ob[b]], writes=[self.t_out])

    def dump_xs(self, ph):
        fw = self.fw
        buf = [ph.sb([128, D], F32, "cp") for _ in range(2)]
        tb = [T("cp0"), T("cp1")]
        for tt in range(16):
            b = tt % 2
            fw.dma("sp", buf[b][:], self.xs[tt * 128:(tt + 1) * 128, :], reads=[self.t_xs[tt]], writes=[tb[b]])
            fw.dma("sp", self.out[tt * 128:(tt + 1) * 128, :], buf[b][:], reads=[tb[b]], writes=[self.t_out])


def build(n_layers=DEPTH, dbg=None):
    nc = bass.Bass("TRN2", target_bir_lowering=False)
    with ExitStack() as st:
        fw = FW(nc, st)
        pg = Prog(nc, fw, n_layers, dbg)
        phc = Phase(fw)
        pg.consts(phc)
        ph = Phase(fw)
        pg.init_xs(ph)
        pg.mods_phase(ph)
        ph.close()
        for layer in range(n_layers):
            last = layer == DEPTH - 1
            if layer % 2 == 0:
                pg.da_layer(layer)
            else:
                pg.gdn_layer(layer, last)
            if dbg == "mix%d" % layer:
                break
            pg.ffn(layer, list(range(16)) if last else list(range(NT)))
        ph = Phase(fw)
        if dbg is None:
            pg.final_phase(ph)
        else:
            pg.dump_xs(ph)
        fw.barrier()
        ph.close()
        phc.close()
        print("program: %d instructions, %d waits" % (fw.n_inst, fw.n_wait))
    return nc


def rope_tables():
    t = np.arange(T_LAT)
    rows = (t // 64).astype(np.float32)
    cols = (t % 64).astype(np.float32)
    inv = (np.float32(10000.0) ** (-np.arange(32, dtype=np.float32) / np.float32(32))).astype(np.float32)
    ang = np.stack([rows[:, None] * inv, cols[:, None] * inv], axis=1).astype(np.float32)
    cs = np.stack([np.cos(ang), np.sin(ang)], axis=1).astype(np.float32)
    cs = cs.reshape(16, 128, 2, 64).transpose(1, 0, 2, 3)
    return np.ascontiguousarray(cs)


def make_in_maps(inp, cores):
    f = lambda a: np.ascontiguousarray(np.asarray(a, dtype=np.float32))
    shared = {
        "w_mod": f(inp["w_mod"]), "b_mod": f(inp["b_mod"]), "norm_mix": f(inp["norm_mix"]), "norm_ffn": f(inp["norm_ffn"]),
        "da_w_qkv": f(inp["da_w_qkv"]), "da_lambda": f(inp["da_lambda"]),
        "da_gainT": f(np.asarray(inp["da_head_gain"]).reshape(2, 2, 128).transpose(0, 2, 1)),
        "da_w_o": f(inp["da_w_o"]), "gdn_w_in": f(inp["gdn_w_in"]),
        "gdn_convT": f(np.asarray(inp["gdn_conv"]).reshape(2, 5, 64, 128).transpose(0, 3, 2, 1)),
        "gdn_a_log": f(np.asarray(inp["gdn_a_log"]).reshape(2, 64)), "gdn_dt_bias": f(np.asarray(inp["gdn_dt_bias"]).reshape(2, 64)),
        "gdn_norm_gain": f(inp["gdn_norm_gain"]), "gdn_w_o": f(inp["gdn_w_o"]),
        "ffn_w_up": f(inp["ffn_w_up"]),
        "ffn_convT": f(np.asarray(inp["ffn_conv"]).reshape(DEPTH, 3, 2 * NFF, 128).transpose(0, 3, 2, 1)),
        "ffn_w_down": f(inp["ffn_w_down"]), "final_norm": f(inp["final_norm"]),
        "rope": rope_tables(),
    }
    maps = []
    for b in cores:
        m = dict(shared)
        m["x"] = f(inp["x"][b])
        m["ctx"] = f(inp["ctx"][b])
        cT = np.stack([np.asarray(inp["c"][b]).reshape(KD, 128).T, np.asarray(inp["c_ctx"]).reshape(KD, 128).T], axis=-1)
        m["cT"] = f(cT)
        maps.append(m)
    return maps


def kernel(**inputs):
    nc = build()
    maps = make_in_maps(inputs, list(range(8)))
    res = run_bass_kernel_spmd(nc, maps, core_ids=list(range(8)))
    return np.stack([np.asarray(r["out"], dtype=np.float32) for r in res.results], axis=0)


def _gdn_layer(self, layer, last):
    fw, I = self.fw, self.I
    nc = self.nc
    j = layer // 2
    tiles = list(range(NT))
    dt = nc.dram_tensor
    if not hasattr(self, "gq_s"):
        self.gq_s = dt("gq_s", [36, 128, 16, 64], F32, kind="Internal").ap()
        self.gk_s = dt("gk_s", [36, 128, 16, 64], F32, kind="Internal").ap()
        self.gv_s = dt("gv_s", [36, 128, 32, 64], F32, kind="Internal").ap()
        self.z_s = dt("z_s", [T_ALL, 2 * D], BF16, kind="Internal").ap()
        self.o_s = dt("o_s", [2, T_ALL, 2 * D], F32, kind="Internal").ap()
        self.t_gq = [T("gq%d" % i) for i in range(16)]
        self.t_gk = [T("gk%d" % i) for i in range(16)]
        self.t_gv = [T("gv%d" % i) for i in range(32)]
        self.t_z = [T("z%d" % i) for i in range(8)]
        self.t_os = [[T("os") for _ in range(36)] for _ in range(2)]
    phA = Phase(fw)
    betaS = phA.sb([64, 36, 2, 32], F32, "betaS")
    gS = phA.sb([64, 36, 2, 32], F32, "gS")
    t_bg = T("betag")
    ph0 = Phase(fw)
    hT = ph0.sb([128, KD, T_ALL], BF16, "g_h1T")
    t_hT = [T("gh1T%d" % i) for i in range(NT)]
    ph = Phase(fw)
    self.norm_phase(ph, layer, 1, 0, hT, t_hT, tiles)
    ph.close()
    ph = Phase(fw)
    WP = 2312
    LAT0, CTX0 = 2, 2054
    wu = [ph.sb([128, KD, 128], BF16, "gwu") for _ in range(3)]
    t_wu = [T("gwu%d" % i) for i in range(3)]
    cw = ph.sb([128, 64, 5], F32, "gcw")
    t_cw = T("gcw")
    fw.dma("sp", cw[:], I["gdn_convT"][j], writes=[t_cw])
    U = [ph.sb([128, WP], F32, "gU") for _ in range(2)]
    t_U = [T("gU0"), T("gU1")]
    for b in range(2):
        fw.op("pool", lambda e, b=b: e.memset(U[b][:], 0.0), writes=[t_U[b]])
    cg = ph.sb([128, WP], F32, "gcg")
    cp = ph.sb([128, WP], F32, "gcp")
    sg2 = [ph.sb([128, WP], F32, "gsg") for _ in range(2)]
    sq_ = ph.sb([128, WP], F32, "gsq")
    sq2 = [sq_, sq_]
    t_sq_ = T("sq")
    t_sg2, t_sq2 = [T("sg0"), T("sg1")], [t_sq_, t_sq_]
    pend_post = []
    sd = ph.sb([128, WP], F32, "gsd")
    xn = [ph.sb([128, WP], F32, "gxn") for _ in range(2)]
    t_cg, t_cp, t_sd = T("cg"), T("cp"), T("sd")
    t_xn = [T("xn0"), T("xn1")]
    PA = ph.ps([128, 2048], F32, "gPA")
    PB = ph.ps([128, 2048], F32, "gPB")
    t_PA, t_PB = T("gPA"), T("gPB")
    C0, C1 = 2, 2310

    def load_w(jc):
        fw.dma("pool", wu[jc % 3][:], I["gdn_w_in"][j, :, jc * 128:(jc + 1) * 128].rearrange("(k p) n -> p k n", p=128), writes=[t_wu[jc % 3]])

    load_w(0)
    load_w(1)
    for jc in range(64):
        if jc + 2 < 64:
            load_w(jc + 2)
        w, tw = wu[jc % 3], t_wu[jc % 3]
        ub = jc % 2
        Ub, tU = U[ub], t_U[ub]
        for blk in range(4):
            for k in range(KD):
                fw.op("pe", lambda e, blk=blk, k=k: e.matmul(PA[:, blk * 512:(blk + 1) * 512], lhsT=w[:, k, :], rhs=hT[:, k, blk * 512:(blk + 1) * 512],
                                                            start=(k == 0), stop=(k == KD - 1)),
                      reads=[tw] + t_hT[blk * 4:(blk + 1) * 4], writes=[t_PA], sig=(k == KD - 1))
        fw.op("act", lambda e: e.activation(out=Ub[:, LAT0:LAT0 + 2048], in_=PA[:], func=AF.Copy), reads=[t_PA], writes=[tU])
        while pend_post:
            pend_post.pop(0)()
        for k in range(KD):
            fw.op("pe", lambda e, k=k: e.matmul(PA[:, 0:256], lhsT=w[:, k, :], rhs=hT[:, k, 2048:2304], start=(k == 0), stop=(k == KD - 1)),
                  reads=[tw] + t_hT[16:18], writes=[t_PA], sig=(k == KD - 1))
        fw.op("act", lambda e: e.activation(out=Ub[:, CTX0:CTX0 + 256], in_=PA[:, 0:256], func=AF.Copy), reads=[t_PA], writes=[tU])
        fw.op("dve", lambda e: e.tensor_scalar(out=cg[:, C0:C1], in0=Ub[:, C0 - 2:C1 - 2], scalar1=cw[:, jc, 0:1], scalar2=None, op0=ALU.mult),
              reads=[tU, t_cw], writes=[t_cg])
        for tap in (1, 2):
            fw.op("dve", lambda e, tap=tap: e.scalar_tensor_tensor(out=cg[:, C0:C1], in0=Ub[:, C0 - 2 + tap:C1 - 2 + tap], scalar=cw[:, jc, tap:tap + 1],
                                                                in1=cg[:, C0:C1], op0=ALU.mult, op1=ALU.add),
                  reads=[tU, t_cw, t_cg], writes=[t_cg])
        fw.op("pool", lambda e: e.tensor_scalar(out=cp[:, C0:C1], in0=Ub[:, C0 + 1:C1 + 1], scalar1=cw[:, jc, 3:4], scalar2=0.0, op0=ALU.mult, op1=ALU.add),
              reads=[tU, t_cw], writes=[t_cp])
        fw.op("dve", lambda e: e.scalar_tensor_tensor(out=cg[:, C0:C1], in0=Ub[:, C0 + 2:C1 + 2], scalar=cw[:, jc, 4:5], in1=cg[:, C0:C1], op0=ALU.mult, op1=ALU.add),
              reads=[tU, t_cw, t_cg], writes=[t_cg])
        fw.op("dve", lambda e: e.tensor_tensor(out=cg[:, C0:C1], in0=cg[:, C0:C1], in1=cp[:, C0:C1], op=ALU.add), reads=[t_cg, t_cp], writes=[t_cg])
        xo, txo = xn[jc % 2], t_xn[jc % 2]
        if jc < 32:
            sg, sq, t_sg, t_sq = sg2[jc % 2], sq2[jc % 2], t_sg2[jc % 2], t_sq2[jc % 2]
            fw.op("act", lambda e: e.activation(out=sg[:, C0:C1], in_=cg[:, C0:C1], func=AF.Silu), reads=[t_cg], writes=[t_sg])
            fw.op("dve", lambda e: e.tensor_tensor(out=sq[:, C0:C1], in0=sg[:, C0:C1], in1=sg[:, C0:C1], op=ALU.mult), reads=[t_sg], writes=[t_sq])

            def post(jc=jc, sg=sg, sq=sq, t_sg=t_sg, t_sq=t_sq, xo=xo, txo=txo):
                for blk in range(5):
                    a0 = C0 + blk * 512
                    a1 = min(C1, a0 + 512)
                    dstp = PB[:, blk * 512:blk * 512 + (a1 - a0)] if blk < 4 else PB[:, 0:a1 - a0]
                    fw.op("pe", lambda e: e.matmul(dstp, lhsT=self.ones_f[:], rhs=sq[:, a0:a1], start=True, stop=True),
                          reads=[t_sq, self.t_c], writes=[t_PB])
                    if blk == 3:
                        fw.op("act", lambda e: e.activation(out=sd[:, C0:C0 + 2048], in_=PB[:], func=AF.Sqrt, scale=1.0, bias=EPS), reads=[t_PB], writes=[t_sd])
                    if blk == 4:
                        fw.op("act", lambda e: e.activation(out=sd[:, a0:a1], in_=PB[:, 0:a1 - a0], func=AF.Sqrt, scale=1.0, bias=EPS), reads=[t_PB], writes=[t_sd])
                fw.op("dve", lambda e: e.reciprocal(out=sd[:, C0:C1], in_=sd[:, C0:C1]), reads=[t_sd], writes=[t_sd])
                qs = (128 ** -0.5) if jc < 16 else 1.0
                fw.op("dve", lambda e: e.scalar_tensor_tensor(out=xo[:, C0:C1], in0=sg[:, C0:C1], scalar=qs, in1=sd[:, C0:C1], op0=ALU.mult, op1=ALU.mult),
                      reads=[t_sg, t_sd], writes=[txo])
                dst, tl, h = (self.gq_s, self.t_gq, jc) if jc < 16 else (self.gk_s, self.t_gk, jc - 16)
                fw.dma("sp", dst[0:32, :, h, :].rearrange("c p t -> p c t"), xo[:, LAT0:LAT0 + 2048].rearrange("p (c t) -> p c t", t=64), reads=[txo], writes=[tl[h]])
                fw.dma("sp", dst[32:36, :, h, :].rearrange("c p t -> p c t"), xo[:, CTX0:CTX0 + 256].rearrange("p (c t) -> p c t", t=64), reads=[txo], writes=[tl[h]])
            pend_post.append(post)
        else:
            fw.op("act", lambda e: e.activation(out=xo[:, C0:C1], in_=cg[:, C0:C1], func=AF.Silu), reads=[t_cg], writes=[txo])
            dst, tl, h = self.gv_s, self.t_gv, jc - 32
            fw.dma("sp", dst[0:32, :, h, :].rearrange("c p t -> p c t"), xo[:, LAT0:LAT0 + 2048].rearrange("p (c t) -> p c t", t=64), reads=[txo], writes=[tl[h]])
            fw.dma("sp", dst[32:36, :, h, :].rearrange("c p t -> p c t"), xo[:, CTX0:CTX0 + 256].rearrange("p (c t) -> p c t", t=64), reads=[txo], writes=[tl[h]])
    while pend_post:
        pend_post.pop(0)()
    ph.close()
    ph = Phase(fw)
    wz = [ph.sb([128, KD, 512], BF16, "wz") for _ in range(2)]
    t_wz = [T("wz0"), T("wz1")]
    stZ = [ph.sb([128, NT, 512], BF16, "stZ") for _ in range(2)]
    t_stZ = [T("stZ0"), T("stZ1")]
    pz = [ph.ps([128, 512], F32, "pz") for _ in range(2)]
    t_pz = [T("pz0"), T("pz1")]

    def load_wz(n):
        fw.dma("pool", wz[n % 2][:], I["gdn_w_in"][j, :, 8192 + n * 512:8192 + (n + 1) * 512].rearrange("(k p) n -> p k n", p=128), writes=[t_wz[n % 2]])

    load_wz(0)
    it = 0
    for n in range(8):
        if n + 1 < 8:
            load_wz(n + 1)
        w_, tw = wz[n % 2], t_wz[n % 2]
        for tt in tiles:
            b = it % 2
            it += 1
            for k in range(KD):
                fw.op("pe", lambda e, k=k, tt=tt: e.matmul(pz[b][:], lhsT=hT[:, k, tt * 128:(tt + 1) * 128], rhs=w_[:, k, :], start=(k == 0), stop=(k == KD - 1)),
                      reads=[tw, t_hT[tt]], writes=[t_pz[b]], sig=(k == KD - 1))
            fw.op("act", lambda e, tt=tt: e.activation(out=stZ[n % 2][:, tt, :], in_=pz[b][:], func=AF.Silu), reads=[t_pz[b]], writes=[t_stZ[n % 2]])
        fw.dma("sp", self.z_s[:, n * 512:(n + 1) * 512].rearrange("(t p) d -> p t d", p=128), stZ[n % 2][:], reads=[t_stZ[n % 2]], writes=[self.t_z[n]])
    wab = ph.sb([128, KD, 128], BF16, "wab")
    t_wab = T("wab")
    fw.dma("pool", wab[:], I["gdn_w_in"][j, :, 12288:12416].rearrange("(k p) n -> p k n", p=128), writes=[t_wab])
    dtb = ph.sb([64, 64], F32, "dtb")
    nega = ph.sb([64, 64], F32, "nega")
    t_dtb = T("dtb")
    fw.dma("sp", dtb[:], I["gdn_dt_bias"][j].partition_broadcast(64), writes=[t_dtb])
    fw.dma("sp", nega[:], I["gdn_a_log"][j].partition_broadcast(64), writes=[t_dtb])
    fw.op("act", lambda e: e.activation(out=nega[:], in_=nega[:], func=AF.Exp), reads=[t_dtb], writes=[t_dtb])
    fw.op("dve", lambda e: e.tensor_scalar(out=nega[:], in0=nega[:], scalar1=-1.0, scalar2=None, op0=ALU.mult), reads=[t_dtb], writes=[t_dtb])
    sp_t = ph.sb([64, 4, 2, 32], F32, "sp_t")
    t_sp = T("sp_t")
    for c4 in range(9):
        b = c4 % 2
        for cc in range(4):
            c = c4 * 4 + cc
            for k in range(KD):
                fw.op("pe", lambda e, k=k, cc=cc, c=c: e.matmul(pz[b][0:64, cc * 128:(cc + 1) * 128], lhsT=hT[:, k, c * 64:(c + 1) * 64], rhs=wab[:, k, :],
                                                               start=(k == 0), stop=(k == KD - 1)),
                      reads=[t_wab] + t_hT[c // 2:c // 2 + 1], writes=[t_pz[b]], sig=(k == KD - 1))
        pv = pz[b][0:64, :].rearrange("p (c d b h) -> p c d b h", c=4, d=2, b=2)
        fw.op("act", lambda e: e.activation(out=betaS[:, c4 * 4:(c4 + 1) * 4, :, :], in_=pv[:, :, :, 0, :], func=AF.Sigmoid), reads=[t_pz[b]], writes=[t_bg])
        dtv = dtb[:].rearrange("p (d h) -> p d h", d=2).unsqueeze(1).to_broadcast([64, 4, 2, 32])
        ngv = nega[:].rearrange("p (d h) -> p d h", d=2).unsqueeze(1).to_broadcast([64, 4, 2, 32])
        fw.op("dve", lambda e: e.tensor_tensor(out=sp_t[:], in0=pv[:, :, :, 1, :], in1=dtv, op=ALU.add), reads=[t_pz[b], t_dtb], writes=[t_sp])
        fw.op("act", lambda e: e.activation(out=sp_t[:], in_=sp_t[:], func=AF.Exp), reads=[t_sp], writes=[t_sp])
        fw.op("act", lambda e: e.activation(out=sp_t[:], in_=sp_t[:], func=AF.Ln, bias=1.0, scale=1.0), reads=[t_sp], writes=[t_sp])
        fw.op("dve", lambda e: e.tensor_tensor(out=gS[:, c4 * 4:(c4 + 1) * 4, :, :], in0=sp_t[:], in1=ngv, op=ALU.mult), reads=[t_sp, t_dtb], writes=[t_bg])
    ph.close()
    ph0.close()
    ph = Phase(fw)
    NH = 16
    NK = 8
    msk = ph.sb([64, 6, 64], F32, "msk")
    t_msk = T("msk")
    specs = [(1.0, [[1, 64]], -1, ALU.is_ge, 0.0), (1.0, [[-1, 64]], 1, ALU.is_ge, 0.0),
             (0.0, [[-1, 64]], 1, ALU.is_ge, -30000.0), (0.0, [[1, 64]], -1, ALU.is_ge, -30000.0),
             (1.0, [[-1, 64]], 1, ALU.is_gt, 0.0), (1.0, [[1, 64]], -1, ALU.is_gt, 0.0)]
    for mi, (init, pat, cm, cmp_, fill) in enumerate(specs):
        fw.op("pool", lambda e: e.memset(msk[:, mi, :], init), reads=[t_msk], writes=[t_msk])
        fw.op("pool", lambda e: e.affine_select(out=msk[:, mi, :], in_=msk[:, mi, :], compare_op=cmp_, fill=fill, base=0, pattern=pat, channel_multiplier=cm),
              reads=[t_msk], writes=[t_msk])
    U1, L1, NL, NU, SL, SU = [msk[:, i, :] for i in range(6)]
    identf64 = self.ident_f[0:64, 0:64]
    PS = ph.ps([128, 4096], F32, "gPS")
    t_bank = [T("bank%d" % i) for i in range(8)]
    ring = [0]

    def psget(nb):
        s = ring[0]
        if s + nb > 5:
            s = 0
        ring[0] = (s + nb) % 5
        return s, t_bank[s:s + nb]

    def psgetB(role):
        return 5 + role, t_bank[5 + role:6 + role]

    qc = [ph.sb([128, NK, 64], F32, "qc") for _ in range(2)]
    kc = [ph.sb([128, NK, 64], F32, "kc") for _ in range(2)]
    vc = [ph.sb([128, NH, 64], F32, "vc") for _ in range(2)]
    t_in = [T("in0"), T("in1")]
    S = ph.sb([128, NH, 128], F32, "S")
    t_S = [T("S%d" % i) for i in range(4)]
    ktok = ph.sb([64, NK, 128], F32, "ktok")
    vtok = ph.sb([64, NH, 128], F32, "vtok")
    vb = ph.sb([64, NH, 128], BF16, "vb")
    ost = ph.sb([64, NH, 128], F32, "ost")
    t_ost = T("ost")
    kbg = ph.sb([64, NH, 128], BF16, "kbg")
    kdec = ph.sb([64, NH, 128], BF16, "kdec")
    kcb = ph.sb([128, NK, 64], BF16, "kcb")
    qcb = ph.sb([128, NK, 64], BF16, "qcb")
    t_kcb = T("kcb")
    Sbf = ph.sb([128, NH, 128], BF16, "Sbf")
    t_ktok, t_vtok, t_vb, t_kbg, t_kdec = T("ktok"), T("vtok"), T("vb"), T("kbg"), T("kdec")
    sm = ph.sb([128, 8, NH], F32, "sm")
    t_sm = T("sm")
    tmpA = ph.sb([64, NH, 64], F32, "tmpA")
    betaR = ph.sb([64, NH, 64], F32, "betaR")
    decay = ph.sb([64, NH, 64], F32, "decay")
    decayT = ph.sb([64, NH, 64], F32, "decayT")
    qgT = ph.sb([128, NH, 64], BF16, "qgT")
    qgF = ph.sb([128, NH, 64], F32, "qgF")
    t_qgF = T("qgF")
    qkT = ph.sb([64, NH, 64], BF16, "qkT")
    wT = ph.sb([128, NH, 64], BF16, "wT")
    t_tmpA, t_betaR, t_decay, t_decayT, t_qgT, t_qkT, t_wT = T("tmpA"), T("betaR"), T("decay"), T("decayT"), T("qgT"), T("qkT"), T("wT")
    Pb = [ph.sb([64, NH, 64], F32, "Pb")]
    Qb = [ph.sb([64, NH, 64], F32, "Qb")]
    Ps = ph.sb([128, 2, NK, 64], F32, "Ps")
    Qs = ph.sb([128, 2, NK, 64], F32, "Qs")
    Ys = ph.sb([128, 2, NK, 64], F32, "Ys")
    t_Ps, t_Qs, t_Ys = [T("Ps0"), T("Ps1")], [T("Qs0"), T("Qs1")], [T("Ys0"), T("Ys1")]
    I2 = ph.sb([128, 64], F32, "I2")
    t_I2 = T("I2")
    fw.op("act", lambda e: e.activation(out=I2[0:64, :], in_=self.ident_f[0:64, 0:64], func=AF.Copy), reads=[self.t_c], writes=[t_I2])
    fw.op("act", lambda e: e.activation(out=I2[64:128, :], in_=self.ident_f[64:128, 64:128], func=AF.Copy), reads=[self.t_c], writes=[t_I2])
    Ybf = ph.sb([64, NH, 64], BF16, "Ybf")
    t_Ybf = T("Ybf")
    t_Pb, t_Qb = [T("P0")], [T("Q0")]
    vnew = [ph.sb([64, 4, 128], BF16, "vnew") for _ in range(2)]
    t_vnew = [T("vnew0"), T("vnew1")]

    def bc_pairs(ap3):
        return ap3.unsqueeze(2).to_broadcast([ap3.shape[0], NK, 2, ap3.shape[2]])

    def v4(ap3):
        return ap3.rearrange("p (a b) f -> p a b f", b=2)

    def bc_h(ap2, n=NH):
        return ap2.unsqueeze(1).to_broadcast([ap2.shape[0], n, ap2.shape[1]])

    def bc_f(ap2, f):
        return ap2.unsqueeze(2).to_broadcast([ap2.shape[0], ap2.shape[1], f])

    iters = []
    for d in range(2):
        order = [32, 33, 34, 35] + list(range(32)) if d == 0 else [35, 34, 33, 32] + list(range(31, -1, -1))
        for hh in range(2):
            for si, c in enumerate(order):
                iters.append((d, hh, c, si == 0))

    def load_in(ii):
        d, hh, c, _ = iters[ii]
        b = ii % 2
        fw.dma("sp", qc[b][:], self.gq_s[c, :, hh * NK:(hh + 1) * NK, :], reads=self.t_gq[hh * NK:(hh + 1) * NK], writes=[t_in[b]])
        fw.dma("sp", kc[b][:], self.gk_s[c, :, hh * NK:(hh + 1) * NK, :], reads=self.t_gk[hh * NK:(hh + 1) * NK], writes=[t_in[b]])
        fw.dma("sp", vc[b][:], self.gv_s[c, :, hh * NH:(hh + 1) * NH, :], reads=self.t_gv[hh * NH:(hh + 1) * NH], writes=[t_in[b]])

    qgT2, t_qgT2 = [qgT, ph.sb([128, NH, 64], BF16, "qgTb")], [t_qgT, T("qgTb")]
    qkT2, t_qkT2 = [qkT, ph.sb([64, NH, 64], BF16, "qkTb")], [t_qkT, T("qkTb")]
    kdec2, t_kdec2 = [kdec, ph.sb([64, NH, 128], BF16, "kdecb")], [t_kdec, T("kdecb")]
    wT2, t_wT2 = [wT, ph.sb([128, NH, 64], BF16, "wTb")], [t_wT, T("wTb")]
    sm2, t_sm2 = [sm, ph.sb([128, 8, NH], F32, "smb")], [t_sm, T("smb")]
    u2, t_u2 = [ph.sb([64, NH, 128], F32, "u") for _ in range(2)], [T("u0"), T("u1")]

    def stageA(ii):
        d, hh, c, first = iters[ii]
        if ii + 1 < len(iters):
            load_in(ii + 1)
        b = ii % 2
        qgT, t_qgT, qkT, t_qkT, kdec, t_kdec, wT, t_wT, sm, t_sm = qgT2[b], t_qgT2[b], qkT2[b], t_qkT2[b], kdec2[b], t_kdec2[b], wT2[b], t_wT2[b], sm2[b], t_sm2[b]
        u, t_u = u2[b], t_u2[b]
        q_c, k_c, v_c, tin = qc[b], kc[b], vc[b], t_in[b]
        Mc = U1 if d == 0 else L1
        NEG, NEGT = (NL, NU) if d == 0 else (NU, NL)
        ST, STT_ = (SL, SU) if d == 0 else (SU, SL)
        g_c = gS[:, c, d, hh * NH:(hh + 1) * NH]
        b_c = betaS[:, c, d, hh * NH:(hh + 1) * NH]
        fw.op("pool", lambda e: e.tensor_copy(out=kcb[:], in_=k_c[:]), reads=[tin], writes=[t_kcb])
        fw.op("pool", lambda e: e.tensor_copy(out=qcb[:], in_=q_c[:]), reads=[tin], writes=[t_kcb])
        yield
        s, tb = psget(2)
        for h in range(NK):
            fw.op("pe", lambda e, h=h: e.transpose(out=PS[0:64, s * 512 + h * 128:s * 512 + (h + 1) * 128], in_=k_c[:, h, :], identity=self.ident_f[:]),
                  reads=[tin, self.t_c], writes=tb, sig=(h == NK - 1))
        fw.op("act", lambda e: e.activation(out=ktok[:].rearrange("p h f -> p (h f)"), in_=PS[0:64, s * 512:(s + 2) * 512], func=AF.Copy), reads=tb, writes=[t_ktok])
        s, tb = psget(4)
        for h in range(NH):
            fw.op("pe", lambda e, h=h: e.transpose(out=PS[0:64, s * 512 + h * 128:s * 512 + (h + 1) * 128], in_=v_c[:, h, :], identity=self.ident_f[:]),
                  reads=[tin, self.t_c], writes=tb, sig=(h == NH - 1))
        fw.op("act", lambda e: e.activation(out=vtok[:].rearrange("p h f -> p (h f)"), in_=PS[0:64, s * 512:(s + 4) * 512], func=AF.Copy), reads=tb, writes=[t_vtok])
        yield
        s, tb = psget(1)
        fw.op("pe", lambda e: e.matmul(PS[0:64, s * 512:s * 512 + NH], lhsT=Mc, rhs=g_c, start=True, stop=True), reads=[t_msk, t_bg], writes=tb)
        fw.op("pe", lambda e: e.matmul(PS[:, s * 512 + 32:s * 512 + 32 + NH], lhsT=self.ones_f[0:64, :], rhs=g_c, start=True, stop=True), reads=[self.t_c, t_bg], writes=tb)
        gam, glast, eg, gle, dgl, be = [sm[:, i, :] for i in range(6)]
        fw.op("dve", lambda e: e.tensor_copy(out=gam[0:64], in_=PS[0:64, s * 512:s * 512 + NH]), reads=tb, writes=[t_sm])
        fw.op("dve", lambda e: e.tensor_copy(out=glast, in_=PS[:, s * 512 + 32:s * 512 + 32 + NH]), reads=tb, writes=[t_sm])
        fw.op("act", lambda e: e.activation(out=eg[0:64], in_=gam[0:64], func=AF.Exp), reads=[t_sm], writes=[t_sm])
        fw.op("act", lambda e: e.activation(out=gle, in_=glast, func=AF.Exp), reads=[t_sm], writes=[t_sm])
        fw.op("dve", lambda e: e.tensor_tensor(out=dgl[0:64], in0=glast[0:64], in1=gam[0:64], op=ALU.subtract), reads=[t_sm], writes=[t_sm])
        fw.op("act", lambda e: e.activation(out=dgl[0:64], in_=dgl[0:64], func=AF.Exp), reads=[t_sm], writes=[t_sm])
        fw.op("dve", lambda e: e.tensor_tensor(out=be[0:64], in0=b_c, in1=eg[0:64], op=ALU.mult), reads=[t_sm, t_bg], writes=[t_sm])
        yield
        fw.op("dve", lambda e: e.tensor_tensor(out=tmpA[:], in0=bc_f(b_c, 64), in1=bc_h(identf64), op=ALU.mult), reads=[t_bg, self.t_c], writes=[t_tmpA])
        s, tb = psget(2)
        for x in range(2):
            fw.op("pe", lambda e, x=x: e.matmul(PS[0:64, (s + x) * 512:(s + x + 1) * 512], lhsT=self.ones_f[0:64, 0:64], rhs=tmpA[:, x * 8:(x + 1) * 8, :].rearrange("p h f -> p (h f)"),
                                              start=True, stop=True), reads=[t_tmpA, self.t_c], writes=tb)
        fw.op("act", lambda e: e.activation(out=betaR[:].rearrange("p h f -> p (h f)"), in_=PS[0:64, s * 512:(s + 2) * 512], func=AF.Copy), reads=tb, writes=[t_betaR])
        yield
        fw.op("dve", lambda e: e.tensor_tensor(out=tmpA[:], in0=bc_f(g_c, 64), in1=bc_h(Mc), op=ALU.mult), reads=[t_bg, t_msk], writes=[t_tmpA])
        s, tb = psget(2)
        for x in range(2):
            fw.op("pe", lambda e, x=x: e.matmul(PS[:, (s + x) * 512:(s + x + 1) * 512], lhsT=self.ones_f[0:64, :], rhs=tmpA[:, x * 8:(x + 1) * 8, :].rearrange("p h f -> p (h f)"),
                                              start=True, stop=True), reads=[t_tmpA, self.t_c], writes=tb)
        R64 = PS[0:64, s * 512:(s + 2) * 512].rearrange("p (h f) -> p h f", h=NH)
        R128 = PS[:, s * 512:(s + 2) * 512].rearrange("p (h f) -> p h f", h=NH)
        fw.op("dve", lambda e: e.tensor_tensor(out=decay[:], in0=bc_f(gam[0:64], 64), in1=R64, op=ALU.subtract), reads=tb + [t_sm], writes=[t_decay])
        fw.op("pool", lambda e: e.tensor_tensor(out=decay[:], in0=decay[:], in1=bc_h(NEG), op=ALU.add), reads=[t_decay, t_msk], writes=[t_decay])
        fw.op("act", lambda e: e.activation(out=decay[:], in_=decay[:], func=AF.Exp), reads=[t_decay], writes=[t_decay])
        fw.op("dve", lambda e: e.tensor_tensor(out=decayT[:], in0=R64, in1=bc_f(gam[0:64], 64), op=ALU.subtract), reads=tb + [t_sm], writes=[t_decayT])
        fw.op("pool", lambda e: e.tensor_tensor(out=decayT[:], in0=decayT[:], in1=bc_h(NEGT), op=ALU.add), reads=[t_decayT, t_msk], writes=[t_decayT])
        fw.op("act", lambda e: e.activation(out=decayT[:], in_=decayT[:], func=AF.Exp), reads=[t_decayT], writes=[t_decayT])
        fw.op("act", lambda e: e.activation(out=qgF[:], in_=R128, func=AF.Exp), reads=tb, writes=[t_qgF])
        fw.op("dve", lambda e: e.tensor_tensor(out=v4(qgT[:]), in0=v4(qgF[:]), in1=bc_pairs(q_c[:]), op=ALU.mult), reads=[t_qgF, tin], writes=[t_qgT])
        yield
        s, tb = psget(1)
        for h in range(NK):
            fw.op("pe", lambda e, h=h: e.matmul(PS[0:64, s * 512 + h * 64:s * 512 + (h + 1) * 64], lhsT=kcb[:, h, :], rhs=kcb[:, h, :], start=True, stop=True),
                  reads=[t_kcb], writes=tb)
        kkv = bc_pairs(PS[0:64, s * 512:(s + 1) * 512].rearrange("p (h f) -> p h f", h=NK))
        P, Q = Pb[0], Qb[0]
        tP, tQ = t_Pb[0], t_Qb[0]
        fw.op("dve", lambda e: e.tensor_tensor(out=v4(P[:]), in0=v4(decay[:]), in1=kkv, op=ALU.mult), reads=tb + [t_decay], writes=[tP])
        fw.op("dve", lambda e: e.scalar_tensor_tensor(out=P[:], in0=P[:], scalar=-1.0, in1=bc_f(b_c, 64), op0=ALU.mult, op1=ALU.mult), reads=[tP, t_bg], writes=[tP])
        for par in range(2):
            fw.op("dve", lambda e, par=par: e.tensor_tensor(out=Ps[par * 64:(par + 1) * 64, 0, :, :], in0=v4(P[:])[:, :, par, :], in1=bc_h(ST, NK), op=ALU.mult),
                  reads=[tP, t_msk], writes=[t_Ps[0]])
        fw.op("dve", lambda e: e.tensor_tensor(out=v4(Q[:]), in0=v4(decayT[:]), in1=kkv, op=ALU.mult), reads=tb + [t_decayT], writes=[tQ])
        fw.op("dve", lambda e: e.scalar_tensor_tensor(out=Q[:], in0=Q[:], scalar=-1.0, in1=betaR[:], op0=ALU.mult, op1=ALU.mult), reads=[tQ, t_betaR], writes=[tQ])
        for par in range(2):
            fw.op("dve", lambda e, par=par: e.tensor_tensor(out=Qs[par * 64:(par + 1) * 64, 0, :, :], in0=v4(Q[:])[:, :, par, :], in1=bc_h(STT_, NK), op=ALU.mult),
                  reads=[tQ, t_msk], writes=[t_Qs[0]])
        fw.op("pool", lambda e: e.tensor_tensor(out=Ys[:, 0, :, :], in0=Qs[:, 0, :, :], in1=I2[:].unsqueeze(1).to_broadcast([128, NK, 64]), op=ALU.add),
              reads=[t_Qs[0], t_I2], writes=[t_Ys[0]])
        yield
        s, tb = psget(1)
        for h in range(NK):
            fw.op("pe", lambda e, h=h: e.matmul(PS[0:64, s * 512 + h * 64:s * 512 + (h + 1) * 64], lhsT=kcb[:, h, :], rhs=qcb[:, h, :], start=True, stop=True),
                  reads=[t_kcb], writes=tb)
        kqv = bc_pairs(PS[0:64, s * 512:(s + 1) * 512].rearrange("p (h f) -> p h f", h=NK))
        fw.op("dve", lambda e: e.tensor_tensor(out=v4(qkT[:]), in0=v4(decayT[:]), in1=kqv, op=ALU.mult), reads=tb + [t_decayT], writes=[t_qkT])
        yield
        cur = 0

        def quad_mm(sb_, lhs, rhs, tl, tr, tb):
            for m in range(NK):
                for par in range(2):
                    pr = slice(par * 64, (par + 1) * 64)
                    fw.op("pe", lambda e, m=m, pr=pr: e.matmul(PS[pr, sb_ * 512 + m * 64:sb_ * 512 + (m + 1) * 64], lhsT=lhs[pr, m, :], rhs=rhs[pr, m, :], start=True, stop=True),
                          reads=[tl, tr], writes=tb, sig=(m == NK - 1 and par == 1))

        for lev in range(1, 6):
            nxt = 1 - cur
            P_, Q_, Y_ = Ps[:, cur], Qs[:, cur], Ys[:, cur]
            s, tb = psget(1)
            quad_mm(s, Q_, P_, t_Qs[cur], t_Ps[cur], tb)
            fw.op("act", lambda e: e.activation(out=Ps[:, nxt].rearrange("p h f -> p (h f)"), in_=PS[:, s * 512:(s + 1) * 512], func=AF.Copy), reads=tb, writes=[t_Ps[nxt]])
            yield
            if lev < 5:
                s, tb = psget(1)
                quad_mm(s, P_, Q_, t_Ps[cur], t_Qs[cur], tb)
                fw.op("pool" if False else "act", lambda e: e.activation(out=Qs[:, nxt].rearrange("p h f -> p (h f)"), in_=PS[:, s * 512:(s + 1) * 512], func=AF.Copy),
                      reads=tb, writes=[t_Qs[nxt]])
                yield
            s, tb = psget(1)
            quad_mm(s, Ps[:, nxt], Y_, t_Ps[nxt], t_Ys[cur], tb)
            fw.op("dve", lambda e: e.tensor_tensor(out=Ys[:, nxt].rearrange("p h f -> p (h f)"), in0=Y_.rearrange("p h f -> p (h f)"), in1=PS[:, s * 512:(s + 1) * 512], op=ALU.add),
                  reads=tb + [t_Ys[cur]], writes=[t_Ys[nxt]])
            cur = nxt
            yield
        fw.op("act", lambda e: e.activation(out=v4(Ybf[:])[:, :, 0, :], in_=Ys[0:64, cur], func=AF.Copy), reads=[t_Ys[cur]], writes=[t_Ybf])
        fw.op("dve", lambda e: e.tensor_copy(out=v4(Ybf[:])[:, :, 1, :], in_=Ys[64:128, cur]), reads=[t_Ys[cur]], writes=[t_Ybf])
        Y, tY = Ybf, t_Ybf
        yield
        fw.op("dve", lambda e: e.tensor_tensor(out=vb[:], in0=vtok[:], in1=bc_f(b_c, 128), op=ALU.mult), reads=[t_vtok, t_bg], writes=[t_vb])
        fw.op("dve", lambda e: e.tensor_tensor(out=v4(kbg[:]), in0=bc_pairs(ktok[:]), in1=v4(bc_f(be[0:64], 128)), op=ALU.mult), reads=[t_ktok, t_sm], writes=[t_kbg])
        fw.op("pool", lambda e: e.tensor_tensor(out=v4(kdec[:]), in0=bc_pairs(ktok[:]), in1=v4(bc_f(dgl[0:64], 128)), op=ALU.mult), reads=[t_ktok, t_sm], writes=[t_kdec])
        s, tb = psget(4)
        for h in range(NH):
            fw.op("pe", lambda e, h=h: e.matmul(PS[0:64, s * 512 + h * 128:s * 512 + (h + 1) * 128], lhsT=Y[:, h, :], rhs=vb[:, h, :], start=True, stop=True),
                  reads=[tY, t_vb], writes=tb)
        fw.op("act", lambda e: e.activation(out=u[:].rearrange("p h f -> p (h f)"), in_=PS[0:64, s * 512:(s + 4) * 512], func=AF.Copy), reads=tb, writes=[t_u])
        s, tb = psget(2)
        for h in range(NH):
            fw.op("pe", lambda e, h=h: e.matmul(PS[:, s * 512 + h * 64:s * 512 + (h + 1) * 64], lhsT=kbg[:, h, :], rhs=Y[:, h, :], start=True, stop=True),
                  reads=[tY, t_kbg], writes=tb)
        fw.op("act", lambda e: e.activation(out=wT[:].rearrange("p h f -> p (h f)"), in_=PS[:, s * 512:(s + 2) * 512], func=AF.Copy), reads=tb, writes=[t_wT])

    def stageB(ii):
        d, hh, c, first = iters[ii]
        b = ii % 2
        qgT, t_qgT, qkT, t_qkT, kdec, t_kdec, wT, t_wT, sm, t_sm = qgT2[b], t_qgT2[b], qkT2[b], t_qkT2[b], kdec2[b], t_kdec2[b], wT2[b], t_wT2[b], sm2[b], t_sm2[b]
        u, t_u = u2[b], t_u2[b]
        gam, glast, eg, gle, dgl, be = [sm[:, i, :] for i in range(6)]
        if first:
            fw.op("pool", lambda e: e.memset(S[:], 0.0), writes=t_S)
            fw.op("pool", lambda e: e.memset(Sbf[:], 0.0), writes=t_S)
        for hg in range(4):
            vn, tvn = vnew[hg % 2], t_vnew[hg % 2]
            tS = [t_S[hg]]
            s, tb = psgetB(0)
            for hl in range(4):
                h = hg * 4 + hl
                fw.op("pe", lambda e, h=h, hl=hl: e.matmul(PS[0:64, s * 512 + hl * 128:s * 512 + (hl + 1) * 128], lhsT=wT[:, h, :], rhs=Sbf[:, h, :], start=True, stop=True),
                      reads=[t_wT] + tS, writes=tb)
            yield
            fw.op("dve", lambda e: e.tensor_tensor(out=vn[:].rearrange("p h f -> p (h f)"), in0=u[:, hg * 4:(hg + 1) * 4, :].rearrange("p h f -> p (h f)"),
                                                   in1=PS[0:64, s * 512:(s + 1) * 512], op=ALU.subtract), reads=tb + [t_u], writes=[tvn])
            s, tb = psgetB(1)
            for hl in range(4):
                h = hg * 4 + hl
                fw.op("pe", lambda e, h=h, hl=hl: e.matmul(PS[0:64, s * 512 + hl * 128:s * 512 + (hl + 1) * 128], lhsT=qgT[:, h, :], rhs=Sbf[:, h, :], start=True, stop=False),
                      reads=[t_qgT] + tS, writes=tb)
                fw.op("pe", lambda e, h=h, hl=hl: e.matmul(PS[0:64, s * 512 + hl * 128:s * 512 + (hl + 1) * 128], lhsT=qkT[:, h, :], rhs=vn[:, hl, :], start=False, stop=True),
                      reads=[t_qkT, tvn], writes=tb)
            yield
            fw.op("act", lambda e: e.activation(out=ost[:, hg * 4:(hg + 1) * 4, :].rearrange("p h f -> p (h f)"), in_=PS[0:64, s * 512:(s + 1) * 512], func=AF.Copy),
                  reads=tb, writes=[t_ost])
            s, tb = psgetB(2)
            for hl in range(4):
                h = hg * 4 + hl
                fw.op("pe", lambda e, h=h, hl=hl: e.matmul(PS[:, s * 512 + hl * 128:s * 512 + (hl + 1) * 128], lhsT=kdec[:, h, :], rhs=vn[:, hl, :], start=True, stop=True),
                      reads=[t_kdec, tvn], writes=tb)
            yield
            for hl in range(4):
                h = hg * 4 + hl
                fw.op("dve", lambda e, h=h, hl=hl: e.scalar_tensor_tensor(out=S[:, h, :], in0=S[:, h, :], scalar=gle[:, h:h + 1], in1=PS[:, s * 512 + hl * 128:s * 512 + (hl + 1) * 128],
                                                                        op0=ALU.mult, op1=ALU.add), reads=tb + tS + [t_sm], writes=tS)
            yield
            fw.op("act", lambda e: e.activation(out=Sbf[:, hg * 4:(hg + 1) * 4, :], in_=S[:, hg * 4:(hg + 1) * 4, :], func=AF.Copy), reads=tS, writes=tS)
        fw.dma("sp", self.o_s[d, c * 64:(c + 1) * 64, hh * 2048:(hh + 1) * 2048], ost[:].rearrange("p h f -> p (h f)"), reads=[t_ost], writes=[self.t_os[d][c]])

    def drive(ga, gb, ra=2):
        alive_a, alive_b = ga is not None, gb is not None
        while alive_a or alive_b:
            if alive_a:
                for _ in range(ra):
                    try:
                        next(ga)
                    except StopIteration:
                        alive_a = False
                        break
            if alive_b:
                try:
                    next(gb)
                except StopIteration:
                    alive_b = False

    load_in(0)
    drive(stageA(0), None)
    for ii in range(len(iters)):
        drive(stageA(ii + 1) if ii + 1 < len(iters) else None, stageB(ii))
    ph.close()
    phA.close()
    ph = Phase(fw)
    otiles = list(range(16)) if last else tiles
    gate = [ph.sb([128, D], F32, "ggate") for _ in range(2)]
    t_gate = T("ggate")
    for r in range(2):
        self.load_bc(gate[r][:], t_gate, layer, r, 2)
    ng = ph.sb([128, 128], F32, "ng")
    t_ng = T("ng")
    fw.dma("sp", ng[:], I["gdn_norm_gain"][j].partition_broadcast(128), writes=[t_ng])
    wo = ph.sb([128, 32, D], BF16, "gwo")
    t_wo = [T("gwo%d" % i) for i in range(4)]
    for n in range(4):
        for kh in range(2):
            fw.dma("pool", wo[:, kh * 16:(kh + 1) * 16, n * 512:(n + 1) * 512],
                   I["gdn_w_o"][j, kh * 2048:(kh + 1) * 2048, n * 512:(n + 1) * 512].rearrange("(k p) n -> p k n", p=128), writes=[t_wo[n]])
    HC = D
    of2 = [ph.sb([128, HC], F32, "of") for _ in range(2)]
    ob2 = [ph.sb([128, HC], F32, "ob") for _ in range(2)]
    zt2 = [ph.sb([128, HC], BF16, "zt") for _ in range(2)]
    t_of2, t_ob2, t_zt2 = [T("of0"), T("of1")], [T("ob0"), T("ob1")], [T("zt0"), T("zt1")]
    ssn2 = [ph.sb([128, 3, 16], F32, "ssn") for _ in range(2)]
    t_ssn2 = [T("ssn0"), T("ssn1")]
    yT2 = [ph.sb([128, 16, 128], BF16, "yT") for _ in range(2)]
    t_yT2 = [T("yT0"), T("yT1")]
    xsl = [ph.sb([128, 512], F32, "gxsl") for _ in range(4)]
    t_xsl = [T("gxsl%d" % i) for i in range(4)]
    tmp = ph.sb([128, 512], F32, "gtmp")
    t_tmp = T("gtmp")
    pt = [ph.ps([128, 1024], BF16, "gpt") for _ in range(2)]
    t_pt = [T("gpt0"), T("gpt1")]
    po = [ph.ps([128, 512], F32, "gpo") for _ in range(4)]
    t_po = [T("gpo%d" % i) for i in range(4)]
    steps = [(ti, tt, hf) for ti, tt in enumerate(otiles) for hf in range(2)]

    def chain(si):
        ti, tt, hf = steps[si]
        sb = si % 2
        of, ob, zt, ssn = of2[sb], ob2[sb], zt2[sb], ssn2[sb]
        t_of, t_ob, t_zt, t_ssn = t_of2[sb], t_ob2[sb], t_zt2[sb], t_ssn2[sb]
        rows = slice(tt * 128, (tt + 1) * 128)
        cols = slice(hf * HC, (hf + 1) * HC)
        fw.dma("sp", of[:], self.o_s[0, rows, cols], reads=self.t_os[0][2 * tt:2 * tt + 2], writes=[t_of])
        fw.dma("sp", ob[:], self.o_s[1, rows, cols], reads=self.t_os[1][2 * tt:2 * tt + 2], writes=[t_ob])
        fw.dma("sp", zt[:], self.z_s[rows, cols], reads=self.t_z, writes=[t_zt])
        fw.op("pool", lambda e: e.tensor_tensor(out=of[:], in0=of[:], in1=ob[:], op=ALU.add), reads=[t_of, t_ob], writes=[t_of])
        fw.op("act", lambda e: e.activation(out=ob[:], in_=of[:], func=AF.Square), reads=[t_of], writes=[t_ob])
        fw.op("dve", lambda e: e.tensor_reduce(out=ssn[:, 0, :], in_=ob[:].rearrange("p (h f) -> p h f", h=16), axis=AX.X, op=ALU.add), reads=[t_ob], writes=[t_ssn])
        fw.op("act", lambda e: e.activation(out=ssn[:, 1, :], in_=ssn[:, 0, :], func=AF.Sqrt, scale=1.0 / 128, bias=EPS), reads=[t_ssn], writes=[t_ssn])
        fw.op("dve", lambda e: e.reciprocal(out=ssn[:, 2, :], in_=ssn[:, 1, :]), reads=[t_ssn], writes=[t_ssn])
        o3 = of[:].rearrange("p (h f) -> p h f", h=16)
        fw.op("dve", lambda e: e.tensor_tensor(out=o3, in0=o3, in1=ssn[:, 2, :].unsqueeze(2).to_broadcast([128, 16, 128]), op=ALU.mult), reads=[t_of, t_ssn], writes=[t_of])
        fw.op("pool", lambda e: e.tensor_tensor(out=o3, in0=o3, in1=ng[:].unsqueeze(1).to_broadcast([128, 16, 128]), op=ALU.mult), reads=[t_of, t_ng], writes=[t_of])
        fw.op("dve", lambda e: e.tensor_tensor(out=zt[:], in0=of[:], in1=zt[:], op=ALU.mult), reads=[t_of, t_zt], writes=[t_zt])

    def trans(si):
        sb = si % 2
        yb, t_yb, yT, t_yT = zt2[sb], t_zt2[sb], yT2[sb], t_yT2[sb]
        for q4 in range(2):
            p_ = pt[q4]
            for kk in range(8):
                k = q4 * 8 + kk
                fw.op("pe", lambda e, k=k, kk=kk: e.transpose(out=p_[:, kk * 128:(kk + 1) * 128], in_=yb[:, k * 128:(k + 1) * 128], identity=self.ident_bf[:]),
                      reads=[t_yb, self.t_c], writes=[t_pt[q4]], sig=(kk == 7))
            fw.op("act", lambda e: e.activation(out=yT[:, q4 * 8:(q4 + 1) * 8, :], in_=p_[:].rearrange("p (k t) -> p k t", k=8), func=AF.Copy), reads=[t_pt[q4]], writes=[t_yT])

    def mm(si):
        ti, tt, hf = steps[si]
        sb = si % 2
        yT, t_yT = yT2[sb], t_yT2[sb]
        r = 0 if tt < 16 else 1
        if hf == 1:
            for n in range(4):
                fw.dma("sp", xsl[n][:], self.xs[tt * 128:(tt + 1) * 128, n * 512:(n + 1) * 512], reads=[self.t_xs[tt]], writes=[t_xsl[n]])
        for n in range(4):
            p_, tp = po[n], t_po[n]
            for kk in range(16):
                k = hf * 16 + kk
                fw.op("pe", lambda e, k=k, kk=kk, n=n: e.matmul(p_[:], lhsT=yT[:, kk, :], rhs=wo[:, k, n * 512:(n + 1) * 512], start=(k == 0), stop=(k == 31)),
                      reads=[t_yT, t_wo[n]], writes=[tp], sig=(kk == 15))
            if hf == 1:
                fw.op("dve", lambda e, n=n: e.tensor_tensor(out=tmp[:], in0=p_[:], in1=gate[r][:, n * 512:(n + 1) * 512], op=ALU.mult), reads=[tp, t_gate], writes=[t_tmp])
                fw.op("dve", lambda e, n=n: e.tensor_tensor(out=xsl[n][:], in0=xsl[n][:], in1=tmp[:], op=ALU.add), reads=[t_xsl[n], t_tmp], writes=[t_xsl[n]])
                fw.dma("sp", self.xs[tt * 128:(tt + 1) * 128, n * 512:(n + 1) * 512], xsl[n][:], reads=[t_xsl[n]], writes=[self.t_xs[tt]])

    chain(0)
    for si in range(len(steps)):
        trans(si)
        if si + 1 < len(steps):
            chain(si + 1)
        mm(si)
    ph.close()


Prog.gdn_layer = _gdn_layer
```
